# Optimizing a Trainium2 kernel written in Bass

```python
import math
import jax, jax.numpy as jnp
from jax import lax
import numpy as np

D_MODEL = 1024
BATCH = 8
SEQ = 2048
DEPTH = 2
DEC_BATCH = 32
DEC_SEQ = 8
PAST_LEN = 8192
PAGE_SIZE = 128

N_HEADS = 16
HEAD_DIM = D_MODEL // N_HEADS
Q_BLOCK = 128
SB_BIAS_INIT = -6.0
SSM_GROUP = 16
D_INNER = D_MODEL
N_GROUPS = D_INNER // SSM_GROUP
STATE_DIM = 64
DT_MIN = 0.001
DT_MAX = 0.1
D_FF = 2816
CONV_W = 3
N_ATTN = (DEPTH + 1) // 2
N_SSM = DEPTH // 2
NORM_EPS = 1e-6

kernel_name = "hybrid_stickbreak_s5_convffn_step"


def rms_norm(x, gain):
    xf = x.astype(jnp.float32)
    y = xf * lax.rsqrt(jnp.mean(xf * xf, axis=-1, keepdims=True) + NORM_EPS)
    return (y * gain.astype(jnp.float32)).astype(x.dtype)


def stick_breaking_block(q, k, v, bias, q_start):
    tq, tk = q.shape[1], k.shape[1]
    z = (jnp.einsum("bqhd,bkhd->bhqk", q.astype(jnp.float32), k.astype(jnp.float32)) * (HEAD_DIM ** -0.5)
         + bias.astype(jnp.float32)[None, :, None, None])
    q_pos = q_start + jnp.arange(tq)
    k_pos = jnp.arange(tk)
    causal = k_pos[None, :] < q_pos[:, None]
    log_keep = jnp.where(causal, jax.nn.log_sigmoid(-z), 0.0)
    between = lax.cumsum(log_keep, axis=3, reverse=True) - log_keep
    weights = jnp.where(causal, jnp.exp(jax.nn.log_sigmoid(z) + between), 0.0)
    return jnp.einsum("bhqk,bkhd->bqhd", weights.astype(v.dtype), v)


def stick_breaking_attention(q, k, v, bias, q_offset):
    tq = q.shape[1]
    outs = []
    for start in range(0, tq, Q_BLOCK):
        stop = min(start + Q_BLOCK, tq)
        k_end = q_offset + stop
        outs.append(stick_breaking_block(q[:, start:stop], k[:, :k_end], v[:, :k_end], bias, q_offset + start))
    return jnp.concatenate(outs, axis=1)


def attn_mixer(h, w_qkv, q_gain, k_gain, logit_bias, w_o, past_k, past_v):
    b, s, _ = h.shape
    qkv = (h @ w_qkv).reshape(b, s, 3, N_HEADS, HEAD_DIM)
    q = rms_norm(qkv[:, :, 0], q_gain)
    k = rms_norm(qkv[:, :, 1], k_gain)
    v = qkv[:, :, 2]
    if past_k is None:
        k_all, v_all, offset = k, v, 0
    else:
        k_all = jnp.concatenate([past_k.astype(k.dtype), k], axis=1)
        v_all = jnp.concatenate([past_v.astype(v.dtype), v], axis=1)
        offset = past_k.shape[1]
    o = stick_breaking_attention(q, k_all, v_all, logit_bias, offset)
    y = o.reshape(b, s, N_HEADS * HEAD_DIM) @ w_o
    return y, k, v


def _complex_affine_combine(earlier, later):
    a1r, a1i, b1r, b1i = earlier
    a2r, a2i, b2r, b2i = later
    return (a2r * a1r - a2i * a1i,
            a2r * a1i + a2i * a1r,
            a2r * b1r - a2i * b1i + b2r,
            a2r * b1i + a2i * b1r + b2i)


def s5_mixer(h, w_in, lam_re, lam_im, log_dt, b_re, b_im, c_re, c_im, d_skip, w_glu, s_prev_re, s_prev_im):
    f32 = jnp.float32
    bsz, s, _ = h.shape
    u = (h @ w_in).astype(f32)
    ug = u.reshape(bsz, s, N_GROUPS, SSM_GROUP)
    lr, li = lam_re.astype(f32), lam_im.astype(f32)
    dt = jnp.exp(log_dt.astype(f32))[:, None]
    mag = jnp.exp(lr * dt)
    a_re = mag * jnp.cos(li * dt)
    a_im = mag * jnp.sin(li * dt)
    den = lr * lr + li * li
    zr = ((a_re - 1.0) * lr + a_im * li) / den
    zi = (a_im * lr - (a_re - 1.0) * li) / den
    br, bi = b_re.astype(f32), b_im.astype(f32)
    bbar_re = zr[..., None] * br - zi[..., None] * bi
    bbar_im = zr[..., None] * bi + zi[..., None] * br
    bu_re = jnp.einsum("bsgc,gpc->bsgp", ug, bbar_re)
    bu_im = jnp.einsum("bsgc,gpc->bsgp", ug, bbar_im)
    if s_prev_re is not None:
        pr, pi = s_prev_re.astype(f32), s_prev_im.astype(f32)
        bu_re = bu_re.at[:, 0].add(a_re * pr - a_im * pi)
        bu_im = bu_im.at[:, 0].add(a_re * pi + a_im * pr)
    ar = jnp.broadcast_to(a_re, bu_re.shape)
    ai = jnp.broadcast_to(a_im, bu_im.shape)
    _, _, st_re, st_im = lax.associative_scan(_complex_affine_combine, (ar, ai, bu_re, bu_im), axis=1)
    y = (jnp.einsum("bsgp,gcp->bsgc", st_re, c_re.astype(f32))
         - jnp.einsum("bsgp,gcp->bsgc", st_im, c_im.astype(f32)))
    y = y.reshape(bsz, s, D_INNER) + d_skip.astype(f32) * u
    g = jax.nn.gelu(y).astype(h.dtype)
    ga, gb = jnp.split(g @ w_glu, 2, axis=-1)
    out = ga * jax.nn.sigmoid(gb)
    return out, st_re[:, -1], st_im[:, -1]


def conv_ffn(h, w_up, conv_w, conv_b, w_down, prev):
    s = h.shape[1]
    a, b = jnp.split(h @ w_up, 2, axis=-1)
    padded = jnp.concatenate([prev.astype(a.dtype), a], axis=1)
    c = conv_b + conv_w[0] * padded[:, 0:s]
    for j in range(1, CONV_W):
        c = c + conv_w[j] * padded[:, j:j + s]
    out = (jax.nn.silu(c) * b) @ w_down
    return out, padded[:, s:]


def trunk(x, past_kv, ssm_prev, conv_prev, p):
    new_k, new_v, new_sr, new_si, new_conv = [], [], [], [], []
    for i in range(DEPTH):
        j = i // 2
        h = rms_norm(x, p["norm_mix"][i])
        if i % 2 == 0:
            pk, pv = past_kv[j] if past_kv is not None else (None, None)
            y, k, v = attn_mixer(h, p["attn_w_qkv"][j], p["attn_q_gain"][j], p["attn_k_gain"][j],
                                 p["attn_logit_bias"][j], p["attn_w_o"][j], pk, pv)
            new_k.append(k)
            new_v.append(v)
        else:
            if ssm_prev is not None:
                sr, si = ssm_prev[0][j], ssm_prev[1][j]
            else:
                sr, si = None, None
            y, fr, fi = s5_mixer(h, p["ssm_w_in"][j], p["ssm_lambda_re"][j], p["ssm_lambda_im"][j],
                                 p["ssm_log_dt"][j], p["ssm_b_re"][j], p["ssm_b_im"][j],
                                 p["ssm_c_re"][j], p["ssm_c_im"][j], p["ssm_d"][j], p["ssm_w_glu"][j], sr, si)
            new_sr.append(fr)
            new_si.append(fi)
        x = x + y.astype(x.dtype)
        h = rms_norm(x, p["norm_ffn"][i])
        if conv_prev is not None:
            prev = conv_prev[i]
        else:
            prev = jnp.zeros((x.shape[0], CONV_W - 1, D_FF), x.dtype)
        y, cs = conv_ffn(h, p["ffn_w_up"][i], p["ffn_conv_w"][i], p["ffn_conv_b"][i], p["ffn_w_down"][i], prev)
        x = x + y.astype(x.dtype)
        new_conv.append(cs)
    return (x, jnp.stack(new_k), jnp.stack(new_v), jnp.stack(new_sr), jnp.stack(new_si), jnp.stack(new_conv))


def setup_inputs(seed: int = 0) -> dict:
    key = jax.random.key(seed)
    ks = jax.random.split(key, 32)
    f32 = jnp.float32
    n_pages = PAST_LEN // PAGE_SIZE
    n_used = DEC_BATCH * n_pages
    n_pool = n_used + n_used // 4
    nrm = lambda k, shape, scale: jax.random.normal(k, shape, f32) * scale
    page_table = jax.random.permutation(ks[0], n_pool)[:n_used].reshape(DEC_BATCH, n_pages).astype(jnp.int32)
    lam_im = (math.pi * jnp.arange(STATE_DIM, dtype=f32))[None, None, :] + nrm(ks[14], (N_SSM, N_GROUPS, STATE_DIM), 0.01)
    return {
        "x_prompt": nrm(ks[1], (BATCH, SEQ, D_MODEL), 1.0),
        "x_sample": nrm(ks[2], (DEC_BATCH, DEC_SEQ, D_MODEL), 1.0),
        "cache_k": nrm(ks[3], (N_ATTN, n_pool, PAGE_SIZE, N_HEADS, HEAD_DIM), 1.0),
        "cache_v": nrm(ks[4], (N_ATTN, n_pool, PAGE_SIZE, N_HEADS, HEAD_DIM), 1.0),
        "state_ssm_re": nrm(ks[5], (N_SSM, DEC_BATCH, N_GROUPS, STATE_DIM), 0.1),
        "state_ssm_im": nrm(ks[6], (N_SSM, DEC_BATCH, N_GROUPS, STATE_DIM), 0.1),
        "state_ffn_conv": nrm(ks[7], (DEPTH, DEC_BATCH, CONV_W - 1, D_FF), 1.0),
        "page_table": page_table,
        "norm_mix": 1.0 + nrm(ks[8], (DEPTH, D_MODEL), 0.01),
        "norm_ffn": 1.0 + nrm(ks[9], (DEPTH, D_MODEL), 0.01),
        "attn_w_qkv": nrm(ks[10], (N_ATTN, D_MODEL, 3 * N_HEADS * HEAD_DIM), D_MODEL ** -0.5),
        "attn_q_gain": 1.0 + nrm(ks[11], (N_ATTN, HEAD_DIM), 0.01),
        "attn_k_gain": 1.0 + nrm(ks[12], (N_ATTN, HEAD_DIM), 0.01),
        "attn_logit_bias": SB_BIAS_INIT + nrm(ks[28], (N_ATTN, N_HEADS), 0.1),
        "attn_w_o": nrm(ks[13], (N_ATTN, N_HEADS * HEAD_DIM, D_MODEL), (N_HEADS * HEAD_DIM) ** -0.5),
        "ssm_w_in": nrm(ks[15], (N_SSM, D_MODEL, D_INNER), D_MODEL ** -0.5),
        "ssm_lambda_re": -0.5 + nrm(ks[16], (N_SSM, N_GROUPS, STATE_DIM), 0.01),
        "ssm_lambda_im": lam_im,
        "ssm_log_dt": jax.random.uniform(ks[17], (N_SSM, N_GROUPS), f32, math.log(DT_MIN), math.log(DT_MAX)),
        "ssm_b_re": nrm(ks[18], (N_SSM, N_GROUPS, STATE_DIM, SSM_GROUP), (2 * SSM_GROUP) ** -0.5),
        "ssm_b_im": nrm(ks[19], (N_SSM, N_GROUPS, STATE_DIM, SSM_GROUP), (2 * SSM_GROUP) ** -0.5),
        "ssm_c_re": nrm(ks[20], (N_SSM, N_GROUPS, SSM_GROUP, STATE_DIM), STATE_DIM ** -0.5),
        "ssm_c_im": nrm(ks[21], (N_SSM, N_GROUPS, SSM_GROUP, STATE_DIM), STATE_DIM ** -0.5),
        "ssm_d": nrm(ks[22], (N_SSM, D_INNER), 1.0),
        "ssm_w_glu": nrm(ks[23], (N_SSM, D_INNER, 2 * D_MODEL), D_INNER ** -0.5),
        "ffn_w_up": nrm(ks[24], (DEPTH, D_MODEL, 2 * D_FF), D_MODEL ** -0.5),
        "ffn_conv_w": nrm(ks[25], (DEPTH, CONV_W, D_FF), CONV_W ** -0.5),
        "ffn_conv_b": nrm(ks[26], (DEPTH, D_FF), 0.01),
        "ffn_w_down": nrm(ks[27], (DEPTH, D_FF, D_MODEL), D_FF ** -0.5),
    }


def reference(x_prompt, x_sample, cache_k, cache_v, state_ssm_re, state_ssm_im, state_ffn_conv, page_table,
              norm_mix, norm_ffn, attn_w_qkv, attn_q_gain, attn_k_gain, attn_logit_bias, attn_w_o,
              ssm_w_in, ssm_lambda_re, ssm_lambda_im, ssm_log_dt, ssm_b_re, ssm_b_im, ssm_c_re, ssm_c_im,
              ssm_d, ssm_w_glu, ffn_w_up, ffn_conv_w, ffn_conv_b, ffn_w_down):
    p = {
        "norm_mix": norm_mix, "norm_ffn": norm_ffn,
        "attn_w_qkv": attn_w_qkv, "attn_q_gain": attn_q_gain, "attn_k_gain": attn_k_gain,
        "attn_logit_bias": attn_logit_bias, "attn_w_o": attn_w_o,
        "ssm_w_in": ssm_w_in, "ssm_lambda_re": ssm_lambda_re, "ssm_lambda_im": ssm_lambda_im,
        "ssm_log_dt": ssm_log_dt, "ssm_b_re": ssm_b_re, "ssm_b_im": ssm_b_im,
        "ssm_c_re": ssm_c_re, "ssm_c_im": ssm_c_im, "ssm_d": ssm_d, "ssm_w_glu": ssm_w_glu,
        "ffn_w_up": ffn_w_up, "ffn_conv_w": ffn_conv_w, "ffn_conv_b": ffn_conv_b, "ffn_w_down": ffn_w_down,
    }
    y_prompt, k_prompt, v_prompt, ssm_re_prompt, ssm_im_prompt, conv_prompt = trunk(x_prompt, None, None, None, p)
    db, n_pages = page_table.shape
    past_kv = [(cache_k[j][page_table].reshape(db, n_pages * PAGE_SIZE, N_HEADS, HEAD_DIM),
                cache_v[j][page_table].reshape(db, n_pages * PAGE_SIZE, N_HEADS, HEAD_DIM))
               for j in range(N_ATTN)]
    y_sample, k_sample, v_sample, ssm_re_sample, ssm_im_sample, conv_sample = trunk(
        x_sample, past_kv, (state_ssm_re, state_ssm_im), state_ffn_conv, p)
    return (y_prompt, y_sample, k_prompt, v_prompt, k_sample, v_sample,
            ssm_re_prompt, ssm_im_prompt, ssm_re_sample, ssm_im_sample, conv_prompt, conv_sample)
```

```python
import math
from contextlib import ExitStack

import numpy as np
import concourse.bass as bass
import concourse.mybir as mybir
from concourse.bass_utils import run_bass_kernel_spmd

F32 = mybir.dt.float32
BF16 = mybir.dt.bfloat16
I32 = mybir.dt.int32
AF = mybir.ActivationFunctionType
ALU = mybir.AluOpType
AX = mybir.AxisListType

NCORES = 8
D = 1024
SEQ = 2048
NS = 32
NT = SEQ + NS
DFF = 2816
NFC = 22
EPS = 1e-6
PI = math.pi
PI_SAFE = 3.1415925
FFN_GROUPS = [list(range(0, 8)), list(range(8, 15)), list(range(15, 22))]
DEBUG_SKIP = set()


class Buf:
    __slots__ = ("w", "r")

    def __init__(self):
        self.w = None
        self.r = {}


class Sched:
    def __init__(self, nc, es):
        self.nc = nc
        self.es = es
        self.h = {"pe": nc.tensor, "act": nc.scalar, "dve": nc.vector, "pool": nc.gpsimd, "sp": nc.sync}
        self.semobj = {}
        self.cnt = {}
        for k in ("pe", "act", "dve", "pool"):
            self.semobj[k] = es.enter_context(nc.semaphore("s_" + k))
            self.cnt[k] = 0
        self.waited = {k: {} for k in self.h}
        self.chans = []
        self.pend = {k: ([], []) for k in self.cnt}

    def _deps(self, reads, writes):
        deps = {}

        def add(k, v):
            if deps.get(k, 0) < v:
                deps[k] = v

        for b in reads:
            if b.w is not None:
                add(*b.w)
        for b in writes:
            if b.w is not None:
                add(*b.w)
            for k, v in b.r.items():
                add(k, v)
        return deps

    def _wait(self, eng, deps):
        for k, v in deps.items():
            if eng == "pe" and k == "pe":
                continue
            if self.waited[eng].get(k, 0) >= v:
                continue
            self.h[eng].wait_ge(self.semobj[k], v)
            self.waited[eng][k] = v

    def _mark(self, ticket, reads, writes):
        k, v = ticket
        for b in reads:
            if b.r.get(k, 0) < v:
                b.r[k] = v
        for b in writes:
            b.w = ticket
            b.r = {}

    def op(self, eng, fn, reads=(), writes=(), inc=True):
        if eng == "pool":
            eng = "dve"
        self._wait(eng, self._deps(reads, writes))
        ins = fn(self.h[eng])
        if not inc:
            self.pend[eng][0].extend(reads)
            self.pend[eng][1].extend(writes)
            return
        self.cnt[eng] += 1
        ins.then_inc(self.semobj[eng], 1)
        pr, pw = self.pend[eng]
        self._mark((eng, self.cnt[eng]), list(reads) + pr, list(writes) + pw)
        self.pend[eng] = ([], [])

    def chan(self, name):
        c = {"key": "c_" + name, "n": 0}
        self.semobj[c["key"]] = self.es.enter_context(self.nc.semaphore("c_" + name))
        self.chans.append(c)
        return c

    def dma(self, q, ch, out, in_, reads=(), writes=(), **kw):
        self._wait(q, self._deps(reads, writes))
        ins = self.h[q].dma_start(out=out, in_=in_, **kw)
        ch["n"] += 16
        ins.then_inc(self.semobj[ch["key"]], 16)
        self._mark((ch["key"], ch["n"]), reads, writes)

    def idma(self, ch, out, in_, idx_ap, reads=(), writes=()):
        q = "pool"
        self._wait(q, self._deps(reads, writes))
        ins = self.h[q].indirect_dma_start(out=out, out_offset=None, in_=in_,
                                           in_offset=bass.IndirectOffsetOnAxis(ap=idx_ap, axis=0))
        ch["n"] += 16
        ins.then_inc(self.semobj[ch["key"]], 16)
        self._mark((ch["key"], ch["n"]), reads, writes)

    def barrier(self, engines=None):
        allt = {k: v for k, v in self.cnt.items() if v > 0}
        for c in self.chans:
            if c["n"]:
                allt[c["key"]] = c["n"]
        for eng in (engines or self.h.keys()):
            self._wait(eng, allt)


def build_program(pool_rows=2560 * 128):
    nc = bass.Bass("TRN2", target_bir_lowering=False)

    def din(name, shape, dt=F32):
        return nc.dram_tensor(name, list(shape), dt, kind="ExternalInput").ap()

    def dout(name, shape, dt=F32):
        return nc.dram_tensor(name, list(shape), dt, kind="ExternalOutput").ap()

    xp = din("xp", [SEQ, D]); xs = din("xs", [NS, D])
    ck = din("ck", [pool_rows, D]); cv = din("cv", [pool_rows, D])
    ptab = din("ptab", [1, 256], I32)
    norm_mix = din("norm_mix", [2, D]); norm_ffn = din("norm_ffn", [2, D])
    w_qkv = din("w_qkv", [D, 3 * D]); q_gain = din("q_gain", [1, 64]); k_gain = din("k_gain", [1, 64])
    lbias = din("lbias", [1, 16]); lbias_hq = din("lbias_hq", [128, 1])
    w_o = din("w_o", [D, D]); w_in = din("w_in", [D, D]); w_glu = din("w_glu", [D, 2 * D])
    lamT_re = din("lamT_re", [128, 32]); lamT_im = din("lamT_im", [128, 32]); logdtT = din("logdtT", [128, 32])
    BT_re = din("BT_re", [128, 32, 16]); BT_im = din("BT_im", [128, 32, 16])
    CT_re = din("CT_re", [128, 32, 16]); CT_im = din("CT_im", [128, 32, 16])
    d_lay = din("d_lay", [128, 64])
    sprev_re = din("sprev_re", [128, 32, 4]); sprev_im = din("sprev_im", [128, 32, 4])
    w_up = din("w_up", [2, D, 2 * DFF]); w_down = din("w_down", [2, DFF, D])
    conv_wl = din("conv_wl", [2, 128, NFC, 3]); conv_bl = din("conv_bl", [2, 128, NFC])
    conv_prevT = din("conv_prevT", [2, 128, NFC, 4, 2])
    c_ident = din("c_ident", [128, 128]); c_trim8 = din("c_trim8", [128, 128]); c_onesm8 = din("c_onesm8", [128, 128])
    c_maskd = din("c_maskd", [128, 128]); c_masknew = din("c_masknew", [128, 4, 32]); c_maskM = din("c_maskM", [128, 128])
    c_sel = din("c_sel", [128, 64, 128]); c_iotap = din("c_iotap", [128, 1]); c_iotan = din("c_iotan", [128, 256])

    y_p = dout("y_p", [SEQ, D]); y_s = dout("y_s", [NS, D])
    kv_dev = dout("kv_dev", [8, 128, 17, 256])
    sre_p = dout("sre_p", [128, 32]); sim_p = dout("sim_p", [128, 32])
    sre_s = dout("sre_s", [128, 32, 4]); sim_s = dout("sim_s", [128, 32, 4])
    conv_p = dout("conv_p", [2, 128, NFC, 2]); conv_s = dout("conv_s", [2, 128, NFC, 4, 2])

    with ExitStack() as es:
        S = Sched(nc, es)

        _nm = [0]

        def sb(scope, name, shape, dt):
            _nm[0] += 1
            return scope.enter_context(nc.sbuf_tensor("%s_%d" % (name, _nm[0]), list(shape), dt))

        psF = [es.enter_context(nc.psum_tensor("psF%d" % i, [128, 512], F32)) for i in range(6)]
        psFb = [Buf() for _ in range(6)]
        psB32 = [es.enter_context(nc.psum_tensor("psB%d" % i, [128, 512], F32)) for i in range(2)]
        psB = [t[:].bitcast(BF16) for t in psB32]
        psBb = [Buf() for _ in range(2)]

        x_sb = sb(es, "x_sb", [128, 17, D], F32)
        xb = [Buf() for _ in range(17)]
        ident = sb(es, "ident", [128, 128], BF16)
        trim8 = sb(es, "trim8", [128, 128], BF16)
        onesm8 = sb(es, "onesm8", [128, 128], BF16)
        maskd = sb(es, "maskd", [128, 128], F32)
        bias_rep = sb(es, "bias_rep", [128, 1, 16], F32)
        bias_hq = sb(es, "bias_hq", [128, 1], F32)
        qg_rep = sb(es, "qg_rep", [128, 1, 64], F32)
        kg_rep = sb(es, "kg_rep", [128, 1, 64], F32)
        ssn = sb(es, "ssn", [128, 17], F32)
        rstd = sb(es, "rstd", [128, 17], F32)
        constb = Buf(); g_repb = Buf(); ssb = [Buf() for _ in range(17)]; xnb = [Buf(), Buf()]

        ch_x = S.chan("x"); ch_c = S.chan("const"); ch_g = S.chan("gain")

        for hf in range(2):
            S.dma("sp", ch_x, x_sb[:, hf * 8:(hf + 1) * 8, :],
                  xp[hf * 1024:(hf + 1) * 1024, :].rearrange("(p j) d -> p j d", j=8),
                  writes=xb[hf * 8:(hf + 1) * 8])
        S.dma("sp", ch_x, x_sb[0:NS, 16, :], xs[:, :], writes=[xb[16]])
        S.dma("pool", ch_c, ident[:], c_ident[:, :], writes=[constb])
        S.dma("pool", ch_c, trim8[:], c_trim8[:, :], writes=[constb])
        S.dma("pool", ch_c, onesm8[:], c_onesm8[:, :], writes=[constb])
        S.dma("sp", ch_c, maskd[:], c_maskd[:, :], writes=[constb])
        S.dma("sp", ch_c, bias_rep[:], lbias[0:1, :].partition_broadcast(128), writes=[constb])
        S.dma("sp", ch_c, bias_hq[:], lbias_hq[:, :], writes=[constb])
        S.dma("sp", ch_c, qg_rep[:], q_gain[0:1, :].partition_broadcast(128), writes=[constb])
        S.dma("sp", ch_c, kg_rep[:], k_gain[0:1, :].partition_broadcast(128), writes=[constb])

        def tcols(s):
            if s < 16:
                hf, j = divmod(s, 8)
                return slice(hf * 1024 + j, (hf + 1) * 1024, 8)
            return slice(SEQ, NT)

        def tp(s):
            return 128 if s < 16 else NS

        evac_rr = [0]

        def evac(out, in_, reads, writes, eng=None):
            if eng is None:
                eng = ("act", "dve")[evac_rr[0] % 2]
                evac_rr[0] += 1
            if eng == "act":
                S.op("act", lambda h: h.activation(out=out, in_=in_, func=AF.Copy), reads, writes)
            else:
                S.op(eng, lambda h: h.tensor_copy(out=out, in_=in_), reads, writes)

        def rms_to_hT(gain_row, hT, hTb):
          with ExitStack() as scn:
            g_rep = sb(scn, "g_rep", [128, 1, D], F32)
            junk = sb(scn, "junk", [128, D], BF16)
            xn = [sb(scn, "xn%d" % i, [128, D], BF16) for i in range(2)]
            S.dma("sp", ch_g, g_rep[:], gain_row.partition_broadcast(128), writes=[g_repb])
            for s in ([] if "norm" in DEBUG_SKIP else range(17)):
                P = tp(s)
                S.op("act", lambda h: h.activation(out=junk[:P, :], in_=x_sb[:P, s, :], func=AF.Square,
                                                   accum_out=ssn[:P, s:s + 1]), [xb[s]], [ssb[s]])
                S.op("act", lambda h: h.activation(out=rstd[:P, s:s + 1], in_=ssn[:P, s:s + 1], func=AF.Ln, bias=EPS, scale=1.0 / D),
                     [ssb[s]], [ssb[s]])
                S.op("act", lambda h: h.activation(out=rstd[:P, s:s + 1], in_=rstd[:P, s:s + 1], func=AF.Exp, scale=-0.5),
                     [ssb[s]], [ssb[s]])
                xt = xn[s % 2]
                S.op("dve", lambda h: h.scalar_tensor_tensor(out=xt[:P, :], in0=x_sb[:P, s, :], scalar=rstd[:P, s:s + 1],
                                                             in1=g_rep[:P, 0, :], op0=ALU.mult, op1=ALU.mult),
                     [xb[s], ssb[s], g_repb], [xnb[s % 2]])
                pb = psB[s % 2]
                for c in range(8):
                    S.op("pe", lambda h: h.transpose(pb[:, c * 128:c * 128 + P], xt[:P, c * 128:(c + 1) * 128], ident[:P, :P]),
                         [xnb[s % 2], constb], [psBb[s % 2]], inc=(c == 7))
                evac(hT[:, :, tcols(s)], pb.rearrange("p (c t) -> p c t", c=8)[:, :, 0:P], [psBb[s % 2]], [hTb])
            S.barrier()

        with ExitStack() as sc_attn:
            OT = sb(sc_attn, "OT", [128, 8, NT], BF16)
            OTb = Buf()
            Qbd = sb(sc_attn, "Qbd", [128, 4, 8, 128], BF16)
            KTs = sb(sc_attn, "KTs", [128, 8, NS], BF16)
            Vs16 = sb(sc_attn, "Vs16", [NS, D], BF16)
            smpb = Buf()
            S.op("pool", lambda h: h.memset(Qbd[:], 0.0), [], [smpb])

            with ExitStack() as sc1:
                kv32 = sb(sc1, "kv32", [128, 9, 256], F32)
                hT = sb(sc1, "hT", [128, 8, NT], BF16)
                hTb = Buf()
                rms_to_hT(norm_mix[0:1, :], hT, hTb)

                wq_sb = [sb(sc1, "wq%d" % i, [128, 8, 384], BF16) for i in range(2)]
                wqb = [Buf(), Buf()]
                ch_wq = [S.chan("wq0"), S.chan("wq1")]
                QKT = sb(sc1, "QKT", [128, 2, NT], BF16)
                V16 = sb(sc1, "V16", [128, 17, 128], BF16)
                QKTb = Buf(); V16b = Buf()
                k32b = Buf(); v32b = k32b
                ch_ko = S.chan("kvout")
                sq = [sb(sc1, "sq%d" % i, [128, 256], F32) for i in range(2)]
                sqb = [Buf(), Buf()]
                ss4 = [sb(sc1, "ss4%d" % i, [128, 4], F32) for i in range(2)]
                ss4b = [Buf(), Buf()]
                q16 = [sb(sc1, "q16%d" % i, [128, 128], BF16) for i in range(2)]
                k16 = [sb(sc1, "k16%d" % i, [128, 128], BF16) for i in range(2)]
                q16b = [Buf(), Buf()]; k16b = [Buf(), Buf()]
                e32 = [sb(sc1, "e32%d" % i, [128, 512], F32) for i in range(2)]
                spb16 = [sb(sc1, "spb%d" % i, [128, 512], BF16) for i in range(2)]
                wb16 = [sb(sc1, "wb%d" % i, [128, 512], BF16) for i in range(2)]
                e32b = [Buf(), Buf()]; spbb = [Buf(), Buf()]; wbb = [Buf(), Buf()]
                sps32 = [sb(sc1, "sps32%d" % i, [128, 512], F32) for i in range(2)]
                sps16 = [sb(sc1, "sps16%d" % i, [128, 512], BF16) for i in range(2)]
                sps32b = [Buf(), Buf()]; sps16b = [Buf(), Buf()]

                w_qkv_v = w_qkv.rearrange("(c p) n -> p c n", p=128)

                def load_wq(hp):
                    i = hp % 2
                    for part in range(3):
                        S.dma("pool", ch_wq[i], wq_sb[i][:, :, part * 128:(part + 1) * 128],
                              w_qkv_v[:, :, part * D + hp * 128: part * D + (hp + 1) * 128], writes=[wqb[i]])

                load_wq(0)
                blk = [0]
                grp = [0]
                for hp in ([] if "qkv" in DEBUG_SKIP else range(1 if "qkv1" in DEBUG_SKIP else 8)):
                    if hp + 1 < 8:
                        load_wq(hp + 1)
                    wq = wq_sb[hp % 2]
                    for t in range(17):
                        P = 128 if t < 16 else NS
                        cols = slice(t * 128, t * 128 + P)
                        ps = psF[t % 2]; psb_ = psFb[t % 2]
                        for c in range(8):
                            S.op("pe", lambda h: h.matmul(ps[:P, 0:384], lhsT=hT[:, c, cols], rhs=wq[:, c, :],
                                                          start=(c == 0), stop=(c == 7)),
                                 [hTb, wqb[hp % 2]], [psb_], inc=(c == 7))
                        i2 = t % 2
                        slot = t if t < 9 else t - 9
                        S.op("act", lambda h: h.activation(out=sq[i2][:P, :], in_=ps[:P, 0:256], func=AF.Square),
                             [psb_], [sqb[i2]])
                        S.op("dve", lambda h: h.tensor_reduce(out=ss4[i2][:P, :], in_=sq[i2][:P, :].rearrange("p (g d) -> p g d", d=64),
                                                              axis=AX.X, op=ALU.add), [sqb[i2]], [ss4b[i2]])
                        S.op("act", lambda h: h.activation(out=ss4[i2][:P, :], in_=ss4[i2][:P, :], func=AF.Ln, bias=EPS, scale=1.0 / 64),
                             [ss4b[i2]], [ss4b[i2]])
                        S.op("act", lambda h: h.activation(out=ss4[i2][:P, :], in_=ss4[i2][:P, :], func=AF.Exp, scale=-0.5),
                             [ss4b[i2]], [ss4b[i2]])
                        for g in range(2):
                            S.op("dve", lambda h: h.scalar_tensor_tensor(out=q16[i2][:P, g * 64:(g + 1) * 64], in0=ps[:P, g * 64:(g + 1) * 64],
                                                                         scalar=ss4[i2][:P, g:g + 1], in1=qg_rep[:P, 0, :],
                                                                         op0=ALU.mult, op1=ALU.mult),
                                 [psb_, ss4b[i2], constb], [q16b[i2]])
                        for g in range(2):
                            S.op("dve", lambda h: h.scalar_tensor_tensor(out=kv32[:P, slot, g * 64:(g + 1) * 64],
                                                                         in0=ps[:P, 128 + g * 64:128 + (g + 1) * 64],
                                                                         scalar=ss4[i2][:P, 2 + g:3 + g], in1=kg_rep[:P, 0, :],
                                                                         op0=ALU.mult, op1=ALU.mult),
                                 [psb_, ss4b[i2], constb], [k32b])
                        S.op("act", lambda h: h.activation(out=kv32[:P, slot, 128:256], in_=ps[:P, 256:384], func=AF.Copy),
                             [psb_], [v32b])
                        ceng = "dve" if "nopool" in DEBUG_SKIP else "pool"
                        S.op(ceng, lambda h: h.tensor_copy(out=k16[i2][:P, :], in_=kv32[:P, slot, 0:128]), [k32b], [k16b[i2]])
                        S.op(ceng, lambda h: h.tensor_copy(out=V16[:P, t, :], in_=kv32[:P, slot, 128:256]), [v32b], [V16b])
                        pb = psB[t % 2]
                        S.op("pe", lambda h: h.transpose(pb[:, 0:P], q16[i2][:P, :], ident[:P, :P]), [q16b[i2], constb], [psBb[t % 2]], inc=False)
                        S.op("pe", lambda h: h.transpose(pb[:, 128:128 + P], k16[i2][:P, :], ident[:P, :P]), [k16b[i2], constb], [psBb[t % 2]])
                        evac(QKT[:, :, cols], pb[:, 0:256].rearrange("p (a t) -> p a t", a=2)[:, :, 0:P], [psBb[t % 2]], [QKTb])
                        if t == 8 and "kvout" not in DEBUG_SKIP:
                            S.dma("sp", ch_ko, kv_dev[hp, :, 0:9, :], kv32[:, 0:9, :], reads=[k32b])
                        if t == 16 and "kvout" not in DEBUG_SKIP:
                            S.dma("sp", ch_ko, kv_dev[hp, :, 9:17, :], kv32[:, 0:8, :], reads=[k32b])
                    ceng = "dve" if "nopool" in DEBUG_SKIP else "pool"
                    for hh in range(2):
                        r = slice(hh * 64, (hh + 1) * 64)
                        hq = (2 * hp + hh) * 8
                        S.op(ceng, lambda h: h.tensor_copy(out=Qbd[r, :, hp, hq:hq + 8],
                                                             in_=QKT[r, 0, SEQ:NT].rearrange("p (b t) -> p b t", t=8)),
                             [QKTb], [smpb])
                    S.op(ceng, lambda h: h.tensor_copy(out=KTs[:, hp, :], in_=QKT[:, 1, SEQ:NT]), [QKTb], [smpb])
                    S.op(ceng, lambda h: h.tensor_copy(out=Vs16[:, hp * 128:(hp + 1) * 128], in_=V16[0:NS, 16, :]), [V16b], [smpb])

                    for hh in range(2):
                        hd = 2 * hp + hh
                        r = slice(hh * 64, (hh + 1) * 64)
                        for sbk in ([] if "attn" in DEBUG_SKIP else range(4)):
                            Q0 = 512 * sbk
                            gi = grp[0] % 2; grp[0] += 1
                            po = psF[2 + gi]; pob = psFb[2 + gi]
                            S.op("pool", lambda h: h.memset(sps32[gi][:], 0.0), [], [sps32b[gi]])
                            kbs = list(range(4 * sbk + 3, -1, -1))
                            for ki, kb in enumerate(kbs):
                                first = ki == 0
                                last = ki == len(kbs) - 1
                                diag = kb >= 4 * sbk
                                qlo = (kb - 4 * sbk) * 128 if diag else 0
                                N = 512 - qlo
                                bi = blk[0] % 2; blk[0] += 1
                                pz = psF[4 + bi]; pzb = psFb[4 + bi]
                                S.op("pe", lambda h: h.matmul(pz[:, 0:N], lhsT=QKT[r, 1, kb * 128:(kb + 1) * 128],
                                                              rhs=QKT[r, 0, Q0 + qlo:Q0 + 512], start=True, stop=False,
                                                              skip_group_check=True), [QKTb], [pzb])
                                S.op("act", lambda h: h.activation(out=e32[bi][:, 0:N], in_=pz[:, 0:N], func=AF.Exp,
                                                                   bias=bias_rep[:, 0, hd:hd + 1], scale=0.125),
                                     [pzb, constb], [e32b[bi]])
                                if diag:
                                    S.op("dve", lambda h: h.tensor_tensor(out=e32[bi][:, 0:128], in0=e32[bi][:, 0:128], in1=maskd[:, :],
                                                                          op=ALU.mult), [e32b[bi], constb], [e32b[bi]])
                                S.op("act", lambda h: h.activation(out=spb16[bi][:, 0:N], in_=e32[bi][:, 0:N], func=AF.Ln,
                                                                   bias=1.0, scale=1.0), [e32b[bi]], [spbb[bi]])
                                S.op("pe", lambda h: h.matmul(pz[:, 0:N], lhsT=trim8[:, :], rhs=spb16[bi][:, 0:N], start=False,
                                                              stop=first, skip_group_check=True), [spbb[bi], constb, pzb], [pzb],
                                     inc=first)
                                if not first:
                                    S.op("pe", lambda h: h.matmul(pz[:, 0:N], lhsT=onesm8[:, :], rhs=sps16[gi][:, qlo:512], start=False,
                                                                  stop=True, skip_group_check=True), [sps16b[gi], constb, pzb], [pzb])
                                S.op("act", lambda h: h.activation(out=wb16[bi][:, 0:N], in_=pz[:, 0:N], func=AF.Exp,
                                                                   bias=bias_rep[:, 0, hd:hd + 1], scale=0.125),
                                     [pzb, constb], [wbb[bi]])
                                if diag:
                                    S.op("dve", lambda h: h.tensor_tensor(out=wb16[bi][:, 0:128], in0=wb16[bi][:, 0:128], in1=maskd[:, :],
                                                                          op=ALU.mult), [wbb[bi], constb], [wbb[bi]])
                                S.op("pe", lambda h: h.matmul(po[:, qlo:512], lhsT=V16[:, kb, :], rhs=wb16[bi][:, 0:N], start=first,
                                                              stop=last, skip_group_check=True), [V16b, wbb[bi], pob], [pob])
                                if not last:
                                    qn = (kbs[ki + 1] - 4 * sbk) * 128 if kbs[ki + 1] >= 4 * sbk else 0
                                    S.op("dve", lambda h: h.tensor_tensor(out=sps32[gi][:, qlo:512], in0=sps32[gi][:, qlo:512],
                                                                          in1=spb16[bi][:, 0:N], op=ALU.add),
                                         [sps32b[gi], spbb[bi]], [sps32b[gi]])
                                    S.op("pool", lambda h: h.tensor_copy(out=sps16[gi][:, qn:512], in_=sps32[gi][:, qn:512]),
                                         [sps32b[gi]], [sps16b[gi]])
                            evac(OT[r, hp, Q0:Q0 + 512], po[r, 0:512], [pob], [OTb])
                S.barrier()

            with ExitStack() as sc2:
                NRING = 8
                Kr = [sb(sc2, "Kr%d" % i, [128, D], BF16) for i in range(NRING)]
                Vr = [sb(sc2, "Vr%d" % i, [128, D], BF16) for i in range(NRING)]
                Krb = [Buf() for _ in range(NRING)]; Vrb = [Buf() for _ in range(NRING)]
                ch_K = [S.chan("K%d" % i) for i in range(NRING)]
                ch_V = [S.chan("V%d" % i) for i in range(NRING)]
                KTb_sb = [sb(sc2, "KTb%d" % i, [128, 8, 512], BF16) for i in range(2)]
                KTbb = [Buf(), Buf()]
                es32 = [sb(sc2, "es32%d" % i, [128, 512], F32) for i in range(2)]
                sp32 = [sb(sc2, "sp32%d" % i, [128, 512], F32) for i in range(2)]
                cs32 = [sb(sc2, "cs32%d" % i, [128, 513], F32) for i in range(2)]
                ec32 = [sb(sc2, "ec32%d" % i, [128, 512], F32) for i in range(2)]
                w16 = [sb(sc2, "w16%d" % i, [128, 512], BF16) for i in range(2)]
                wT = [sb(sc2, "wT%d" % i, [128, 4, 128], BF16) for i in range(2)]
                esb = [Buf(), Buf()]; spb_ = [Buf(), Buf()]; csb = [Buf(), Buf()]; ecb = [Buf(), Buf()]
                w16b = [Buf(), Buf()]; wTb = [Buf(), Buf()]
                ones32 = sb(sc2, "ones32", [128, 512], F32)
                masknew = sb(sc2, "masknew", [128, 4, 32], F32)
                negR = sb(sc2, "negR", [128, 1], F32)
                negRb = Buf()
                Of16 = sb(sc2, "Of16", [128, D], BF16)
                Ofb = Buf()
                pt_i = sb(sc2, "pt_i", [128, 1, 256], I32)
                pt_f = sb(sc2, "pt_f", [128, 256], F32)
                idx_i = sb(sc2, "idx_i", [128, 256], I32)
                iotap = sb(sc2, "iotap", [128, 1], F32)
                idxb = Buf(); c2b = Buf()
                ch_c2 = S.chan("c2")
                S.dma("sp", ch_c2, pt_i[:], ptab[0:1, :].partition_broadcast(128), writes=[idxb])
                S.dma("sp", ch_c2, iotap[:], c_iotap[:, :], writes=[idxb])
                S.dma("sp", ch_c2, masknew[:], c_masknew[:, :, :], writes=[c2b])
                S.op("pool", lambda h: h.memset(ones32[:], 1.0), [], [c2b])
                S.op("pool", lambda h: h.memset(cs32[0][:, 0:1], 0.0), [], [csb[0]])
                S.op("pool", lambda h: h.memset(cs32[1][:, 0:1], 0.0), [], [csb[1]])
                S.op("dve", lambda h: h.tensor_copy(out=pt_f[:, :], in_=pt_i[:, 0, :]), [idxb], [idxb])
                S.op("dve", lambda h: h.tensor_scalar(out=pt_f[:, :], in0=pt_f[:, :], scalar1=128.0, scalar2=iotap[:, 0:1],
                                                      op0=ALU.mult, op1=ALU.add), [idxb], [idxb])
                S.op("dve", lambda h: h.tensor_copy(out=idx_i[:, :], in_=pt_f[:, :]), [idxb], [idxb])

                ring = [0]; sblk = [0]; tb = [0]

                def stick_s(ps, psb_, Wd, bl, mask_ap):
                    i = sblk[0] % 2; sblk[0] += 1
                    S.op("act", lambda h: h.activation(out=es32[i][:, 0:Wd], in_=ps[:, 0:Wd], func=AF.Exp, bias=bias_hq[:, 0:1], scale=0.125),
                         [psb_, constb], [esb[i]])
                    if mask_ap is not None:
                        S.op("dve", lambda h: h.tensor_tensor(out=es32[i][:, 0:Wd], in0=es32[i][:, 0:Wd], in1=mask_ap, op=ALU.mult),
                             [esb[i], c2b], [esb[i]])
                    S.op("act", lambda h: h.activation(out=sp32[i][:, 0:Wd], in_=es32[i][:, 0:Wd], func=AF.Ln, bias=1.0, scale=1.0),
                         [esb[i]], [spb_[i]])
                    S.op("dve", lambda h: h.tensor_tensor_scan(out=cs32[i][:, 1:Wd + 1], data0=ones32[:, 0:Wd], data1=sp32[i][:, 0:Wd],
                                                               initial=0.0, op0=ALU.mult, op1=ALU.add), [spb_[i], c2b], [csb[i]])
                    S.op("dve", lambda h: h.tensor_tensor(out=negR[:, :], in0=negR[:, :], in1=cs32[i][:, Wd:Wd + 1], op=ALU.subtract),
                         [csb[i], negRb], [negRb])
                    S.op("act", lambda h: h.activation(out=ec32[i][:, 0:Wd], in_=cs32[i][:, 0:Wd], func=AF.Exp, bias=negR[:, 0:1], scale=1.0),
                         [csb[i], negRb], [ecb[i]])
                    S.op("dve", lambda h: h.tensor_tensor(out=w16[i][:, 0:Wd], in0=es32[i][:, 0:Wd], in1=ec32[i][:, 0:Wd], op=ALU.mult),
                         [esb[i], ecb[i]], [w16b[i]])
                    return i

                for bl in ([] if "sample_attn" in DEBUG_SKIP else range(4)):
                    pO = [psF[2], psF[3]]; pOb = [psFb[2], psFb[3]]
                    S.op("pool", lambda h: h.memset(negR[:], 0.0), [], [negRb])
                    ps = psF[0]; psb_ = psFb[0]
                    for c in range(8):
                        S.op("pe", lambda h: h.matmul(ps[:, 0:NS], lhsT=Qbd[:, bl, c, :], rhs=KTs[:, c, :], start=(c == 0), stop=(c == 7)),
                             [smpb], [psb_], inc=(c == 7))
                    i = stick_s(ps, psb_, NS, bl, masknew[:, bl, :])
                    ti = tb[0] % 2; tb[0] += 1
                    S.op("pe", lambda h: h.transpose(psB[ti][0:NS, 0:128], w16[i][:, 0:NS], ident[:, :]), [w16b[i], constb], [psBb[ti]])
                    evac(wT[ti][0:NS, 0, :], psB[ti][0:NS, 0:128], [psBb[ti]], [wTb[ti]])
                    for hv in range(2):
                        S.op("pe", lambda h: h.matmul(pO[hv][:, :], lhsT=wT[ti][0:NS, 0, :], rhs=Vs16[:, hv * 512:(hv + 1) * 512],
                                                      start=True, stop=False, skip_group_check=True), [wTb[ti], smpb], [pOb[hv]])
                    for jb in range(15, -1, -1):
                        kt = jb % 2
                        slots = []
                        for pgi in range(4):
                            pg = 4 * jb + pgi
                            sl = ring[0] % NRING; ring[0] += 1
                            slots.append(sl)
                            col = bl * 64 + pg
                            S.idma(ch_K[sl], Kr[sl][:, :], ck[:, :], idx_i[:, col:col + 1], reads=[idxb], writes=[Krb[sl]])
                            S.idma(ch_V[sl], Vr[sl][:, :], cv[:, :], idx_i[:, col:col + 1], reads=[idxb], writes=[Vrb[sl]])
                        for pgi in range(4):
                            sl = slots[pgi]
                            ti = tb[0] % 2; tb[0] += 1
                            for c in range(8):
                                S.op("pe", lambda h: h.transpose(psB[ti][:, c * 128:(c + 1) * 128], Kr[sl][:, c * 128:(c + 1) * 128], ident[:, :]),
                                     [Krb[sl], constb], [psBb[ti]], inc=(c == 7))
                            evac(KTb_sb[kt][:, :, pgi * 128:(pgi + 1) * 128], psB[ti].rearrange("p (c t) -> p c t", c=8),
                                 [psBb[ti]], [KTbb[kt]])
                        pi = jb % 2
                        ps = psF[pi]; psb_ = psFb[pi]
                        for c in range(8):
                            S.op("pe", lambda h: h.matmul(ps[:, 0:512], lhsT=Qbd[:, bl, c, :], rhs=KTb_sb[kt][:, c, :], start=(c == 0), stop=(c == 7)),
                                 [smpb, KTbb[kt]], [psb_], inc=(c == 7))
                        i = stick_s(ps, psb_, 512, bl, None)
                        ti = tb[0] % 2; tb[0] += 1
                        for pgi in range(4):
                            S.op("pe", lambda h: h.transpose(psB[ti][:, pgi * 128:(pgi + 1) * 128], w16[i][:, pgi * 128:(pgi + 1) * 128], ident[:, :]),
                                 [w16b[i], constb], [psBb[ti]], inc=(pgi == 3))
                        evac(wT[ti][:, :, :], psB[ti][:, 0:512].rearrange("p (a t) -> p a t", a=4), [psBb[ti]], [wTb[ti]])
                        for pgi in range(4):
                            sl = slots[pgi]
                            for hv in range(2):
                                lastmm = (jb == 0 and pgi == 3)
                                S.op("pe", lambda h: h.matmul(pO[hv][:, :], lhsT=wT[ti][:, pgi, :], rhs=Vr[sl][:, hv * 512:(hv + 1) * 512],
                                                              start=False, stop=lastmm, skip_group_check=True),
                                     [wTb[ti], Vrb[sl], pOb[hv]], [pOb[hv]])
                    for hv in range(2):
                        evac(Of16[:, hv * 512:(hv + 1) * 512], pO[hv][:, :], [pOb[hv]], [Ofb])
                    ti = tb[0] % 2; tb[0] += 1
                    for c in range(8):
                        S.op("pe", lambda h: h.transpose(psB[ti][:, c * 128:(c + 1) * 128], Of16[:, c * 128:(c + 1) * 128], ident[:, :]),
                             [Ofb, constb], [psBb[ti]], inc=(c == 7))
                    for c in range(8):
                        for hh in range(2):
                            r = slice(hh * 64, (hh + 1) * 64)
                            cc = c * 128 + (2 * c + hh) * 8
                            evac(OT[r, c, SEQ + bl * 8:SEQ + bl * 8 + 8], psB[ti][r, cc:cc + 8], [psBb[ti]], [OTb])
                S.barrier()

            with ExitStack() as sc3:
                wo_sb = sb(sc3, "wo_sb", [128, 8, D], BF16)
                wob = Buf(); ch_wo = S.chan("wo")
                S.dma("pool", ch_wo, wo_sb[:], w_o.rearrange("(c p) n -> p c n", p=128), writes=[wob])
                for s in ([] if "wo" in DEBUG_SKIP else range(17)):
                    P = tp(s)
                    for hv in range(2):
                        pi = (2 * s + hv) % 4
                        ps = psF[pi]; psb_ = psFb[pi]
                        for c in range(8):
                            S.op("pe", lambda h: h.matmul(ps[:P, :], lhsT=OT[:, c, tcols(s)], rhs=wo_sb[:, c, hv * 512:(hv + 1) * 512],
                                                          start=(c == 0), stop=(c == 7)), [OTb, wob], [psb_], inc=(c == 7))
                        S.op("dve", lambda h: h.tensor_tensor(out=x_sb[:P, s, hv * 512:(hv + 1) * 512], in0=x_sb[:P, s, hv * 512:(hv + 1) * 512],
                                                              in1=ps[:P, :], op=ALU.add), [psb_, xb[s]], [xb[s]])
                S.barrier()

        def conv_ffn(li):
            if "ffn" in DEBUG_SKIP:
                return
            with ExitStack() as sc:
                hT = sb(sc, "hTf", [128, 8, NT], BF16)
                hTb = Buf()
                rms_to_hT(norm_ffn[li:li + 1, :], hT, hTb)
                gT = sb(sc, "gT", [128, 8, NT], BF16)
                gTb = Buf()
                wu = [sb(sc, "wu%d" % i, [128, 8, 256], BF16) for i in range(2)]
                wub = [Buf(), Buf()]; ch_wu = [S.chan("wu%d_%d" % (li, i)) for i in range(2)]
                wd = sb(sc, "wd", [128, 8, D], BF16)
                wdb = Buf(); ch_wd = S.chan("wd%d" % li)
                cw = sb(sc, "cw", [128, NFC, 3], F32)
                cb = sb(sc, "cb", [128, NFC], F32)
                prevT = sb(sc, "prevT", [128, NFC, 4, 2], F32)
                cpb = Buf(); ch_cp = S.chan("cp%d" % li)
                a32 = [sb(sc, "a32%d" % i, [128, 2 + SEQ], F32) for i in range(2)]
                a32b = [Buf(), Buf()]
                as32 = [sb(sc, "as32%d" % i, [128, 4, 10], F32) for i in range(2)]
                as32b = [Buf(), Buf()]
                c32 = [sb(sc, "c32%d" % i, [128, 512], F32) for i in range(2)]
                s32 = [sb(sc, "s32%d" % i, [128, 512], F32) for i in range(2)]
                c32b = [Buf(), Buf()]; s32b = [Buf(), Buf()]
                cst = sb(sc, "cst", [128, NFC, 2], F32)
                csts = sb(sc, "csts", [128, NFC, 4, 2], F32)
                cstb = Buf(); ch_co = S.chan("co%d" % li)
                S.dma("sp", ch_cp, cw[:], conv_wl[li], writes=[cpb])
                S.dma("sp", ch_cp, cb[:], conv_bl[li], writes=[cpb])
                S.dma("sp", ch_cp, prevT[:], conv_prevT[li], writes=[cpb])
                for i in range(2):
                    S.op("pool", lambda h: h.memset(a32[i][:, 0:2], 0.0), [], [a32b[i]])
                w_up_v = w_up[li].rearrange("(c p) n -> p c n", p=128)

                def load_wu(fc, i):
                    S.dma("pool", ch_wu[i], wu[i][:, :, 0:128], w_up_v[:, :, fc * 128:(fc + 1) * 128], writes=[wub[i]])
                    S.dma("pool", ch_wu[i], wu[i][:, :, 128:256], w_up_v[:, :, DFF + fc * 128:DFF + (fc + 1) * 128], writes=[wub[i]])

                cnt = [0]
                load_wu(0, 0)
                fcn = 0
                for grp_fcs in FFN_GROUPS:
                    for k, fc in enumerate(grp_fcs):
                        wi = fcn % 2
                        if fc + 1 < NFC:
                            load_wu(fc + 1, (fcn + 1) % 2)
                        ai = fcn % 2
                        fcn += 1
                        for tbk in range(5):
                            T0 = tbk * 512
                            Nt = 512 if tbk < 4 else NS
                            pa = psF[(2 * cnt[0]) % 4]; pab = psFb[(2 * cnt[0]) % 4]
                            pb_ = psF[(2 * cnt[0] + 1) % 4]; pbb = psFb[(2 * cnt[0] + 1) % 4]
                            ci = cnt[0] % 2
                            cnt[0] += 1
                            for c in range(8):
                                S.op("pe", lambda h: h.matmul(pa[:, 0:Nt], lhsT=wu[wi][:, c, 0:128], rhs=hT[:, c, T0:T0 + Nt],
                                                              start=(c == 0), stop=(c == 7)), [wub[wi], hTb], [pab], inc=(c == 7))
                            for c in range(8):
                                S.op("pe", lambda h: h.matmul(pb_[:, 0:Nt], lhsT=wu[wi][:, c, 128:256], rhs=hT[:, c, T0:T0 + Nt],
                                                              start=(c == 0), stop=(c == 7)), [wub[wi], hTb], [pbb], inc=(c == 7))
                            if tbk < 4:
                                A = a32[ai]
                                S.op("act", lambda h: h.activation(out=A[:, 2 + T0:2 + T0 + Nt], in_=pa[:, 0:Nt], func=AF.Copy),
                                     [pab], [a32b[ai]])
                                srcs = [A[:, T0 + j:T0 + j + Nt] for j in range(3)]
                                cdst = c32[ci][:, 0:Nt]; sdst = s32[ci][:, 0:Nt]
                                gdst = gT[:, k, T0:T0 + Nt]
                                rd = [a32b[ai], cpb]
                            else:
                                A = as32[ai]
                                S.op("pool", lambda h: h.tensor_copy(out=A[:, :, 0:2], in_=prevT[:, fc, :, :]), [cpb], [as32b[ai]])
                                S.op("act", lambda h: h.activation(out=A[:, :, 2:10], in_=pa[:, 0:Nt].rearrange("p (b t) -> p b t", t=8),
                                                                   func=AF.Copy), [pab], [as32b[ai]])
                                srcs = [A[:, :, j:j + 8] for j in range(3)]
                                cdst = c32[ci][:, 0:Nt].rearrange("p (b t) -> p b t", t=8)
                                sdst = s32[ci][:, 0:Nt]
                                gdst = gT[:, k, T0:T0 + Nt]
                                rd = [as32b[ai], cpb]
                            S.op("dve", lambda h: h.tensor_scalar(out=cdst, in0=srcs[0], scalar1=cw[:, fc, 0:1], scalar2=cb[:, fc:fc + 1],
                                                                  op0=ALU.mult, op1=ALU.add), rd, [c32b[ci]])
                            for j in (1, 2):
                                S.op("dve", lambda h: h.scalar_tensor_tensor(out=cdst, in0=srcs[j], scalar=cw[:, fc, j:j + 1], in1=cdst,
                                                                             op0=ALU.mult, op1=ALU.add), rd + [c32b[ci]], [c32b[ci]])
                            S.op("act", lambda h: h.activation(out=sdst, in_=c32[ci][:, 0:Nt], func=AF.Silu), [c32b[ci]], [s32b[ci]])
                            S.op("dve", lambda h: h.tensor_tensor(out=gdst, in0=s32[ci][:, 0:Nt], in1=pb_[:, 0:Nt], op=ALU.mult),
                                 [s32b[ci], pbb], [gTb])
                        S.op("pool", lambda h: h.tensor_copy(out=cst[:, fc, :], in_=a32[ai][:, SEQ:SEQ + 2]), [a32b[ai]], [cstb])
                        S.op("pool", lambda h: h.tensor_copy(out=csts[:, fc, :, :], in_=as32[ai][:, :, 8:10]), [as32b[ai]], [cstb])
                        S.dma("pool", ch_wd, wd[:, k, :], w_down[li, fc * 128:(fc + 1) * 128, :], writes=[wdb])
                    ng = len(grp_fcs)
                    for s in range(17):
                        P = tp(s)
                        for hv in range(2):
                            pi = 4 + (2 * s + hv) % 2
                            ps = psF[pi]; psb_ = psFb[pi]
                            for k in range(ng):
                                S.op("pe", lambda h: h.matmul(ps[:P, :], lhsT=gT[:, k, tcols(s)], rhs=wd[:, k, hv * 512:(hv + 1) * 512],
                                                              start=(k == 0), stop=(k == ng - 1)), [gTb, wdb], [psb_], inc=(k == ng - 1))
                            S.op("dve", lambda h: h.tensor_tensor(out=x_sb[:P, s, hv * 512:(hv + 1) * 512],
                                                                  in0=x_sb[:P, s, hv * 512:(hv + 1) * 512], in1=ps[:P, :], op=ALU.add),
                                 [psb_, xb[s]], [xb[s]])
                S.dma("sp", ch_co, conv_p[li], cst[:], reads=[cstb])
                S.dma("sp", ch_co, conv_s[li], csts[:], reads=[cstb])
                S.barrier()

        conv_ffn(0)

        NCH = 260
        with ExitStack() as sc_s5:
          if "s5" not in DEBUG_SKIP:
              U = sb(sc_s5, "U", [128, 64, NCH], BF16)
              Ub = Buf()
              with ExitStack() as scu:
                  u_tok = sb(scu, "u_tok", [128, 2, 64, 128], BF16)
                  u_toks = sb(scu, "u_toks", [4, 64, 128], BF16)
                  utb = Buf()
                  with ExitStack() as sch:
                      hT = sb(sch, "hTs", [128, 8, NT], BF16)
                      hTb = Buf()
                      rms_to_hT(norm_mix[1:2, :], hT, hTb)
                      win = sb(sch, "win", [128, 8, 512], BF16)
                      winb = Buf(); ch_win = S.chan("win")
                      w_in_v = w_in.rearrange("(c p) n -> p c n", p=128)
                      n = 0
                      for hv in range(2):
                          S.dma("pool", ch_win, win[:], w_in_v[:, :, hv * 512:(hv + 1) * 512], writes=[winb])
                          for s in range(16):
                              ps = psF[n % 4]; psb_ = psFb[n % 4]; n += 1
                              for c in range(8):
                                  S.op("pe", lambda h: h.matmul(ps[:, :], lhsT=hT[:, c, tcols(s)], rhs=win[:, c, :],
                                                                start=(c == 0), stop=(c == 7)), [hTb, winb], [psb_], inc=(c == 7))
                              evac(u_tok[:, s // 8, hv * 32:(hv + 1) * 32, (s % 8) * 16:(s % 8 + 1) * 16], ps[:, :].rearrange("p (g c) -> p g c", c=16), [psb_], [utb])
                          for j in range(8):
                              ps = psF[n % 4]; psb_ = psFb[n % 4]; n += 1
                              for c in range(8):
                                  S.op("pe", lambda h: h.matmul(ps[0:4, :], lhsT=hT[:, c, SEQ + j:NT:8], rhs=win[:, c, :],
                                                                start=(c == 0), stop=(c == 7)), [hTb, winb], [psb_], inc=(c == 7))
                              evac(u_toks[0:4, hv * 32:(hv + 1) * 32, j * 16:(j + 1) * 16], ps[0:4, :].rearrange("p (g c) -> p g c", c=16), [psb_], [utb])
                      S.barrier()
                  for g0 in range(0, 64, 3):
                      gs = list(range(g0, min(g0 + 3, 64)))
                      ti = (g0 // 3) % 2
                      for gi, g in enumerate(gs):
                          base = gi * NCH
                          for hf in range(2):
                              S.op("pe", lambda h: h.transpose(psB[ti][:, base + hf * 128:base + (hf + 1) * 128],
                                                               u_tok[:, hf, g, :], ident[:, :]),
                                   [utb, constb], [psBb[ti]], inc=False)
                          S.op("pe", lambda h: h.transpose(psB[ti][:, base + 256:base + 260], u_toks[0:4, g, :], ident[0:4, 0:4]),
                               [utb, constb], [psBb[ti]], inc=(gi == len(gs) - 1))
                      evac(U[:, g0:g0 + len(gs), :], psB[ti][:, 0:len(gs) * NCH].rearrange("p (g n) -> p g n", n=NCH), [psBb[ti]], [Ub])
                  S.barrier()

              with ExitStack() as scw:
                  Mw = sb(scw, "Mw", [128, 64, 128], BF16)
                  WBre = sb(scw, "WBre", [128, 32, 128], BF16)
                  WBim = sb(scw, "WBim", [128, 32, 128], BF16)
                  WCre = sb(scw, "WCre", [128, 32, 128], BF16)
                  nWCim = sb(scw, "nWCim", [128, 32, 128], BF16)
                  a8re = sb(scw, "a8re", [128, 32], F32); a8im = sb(scw, "a8im", [128, 32], F32); na8im = sb(scw, "na8im", [128, 32], F32)
                  r8 = sb(scw, "r8", [128, 32], F32); phir = sb(scw, "phir", [128, 32], F32)
                  dlay = sb(scw, "dlay", [128, 64], F32)
                  spr = sb(scw, "spr", [128, 32, 4], F32); spi = sb(scw, "spi", [128, 32, 4], F32)
                  spr16 = sb(scw, "spr16", [128, 32, 4], BF16); spi16 = sb(scw, "spi16", [128, 32, 4], BF16)
                  fin_re = sb(scw, "fin_re", [128, 32], F32); fin_im = sb(scw, "fin_im", [128, 32], F32)
                  fins_re = sb(scw, "fins_re", [128, 32, 4], F32); fins_im = sb(scw, "fins_im", [128, 32, 4], F32)
                  iotan = sb(scw, "iotan", [128, 256], F32)
                  maskM = sb(scw, "maskM", [128, 128], F32)
                  Wb = Buf(); prmb = Buf(); finb = Buf()
                  ch_p = S.chan("s5p"); ch_fo = S.chan("s5o")
                  with ExitStack() as scp:
                      def t32(name, shape):
                          return sb(scp, name, shape, F32)
                      lr = t32("lr", [128, 32]); lim = t32("lim", [128, 32]); ldt = t32("ldt", [128, 32])
                      Bre = t32("Bre", [128, 32, 16]); Bim = t32("Bim", [128, 32, 16])
                      Cre = t32("Cre", [128, 32, 16]); Cim = t32("Cim", [128, 32, 16])
                      for dst, src in ((lr, lamT_re), (lim, lamT_im), (ldt, logdtT), (Bre, BT_re), (Bim, BT_im), (Cre, CT_re), (Cim, CT_im),
                                       (dlay, d_lay), (spr, sprev_re), (spi, sprev_im), (iotan, c_iotan), (maskM, c_maskM)):
                          S.dma("sp", ch_p, dst[:], src, writes=[prmb])
                      dt_ = t32("dt_", [128, 32]); lrdt = t32("lrdt", [128, 32]); lidt = t32("lidt", [128, 32]); mag = t32("mag", [128, 32])
                      ang = t32("ang", [128, 32]); sinv = t32("sinv", [128, 32]); cosv = t32("cosv", [128, 32])
                      are = t32("are", [128, 32]); aim = t32("aim", [128, 32]); den = t32("den", [128, 32]); tmp = t32("tmp", [128, 32]); tmp2 = t32("tmp2", [128, 32])
                      zr = t32("zr", [128, 32]); zi = t32("zi", [128, 32]); am1 = t32("am1", [128, 32])
                      Apr = t32("Apr", [128, 32, 9]); Api = t32("Api", [128, 32, 9])
                      Bbr = t32("Bbr", [128, 32, 16]); Bbi = t32("Bbi", [128, 32, 16])
                      T1 = t32("T1", [128, 32, 16]); T2 = t32("T2", [128, 32, 16])
                      Aipr = t32("Aipr", [128, 32, 8]); Aipi = t32("Aipi", [128, 32, 8])
                      ivr = t32("ivr", [128, 32]); ivi = t32("ivi", [128, 32]); m2 = t32("m2", [128, 32])

                      def P_(eng, fn):
                          S.op(eng, fn, [prmb], [prmb])

                      def tt(out, a, b, op, eng="dve"):
                          P_(eng, lambda h: h.tensor_tensor(out=out, in0=a, in1=b, op=op))

                      def ts(out, a, s1, s2, op0, op1=None, eng="dve"):
                          if op1 is None:
                              P_(eng, lambda h: h.tensor_scalar(out=out, in0=a, scalar1=s1, scalar2=None, op0=op0))
                          else:
                              P_(eng, lambda h: h.tensor_scalar(out=out, in0=a, scalar1=s1, scalar2=s2, op0=op0, op1=op1))

                      def act(out, a, func, bias=0.0, scale=1.0):
                          P_("act", lambda h: h.activation(out=out, in_=a, func=func, bias=bias, scale=scale))

                      def cmul(o_re, o_im, ar, ai, br, bi, t1, t2, neg_im=False):
                          tt(t1, ar, br, ALU.mult); tt(t2, ai, bi, ALU.mult)
                          tt(o_re, t1, t2, ALU.subtract)
                          tt(t1, ar, bi, ALU.mult); tt(t2, ai, br, ALU.mult)
                          if neg_im:
                              tt(t1, t1, t2, ALU.add)
                              ts(o_im, t1, -1.0, None, ALU.mult)
                          else:
                              tt(o_im, t1, t2, ALU.add)

                      act(dt_[:], ldt[:], AF.Exp)
                      tt(lrdt[:], lr[:], dt_[:], ALU.mult); tt(lidt[:], lim[:], dt_[:], ALU.mult)
                      act(mag[:], lrdt[:], AF.Exp)
                      ki = sb(scp, "ki", [128, 32], I32)

                      def reduce_pi(dst, x):
                          ts(ki[:], x, 1.0 / (2 * PI), None, ALU.mult)
                          P_("dve", lambda h: h.scalar_tensor_tensor(out=dst, in0=ki[:], scalar=-2 * PI, in1=x, op0=ALU.mult, op1=ALU.add))
                          ts(dst, dst, -PI_SAFE, PI_SAFE, ALU.max, ALU.min)

                      reduce_pi(ang[:], lidt[:])
                      act(sinv[:], ang[:], AF.Sin)
                      act(tmp[:], ang[:], AF.Abs)
                      act(cosv[:], tmp[:], AF.Sin, bias=PI / 2, scale=-1.0)
                      tt(are[:], mag[:], cosv[:], ALU.mult); tt(aim[:], mag[:], sinv[:], ALU.mult)
                      tt(den[:], lr[:], lr[:], ALU.mult); tt(tmp[:], lim[:], lim[:], ALU.mult); tt(den[:], den[:], tmp[:], ALU.add)
                      P_("dve", lambda h: h.reciprocal(out=den[:], in_=den[:]))
                      ts(am1[:], are[:], -1.0, None, ALU.add)
                      tt(tmp[:], am1[:], lr[:], ALU.mult); tt(tmp2[:], aim[:], lim[:], ALU.mult); tt(tmp[:], tmp[:], tmp2[:], ALU.add)
                      tt(zr[:], tmp[:], den[:], ALU.mult)
                      tt(tmp[:], aim[:], lr[:], ALU.mult); tt(tmp2[:], am1[:], lim[:], ALU.mult); tt(tmp[:], tmp[:], tmp2[:], ALU.subtract)
                      tt(zi[:], tmp[:], den[:], ALU.mult)
                      zrb = zr[:, :].unsqueeze(2).broadcast_to([128, 32, 16]); zib = zi[:, :].unsqueeze(2).broadcast_to([128, 32, 16])
                      cmul(Bbr[:], Bbi[:], zrb, zib, Bre[:], Bim[:], T1[:], T2[:])
                      P_("dve", lambda h: h.memset(Apr[:, :, 0], 1.0)); P_("dve", lambda h: h.memset(Api[:, :, 0], 0.0))
                      for k in range(1, 9):
                          cmul(Apr[:, :, k], Api[:, :, k], Apr[:, :, k - 1], Api[:, :, k - 1], are[:], aim[:], tmp[:], tmp2[:])
                      tt(m2[:], mag[:], mag[:], ALU.mult)
                      P_("dve", lambda h: h.reciprocal(out=m2[:], in_=m2[:]))
                      tt(ivr[:], are[:], m2[:], ALU.mult); tt(ivi[:], aim[:], m2[:], ALU.mult); ts(ivi[:], ivi[:], -1.0, None, ALU.mult)
                      P_("dve", lambda h: h.memset(Aipr[:, :, 0], 1.0)); P_("dve", lambda h: h.memset(Aipi[:, :, 0], 0.0))
                      for k in range(1, 8):
                          cmul(Aipr[:, :, k], Aipi[:, :, k], Aipr[:, :, k - 1], Aipi[:, :, k - 1], ivr[:], ivi[:], tmp[:], tmp2[:])
                      P_("dve", lambda h: h.tensor_copy(out=a8re[:], in_=Apr[:, :, 8])); P_("dve", lambda h: h.tensor_copy(out=a8im[:], in_=Api[:, :, 8]))
                      ts(na8im[:], a8im[:], -1.0, None, ALU.mult)
                      ts(tmp[:], lrdt[:], 8.0, None, ALU.mult)
                      act(r8[:], tmp[:], AF.Exp)
                      ts(tmp2[:], lidt[:], 8.0, None, ALU.mult)
                      reduce_pi(phir[:], tmp2[:])

                      def b16(t, gsl, k):
                          ng = gsl.stop - gsl.start
                          return t[:, gsl, k:k + 1].broadcast_to([128, ng, 16])

                      allg = slice(0, 32)
                      for i in range(8):
                          cmul(WCre[:, :, i * 16:(i + 1) * 16], nWCim[:, :, i * 16:(i + 1) * 16], b16(Apr, allg, i + 1), b16(Api, allg, i + 1),
                               Cre[:], Cim[:], T1[:], T2[:], neg_im=True)
                      P_("dve", lambda h: h.tensor_copy(out=spr16[:], in_=spr[:])); P_("dve", lambda h: h.tensor_copy(out=spi16[:], in_=spi[:]))
                      for half in range(2):
                          gsl = slice(16 * half, 16 * half + 16)
                          with ExitStack() as sch2:
                              Lre = sb(sch2, "Lre", [128, 16, 128], BF16); nLim = sb(sch2, "nLim", [128, 16, 128], BF16)
                              Rre = sb(sch2, "Rre", [128, 16, 128], BF16); Rim = sb(sch2, "Rim", [128, 16, 128], BF16)
                              WBr_qp = sb(sch2, "WBr_qp", [128, 16, 128], BF16); WBi_qp = sb(sch2, "WBi_qp", [128, 16, 128], BF16)
                              t1 = T1[:, 0:16, :]; t2 = T2[:, 0:16, :]
                              for j in range(8):
                                  js = slice(j * 16, (j + 1) * 16)
                                  cmul(Lre[:, :, js], nLim[:, :, js], b16(Aipr, gsl, j), b16(Aipi, gsl, j), Bbr[:, gsl, :], Bbi[:, gsl, :], t1, t2, neg_im=True)
                                  cmul(Rre[:, :, js], Rim[:, :, js], b16(Apr, gsl, j), b16(Api, gsl, j), Cre[:, gsl, :], Cim[:, gsl, :], t1, t2)
                                  cmul(WBr_qp[:, :, js], WBi_qp[:, :, js], b16(Apr, gsl, 7 - j), b16(Api, gsl, 7 - j), Bbr[:, gsl, :], Bbi[:, gsl, :], t1, t2)
                              for gl_ in range(16):
                                  gp = 16 * half + gl_
                                  ti = gp % 2
                                  S.op("pe", lambda h: h.transpose(psB[ti][:, 0:128], WBr_qp[:, gl_, :], ident[:, :]), [prmb, constb], [psBb[ti]], inc=False)
                                  S.op("pe", lambda h: h.transpose(psB[ti][:, 128:256], WBi_qp[:, gl_, :], ident[:, :]), [prmb, constb], [psBb[ti]])
                                  evac(WBre[:, gp, :], psB[ti][:, 0:128], [psBb[ti]], [Wb])
                                  evac(WBim[:, gp, :], psB[ti][:, 128:256], [psBb[ti]], [Wb])
                              for gg in range(32):
                                  gl_, q = divmod(gg, 2)
                                  g = 32 * half + gg
                                  r = slice(q * 64, (q + 1) * 64)
                                  pi = g % 4
                                  S.op("pe", lambda h: h.matmul(psF[pi][:, 0:128], lhsT=Lre[r, gl_, :], rhs=Rre[r, gl_, :], start=True, stop=False),
                                       [prmb], [psFb[pi]], inc=False)
                                  S.op("pe", lambda h: h.matmul(psF[pi][:, 0:128], lhsT=nLim[r, gl_, :], rhs=Rim[r, gl_, :], start=False, stop=True),
                                       [prmb], [psFb[pi]])
                                  S.op("dve", lambda h: h.tensor_tensor(out=Mw[:, g, :], in0=psF[pi][:, 0:128], in1=maskM[:, :], op=ALU.mult),
                                       [psFb[pi], prmb], [Wb])
                              S.barrier()
                      S.barrier()

                  with ExitStack() as scc:
                      def t32(name, shape):
                          return sb(scc, name, shape, F32)
                      NB = 2
                      angt = [t32("angt%d" % i, [128, 256]) for i in range(NB)]
                      angc = [t32("angc%d" % i, [128, 256]) for i in range(NB)]
                      kI = [sb(scc, "kI%d" % i, [128, 256], I32) for i in range(NB)]
                      sn = [t32("sn%d" % i, [128, 256]) for i in range(NB)]
                      cn = [t32("cn%d" % i, [128, 256]) for i in range(NB)]
                      Fre = [t32("Fre%d" % i, [128, NCH]) for i in range(NB)]
                      Fim = [t32("Fim%d" % i, [128, NCH]) for i in range(NB)]
                      Gre = [t32("Gre%d" % i, [128, 256]) for i in range(NB)]
                      Gim = [t32("Gim%d" % i, [128, 256]) for i in range(NB)]
                      Tre = [t32("Tre%d" % i, [128, 256]) for i in range(NB)]
                      Tim = [t32("Tim%d" % i, [128, 256]) for i in range(NB)]
                      q1 = [t32("q1%d" % i, [128, 256]) for i in range(NB)]
                      q2 = [t32("q2%d" % i, [128, 256]) for i in range(NB)]
                      Sre = [t32("Sre%d" % i, [128, 256]) for i in range(NB)]
                      Sim = [t32("Sim%d" % i, [128, 256]) for i in range(NB)]
                      Sxr = [sb(scc, "Sxr%d" % i, [128, 257], BF16) for i in range(NB)]
                      Sxi = [sb(scc, "Sxi%d" % i, [128, 257], BF16) for i in range(NB)]
                      y32 = [t32("y32%d" % i, [128, NCH]) for i in range(2)]
                      z32 = [t32("z32%d" % i, [128, NCH]) for i in range(2)]
                      tb_ = [Buf() for _ in range(NB)]
                      yb = [Buf(), Buf()]
                      for i in range(NB):
                          S.op("pool", lambda h: h.memset(Sxr[i][:, 0:1], 0.0), [], [tb_[i]])
                          S.op("pool", lambda h: h.memset(Sxi[i][:, 0:1], 0.0), [], [tb_[i]])
                      GC = math.sqrt(2.0 / PI)
                      for gp in range(32):
                          i = gp % NB
                          B_ = tb_[i]
                          S.op("pool", lambda h: h.tensor_scalar(out=angc[i][:], in0=iotan[:], scalar1=phir[:, gp:gp + 1], scalar2=None,
                                                                 op0=ALU.mult), [prmb, B_], [B_])
                          S.op("pool", lambda h: h.tensor_scalar(out=kI[i][:], in0=angc[i][:], scalar1=1.0 / (2 * PI), scalar2=None,
                                                                 op0=ALU.mult), [B_], [B_])
                          S.op("dve", lambda h: h.scalar_tensor_tensor(out=angt[i][:], in0=kI[i][:], scalar=-2 * PI, in1=angc[i][:],
                                                                        op0=ALU.mult, op1=ALU.add), [B_], [B_])
                          S.op("pool", lambda h: h.tensor_scalar(out=angt[i][:], in0=angt[i][:], scalar1=-PI_SAFE, scalar2=PI_SAFE,
                                                                 op0=ALU.max, op1=ALU.min), [B_], [B_])
                          S.op("act", lambda h: h.activation(out=sn[i][:], in_=angt[i][:], func=AF.Sin), [B_], [B_])
                          S.op("act", lambda h: h.activation(out=angc[i][:], in_=angt[i][:], func=AF.Abs), [B_], [B_])
                          S.op("act", lambda h: h.activation(out=cn[i][:], in_=angc[i][:], func=AF.Sin, bias=PI / 2, scale=-1.0), [B_], [B_])
                          for q in range(2):
                              g = 2 * gp + q
                              r = slice(q * 64, (q + 1) * 64)
                              S.op("pe", lambda h: h.matmul(psF[0][r, 0:NCH], lhsT=WBre[:, gp, r], rhs=U[:, g, :], start=True, stop=True),
                                   [Wb, Ub], [psFb[0]])
                              S.op("pe", lambda h: h.matmul(psF[1][r, 0:NCH], lhsT=WBim[:, gp, r], rhs=U[:, g, :], start=True, stop=True),
                                   [Wb, Ub], [psFb[1]])
                          S.op("act", lambda h: h.activation(out=Fre[i][:], in_=psF[0][:, 0:NCH], func=AF.Copy), [psFb[0], B_], [B_])
                          S.op("act", lambda h: h.activation(out=Fim[i][:], in_=psF[1][:, 0:NCH], func=AF.Copy), [psFb[1], B_], [B_])

                          def D_(fn, eng="dve"):
                              S.op(eng, fn, [B_, prmb], [B_])
                          D_(lambda h: h.tensor_tensor(out=q1[i][:], in0=cn[i][:], in1=Fre[i][:, 0:256], op=ALU.mult))
                          D_(lambda h: h.tensor_tensor(out=q2[i][:], in0=sn[i][:], in1=Fim[i][:, 0:256], op=ALU.mult), "pool")
                          D_(lambda h: h.tensor_tensor(out=Gre[i][:], in0=q1[i][:], in1=q2[i][:], op=ALU.add))
                          D_(lambda h: h.tensor_tensor(out=q1[i][:], in0=cn[i][:], in1=Fim[i][:, 0:256], op=ALU.mult))
                          D_(lambda h: h.tensor_tensor(out=q2[i][:], in0=sn[i][:], in1=Fre[i][:, 0:256], op=ALU.mult), "pool")
                          D_(lambda h: h.tensor_tensor(out=Gim[i][:], in0=q1[i][:], in1=q2[i][:], op=ALU.subtract))
                          D_(lambda h: h.tensor_tensor_scan(out=Tre[i][:], data0=r8[:, gp:gp + 1].broadcast_to([128, 256]), data1=Gre[i][:], initial=0.0, op0=ALU.mult, op1=ALU.add))
                          D_(lambda h: h.tensor_tensor_scan(out=Tim[i][:], data0=r8[:, gp:gp + 1].broadcast_to([128, 256]), data1=Gim[i][:], initial=0.0, op0=ALU.mult, op1=ALU.add))
                          D_(lambda h: h.tensor_tensor(out=q1[i][:], in0=cn[i][:], in1=Tre[i][:], op=ALU.mult))
                          D_(lambda h: h.tensor_tensor(out=q2[i][:], in0=sn[i][:], in1=Tim[i][:], op=ALU.mult), "pool")
                          D_(lambda h: h.tensor_tensor(out=Sre[i][:], in0=q1[i][:], in1=q2[i][:], op=ALU.subtract))
                          D_(lambda h: h.tensor_tensor(out=q1[i][:], in0=cn[i][:], in1=Tim[i][:], op=ALU.mult))
                          D_(lambda h: h.tensor_tensor(out=q2[i][:], in0=sn[i][:], in1=Tre[i][:], op=ALU.mult), "pool")
                          D_(lambda h: h.tensor_tensor(out=Sim[i][:], in0=q1[i][:], in1=q2[i][:], op=ALU.add))
                          D_(lambda h: h.tensor_copy(out=Sxr[i][:, 1:257], in_=Sre[i][:]), "pool")
                          D_(lambda h: h.tensor_copy(out=Sxi[i][:, 1:257], in_=Sim[i][:]), "pool")
                          S.op("pool", lambda h: h.tensor_copy(out=fin_re[:, gp:gp + 1], in_=Sre[i][:, 255:256]), [B_], [finb])
                          S.op("pool", lambda h: h.tensor_copy(out=fin_im[:, gp:gp + 1], in_=Sim[i][:, 255:256]), [B_], [finb])
                          S.op("dve", lambda h: h.scalar_tensor_tensor(out=q1[i][:, 0:4], in0=spr[:, gp, :], scalar=a8re[:, gp:gp + 1], in1=Fre[i][:, 256:260],
                                                                       op0=ALU.mult, op1=ALU.add), [B_, prmb], [B_])
                          S.op("dve", lambda h: h.scalar_tensor_tensor(out=fins_re[:, gp, :], in0=spi[:, gp, :], scalar=na8im[:, gp:gp + 1], in1=q1[i][:, 0:4],
                                                                       op0=ALU.mult, op1=ALU.add), [B_, prmb], [finb])
                          S.op("dve", lambda h: h.scalar_tensor_tensor(out=q2[i][:, 0:4], in0=spi[:, gp, :], scalar=a8re[:, gp:gp + 1], in1=Fim[i][:, 256:260],
                                                                       op0=ALU.mult, op1=ALU.add), [B_, prmb], [B_])
                          S.op("dve", lambda h: h.scalar_tensor_tensor(out=fins_im[:, gp, :], in0=spr[:, gp, :], scalar=a8im[:, gp:gp + 1], in1=q2[i][:, 0:4],
                                                                       op0=ALU.mult, op1=ALU.add), [B_, prmb], [finb])
                          for q in range(2):
                              g = 2 * gp + q
                              r = slice(q * 64, (q + 1) * 64)
                              pi = 2 + g % 2
                              py = psF[pi]; pyb = psFb[pi]
                              S.op("pe", lambda h: h.matmul(py[:, 0:NCH], lhsT=Mw[:, g, :], rhs=U[:, g, :], start=True, stop=False, skip_group_check=True),
                                   [Wb, Ub], [pyb], inc=False)
                              S.op("pe", lambda h: h.matmul(py[:, 0:256], lhsT=WCre[r, gp, :], rhs=Sxr[i][r, 0:256], start=False, stop=False, skip_group_check=True),
                                   [prmb, B_], [pyb], inc=False)
                              S.op("pe", lambda h: h.matmul(py[:, 0:256], lhsT=nWCim[r, gp, :], rhs=Sxi[i][r, 0:256], start=False, stop=False, skip_group_check=True),
                                   [prmb, B_], [pyb], inc=False)
                              S.op("pe", lambda h: h.matmul(py[:, 256:260], lhsT=WCre[r, gp, :], rhs=spr16[r, gp, :], start=False, stop=False, skip_group_check=True),
                                   [prmb], [pyb], inc=False)
                              S.op("pe", lambda h: h.matmul(py[:, 256:260], lhsT=nWCim[r, gp, :], rhs=spi16[r, gp, :], start=False, stop=True, skip_group_check=True),
                                   [prmb], [pyb])
                              yi = g % 2
                              S.op("dve", lambda h: h.scalar_tensor_tensor(out=y32[yi][:], in0=U[:, g, :], scalar=dlay[:, g:g + 1], in1=py[:, 0:NCH],
                                                                           op0=ALU.mult, op1=ALU.add), [pyb, Ub, prmb], [yb[yi]])
                              S.op("pool", lambda h: h.tensor_tensor(out=z32[yi][:], in0=y32[yi][:], in1=y32[yi][:], op=ALU.mult), [yb[yi]], [yb[yi]])
                              S.op("pool", lambda h: h.tensor_scalar(out=z32[yi][:], in0=z32[yi][:], scalar1=0.044715, scalar2=1.0, op0=ALU.mult, op1=ALU.add),
                                   [yb[yi]], [yb[yi]])
                              S.op("pool", lambda h: h.tensor_tensor(out=z32[yi][:], in0=z32[yi][:], in1=y32[yi][:], op=ALU.mult), [yb[yi]], [yb[yi]])
                              S.op("pool", lambda h: h.tensor_scalar(out=y32[yi][:], in0=y32[yi][:], scalar1=0.5, scalar2=None, op0=ALU.mult),
                                   [yb[yi]], [yb[yi]])
                              S.op("act", lambda h: h.activation(out=z32[yi][:], in_=z32[yi][:], func=AF.Tanh, scale=GC), [yb[yi]], [yb[yi]])
                              S.op("dve", lambda h: h.scalar_tensor_tensor(out=U[:, g, :], in0=z32[yi][:], scalar=1.0, in1=y32[yi][:],
                                                                           op0=ALU.add, op1=ALU.mult), [yb[yi], Ub], [Ub])
                      S.dma("sp", ch_fo, sre_p[:, :], fin_re[:], reads=[finb])
                      S.dma("sp", ch_fo, sim_p[:, :], fin_im[:], reads=[finb])
                      S.dma("sp", ch_fo, sre_s[:, :, :], fins_re[:], reads=[finb])
                      S.dma("sp", ch_fo, sim_s[:, :, :], fins_im[:], reads=[finb])
                      S.barrier()

              with ExitStack() as scg:
                  sel = sb(scg, "sel", [128, 64, 128], BF16)
                  selb = Buf(); ch_sel = S.chan("sel")
                  S.dma("pool", ch_sel, sel[:], c_sel[:, :, :], writes=[selb])
                  gTs = sb(scg, "gTs", [128, 8, NT], BF16)
                  gTsb = Buf()
                  n = 0
                  for fcn in range(8):
                      for i in range(8):
                          ps = psF[n % 4]; psb_ = psFb[n % 4]; n += 1
                          for gl in range(8):
                              S.op("pe", lambda h: h.matmul(ps[:, 0:NCH], lhsT=sel[:, gl * 8 + i, :], rhs=U[:, 8 * fcn + gl, :],
                                                            start=(gl == 0), stop=(gl == 7)), [selb, Ub], [psb_], inc=(gl == 7))
                          evac(gTs[:, fcn, i:NT:8], ps[:, 0:NCH], [psb_], [gTsb])
                  wg = [sb(scg, "wg%d" % i, [128, 8, 1024], BF16) for i in range(2)]
                  wgb = [Buf(), Buf()]; ch_wg = [S.chan("wg0"), S.chan("wg1")]
                  sg32 = [sb(scg, "sg32%d" % i, [128, 512], F32) for i in range(2)]
                  sgb = [Buf(), Buf()]
                  w_glu_v = w_glu.rearrange("(c p) n -> p c n", p=128)
                  for hv in range(2):
                      S.dma("pool", ch_wg[hv], wg[hv][:, :, 0:512], w_glu_v[:, :, hv * 512:(hv + 1) * 512], writes=[wgb[hv]])
                      S.dma("pool", ch_wg[hv], wg[hv][:, :, 512:1024], w_glu_v[:, :, D + hv * 512:D + (hv + 1) * 512], writes=[wgb[hv]])
                  n = 0
                  for hv in range(2):
                      for s in range(17):
                          P = tp(s)
                          pa = psF[(2 * n) % 4]; pab = psFb[(2 * n) % 4]
                          pb_ = psF[(2 * n + 1) % 4]; pbb = psFb[(2 * n + 1) % 4]
                          si = n % 2
                          n += 1
                          for c in range(8):
                              S.op("pe", lambda h: h.matmul(pa[:P, :], lhsT=gTs[:, c, tcols(s)], rhs=wg[hv][:, c, 0:512], start=(c == 0), stop=(c == 7)),
                                   [gTsb, wgb[hv]], [pab], inc=(c == 7))
                          for c in range(8):
                              S.op("pe", lambda h: h.matmul(pb_[:P, :], lhsT=gTs[:, c, tcols(s)], rhs=wg[hv][:, c, 512:1024], start=(c == 0), stop=(c == 7)),
                                   [gTsb, wgb[hv]], [pbb], inc=(c == 7))
                          S.op("act", lambda h: h.activation(out=sg32[si][:P, :], in_=pb_[:P, :], func=AF.Sigmoid), [pbb], [sgb[si]])
                          S.op("dve", lambda h: h.tensor_tensor(out=sg32[si][:P, :], in0=sg32[si][:P, :], in1=pa[:P, :], op=ALU.mult), [sgb[si], pab], [sgb[si]])
                          S.op("dve", lambda h: h.tensor_tensor(out=x_sb[:P, s, hv * 512:(hv + 1) * 512], in0=x_sb[:P, s, hv * 512:(hv + 1) * 512],
                                                                in1=sg32[si][:P, :], op=ALU.add), [sgb[si], xb[s]], [xb[s]])
                  S.barrier()

        conv_ffn(1)

        ch_y = S.chan("yout")
        for hf in range(2):
            S.dma("sp", ch_y, y_p[hf * 1024:(hf + 1) * 1024, :].rearrange("(p j) d -> p j d", j=8), x_sb[:, hf * 8:(hf + 1) * 8, :],
                  reads=xb[hf * 8:(hf + 1) * 8])
        S.dma("sp", ch_y, y_s[:, :], x_sb[0:NS, 16, :], reads=[xb[16]])
        S.barrier()
    return nc


def _consts():
    c = {}
    c["c_ident"] = np.eye(128, dtype=np.float32)
    j = np.arange(128)
    c["c_trim8"] = np.where(j[:, None] >= j[None, :], -8.0, 0.0).astype(np.float32)
    c["c_onesm8"] = np.full((128, 128), -8.0, np.float32)
    c["c_maskd"] = (j[:, None] < j[None, :]).astype(np.float32)
    hq = np.arange(128)
    tq = hq % 8
    mn = np.zeros((128, 4, 32), np.float32)
    for bl in range(4):
        for t in range(8):
            mn[:, bl, bl * 8 + t] = (t < tq).astype(np.float32)
    c["c_masknew"] = mn
    jj = np.arange(128) // 16
    c["c_maskM"] = (jj[None, :] >= jj[:, None]).astype(np.float32)
    sel = np.zeros((128, 64, 128), np.float32)
    for gl in range(8):
        for i in range(8):
            for cc in range(16):
                sel[i * 16 + cc, gl * 8 + i, gl * 16 + cc] = 1.0
    c["c_sel"] = sel
    c["c_iotap"] = np.arange(128, dtype=np.float32).reshape(128, 1)
    c["c_iotan"] = np.tile(np.arange(256, dtype=np.float32)[None, :], (128, 1))
    return c


def _qp(a):
    rest = a.shape[2:]
    return np.ascontiguousarray(a.reshape((32, 2, 64) + rest).transpose((1, 2, 0) + tuple(range(3, 3 + len(rest)))).reshape((128, 32) + rest))


def _qp_inv(a):
    rest = a.shape[2:]
    return a.reshape((2, 64, 32) + rest).transpose((2, 0, 1) + tuple(range(3, 3 + len(rest)))).reshape((64, 64) + rest)


def _host_prep(x_prompt, x_sample, cache_k, cache_v, state_ssm_re, state_ssm_im, state_ffn_conv, page_table,
               norm_mix, norm_ffn, attn_w_qkv, attn_q_gain, attn_k_gain, attn_logit_bias, attn_w_o,
               ssm_w_in, ssm_lambda_re, ssm_lambda_im, ssm_log_dt, ssm_b_re, ssm_b_im, ssm_c_re, ssm_c_im,
               ssm_d, ssm_w_glu, ffn_w_up, ffn_conv_w, ffn_conv_b, ffn_w_down, cores=None):
    f = lambda a: np.ascontiguousarray(np.asarray(a, dtype=np.float32))
    shared = dict(_consts())
    ckf = f(cache_k).reshape(2560 * 128, D)
    cvf = f(cache_v).reshape(2560 * 128, D)
    shared.update(
        ck=ckf, cv=cvf, norm_mix=f(norm_mix), norm_ffn=f(norm_ffn), w_qkv=f(attn_w_qkv)[0], q_gain=f(attn_q_gain), k_gain=f(attn_k_gain),
        lbias=f(attn_logit_bias), lbias_hq=f(np.repeat(np.asarray(attn_logit_bias)[0], 8).reshape(128, 1)),
        w_o=f(attn_w_o)[0], w_in=f(ssm_w_in)[0], w_glu=f(ssm_w_glu)[0],
        lamT_re=_qp(f(ssm_lambda_re)[0]), lamT_im=_qp(f(ssm_lambda_im)[0]),
        logdtT=_qp(f(np.repeat(np.asarray(ssm_log_dt)[0][:, None], 64, axis=1))),
        BT_re=_qp(f(ssm_b_re)[0]), BT_im=_qp(f(ssm_b_im)[0]),
        CT_re=_qp(f(np.transpose(np.asarray(ssm_c_re)[0], (0, 2, 1)))), CT_im=_qp(f(np.transpose(np.asarray(ssm_c_im)[0], (0, 2, 1)))),
        d_lay=f(np.tile(np.asarray(ssm_d)[0].reshape(64, 16).T, (8, 1))),
        w_up=f(ffn_w_up), w_down=f(ffn_w_down),
        conv_wl=f(np.asarray(ffn_conv_w).reshape(2, 3, NFC, 128).transpose(0, 3, 2, 1)),
        conv_bl=f(np.asarray(ffn_conv_b).reshape(2, NFC, 128).transpose(0, 2, 1)),
    )
    in_maps = []
    sfc = np.asarray(state_ffn_conv, dtype=np.float32)
    for c in (cores if cores is not None else range(NCORES)):
        m = dict(shared)
        bs = slice(4 * c, 4 * c + 4)
        m["xp"] = f(x_prompt[c])
        m["xs"] = f(np.asarray(x_sample)[bs].reshape(NS, D))
        m["ptab"] = np.ascontiguousarray(np.asarray(page_table)[bs].reshape(1, 256).astype(np.int32))
        m["sprev_re"] = _qp(f(np.transpose(np.asarray(state_ssm_re)[0, bs], (1, 2, 0))))
        m["sprev_im"] = _qp(f(np.transpose(np.asarray(state_ssm_im)[0, bs], (1, 2, 0))))
        m["conv_prevT"] = f(sfc[:, bs].reshape(2, 4, 2, NFC, 128).transpose(0, 4, 3, 1, 2))
        in_maps.append(m)
    return in_maps


def _assemble(R):
    n = len(R)
    y_prompt = np.stack([R[c]["y_p"] for c in range(n)]).astype(np.float32)
    y_sample = np.concatenate([R[c]["y_s"].reshape(4, 8, D) for c in range(n)]).astype(np.float32)
    def kv_p(a):
        return np.ascontiguousarray(a[:, :, 0:16, :].transpose(2, 1, 0, 3)).reshape(SEQ, 16, 64)

    def kv_s(a):
        return np.ascontiguousarray(a[:, 0:NS, 16, :].transpose(1, 0, 2)).reshape(4, 8, 16, 64)

    k_prompt = np.stack([kv_p(R[c]["kv_dev"][..., 0:128]) for c in range(n)])[None].astype(np.float32)
    v_prompt = np.stack([kv_p(R[c]["kv_dev"][..., 128:256]) for c in range(n)])[None].astype(np.float32)
    k_sample = np.concatenate([kv_s(R[c]["kv_dev"][..., 0:128]) for c in range(n)])[None].astype(np.float32)
    v_sample = np.concatenate([kv_s(R[c]["kv_dev"][..., 128:256]) for c in range(n)])[None].astype(np.float32)
    sre_p = np.stack([_qp_inv(R[c]["sre_p"]) for c in range(n)])[None].astype(np.float32)
    sim_p = np.stack([_qp_inv(R[c]["sim_p"]) for c in range(n)])[None].astype(np.float32)
    sre_s = np.concatenate([np.transpose(_qp_inv(R[c]["sre_s"]), (2, 0, 1)) for c in range(n)])[None].astype(np.float32)
    sim_s = np.concatenate([np.transpose(_qp_inv(R[c]["sim_s"]), (2, 0, 1)) for c in range(n)])[None].astype(np.float32)
    conv_prompt = np.stack([R[c]["conv_p"].transpose(0, 3, 2, 1).reshape(2, 2, DFF) for c in range(n)], axis=1).astype(np.float32)
    conv_sample = np.concatenate([R[c]["conv_s"].transpose(0, 3, 4, 2, 1).reshape(2, 4, 2, DFF) for c in range(n)], axis=1).astype(np.float32)
    return (y_prompt, y_sample, k_prompt, v_prompt, k_sample, v_sample, sre_p, sim_p, sre_s, sim_s, conv_prompt, conv_sample)


def kernel(**inputs):
    in_maps = _host_prep(**inputs)
    nc = build_program()
    res = run_bass_kernel_spmd(nc, in_maps, core_ids=list(range(NCORES)))
    return _assemble(res.results)
```

```python
import math
from contextlib import ExitStack

import numpy as np
import concourse.bass as bass
import concourse.mybir as mybir
from concourse.bass_utils import run_bass_kernel_spmd

F32 = mybir.dt.float32
BF16 = mybir.dt.bfloat16
I32 = mybir.dt.int32
AF = mybir.ActivationFunctionType
ALU = mybir.AluOpType
AX = mybir.AxisListType

NCORES = 8
D = 1024
SEQ = 2048
NS = 32
NT = SEQ + NS
DFF = 2816
NFC = 22
EPS = 1e-6
PI = math.pi
PI_SAFE = 3.1415925
FFN_GROUPS = [list(range(0, 8)), list(range(8, 15)), list(range(15, 22))]
DEBUG_SKIP = set()


class Buf:
    __slots__ = ("w", "r")

    def __init__(self):
        self.w = None
        self.r = {}


class Sched:
    def __init__(self, nc, es):
        self.nc = nc
        self.es = es
        self.h = {"pe": nc.tensor, "act": nc.scalar, "dve": nc.vector, "pool": nc.gpsimd, "sp": nc.sync}
        self.semobj = {}
        self.cnt = {}
        for k in ("pe", "act", "dve", "pool"):
            self.semobj[k] = es.enter_context(nc.semaphore("s_" + k))
            self.cnt[k] = 0
        self.waited = {k: {} for k in self.h}
        self.chans = []
        self.pend = {k: ([], []) for k in self.cnt}

    def _deps(self, reads, writes):
        deps = {}

        def add(k, v):
            if deps.get(k, 0) < v:
                deps[k] = v

        for b in reads:
            if b.w is not None:
                add(*b.w)
        for b in writes:
            if b.w is not None:
                add(*b.w)
            for k, v in b.r.items():
                add(k, v)
        return deps

    def _wait(self, eng, deps):
        for k, v in deps.items():
            if eng == "pe" and k == "pe":
                continue
            if self.waited[eng].get(k, 0) >= v:
                continue
            self.h[eng].wait_ge(self.semobj[k], v)
            self.waited[eng][k] = v

    def _mark(self, ticket, reads, writes):
        k, v = ticket
        for b in reads:
            if b.r.get(k, 0) < v:
                b.r[k] = v
        for b in writes:
            b.w = ticket
            b.r = {}

    def op(self, eng, fn, reads=(), writes=(), inc=True):
        if eng == "pool":
            eng = "dve"
        self._wait(eng, self._deps(reads, writes))
        ins = fn(self.h[eng])
        if not inc:
            self.pend[eng][0].extend(reads)
            self.pend[eng][1].extend(writes)
            return
        self.cnt[eng] += 1
        ins.then_inc(self.semobj[eng], 1)
        pr, pw = self.pend[eng]
        self._mark((eng, self.cnt[eng]), list(reads) + pr, list(writes) + pw)
        self.pend[eng] = ([], [])

    def chan(self, name):
        c = {"key": "c_" + name, "n": 0}
        self.semobj[c["key"]] = self.es.enter_context(self.nc.semaphore("c_" + name))
        self.chans.append(c)
        return c

    def dma(self, q, ch, out, in_, reads=(), writes=(), **kw):
        self._wait(q, self._deps(reads, writes))
        ins = self.h[q].dma_start(out=out, in_=in_, **kw)
        ch["n"] += 16
        ins.then_inc(self.semobj[ch["key"]], 16)
        self._mark((ch["key"], ch["n"]), reads, writes)

    def idma(self, ch, out, in_, idx_ap, reads=(), writes=()):
        q = "pool"
        self._wait(q, self._deps(reads, writes))
        ins = self.h[q].indirect_dma_start(out=out, out_offset=None, in_=in_,
                                           in_offset=bass.IndirectOffsetOnAxis(ap=idx_ap, axis=0))
        ch["n"] += 16
        ins.then_inc(self.semobj[ch["key"]], 16)
        self._mark((ch["key"], ch["n"]), reads, writes)

    def barrier(self, engines=None):
        allt = {k: v for k, v in self.cnt.items() if v > 0}
        for c in self.chans:
            if c["n"]:
                allt[c["key"]] = c["n"]
        for eng in (engines or self.h.keys()):
            self._wait(eng, allt)


def build_program(pool_rows=2560 * 128):
    nc = bass.Bass("TRN2", target_bir_lowering=False)

    def din(name, shape, dt=F32):
        return nc.dram_tensor(name, list(shape), dt, kind="ExternalInput").ap()

    def dout(name, shape, dt=F32):
        return nc.dram_tensor(name, list(shape), dt, kind="ExternalOutput").ap()

    xp = din("xp", [SEQ, D]); xs = din("xs", [NS, D])
    ck = din("ck", [pool_rows, D]); cv = din("cv", [pool_rows, D])
    ptab = din("ptab", [1, 256], I32)
    norm_mix = din("norm_mix", [2, D]); norm_ffn = din("norm_ffn", [2, D])
    w_qkv = din("w_qkv", [D, 3 * D]); q_gain = din("q_gain", [1, 64]); k_gain = din("k_gain", [1, 64])
    lbias = din("lbias", [1, 16]); lbias_hq = din("lbias_hq", [128, 1])
    w_o = din("w_o", [D, D]); w_in = din("w_in", [D, D]); w_glu = din("w_glu", [D, 2 * D])
    lamT_re = din("lamT_re", [128, 32]); lamT_im = din("lamT_im", [128, 32]); logdtT = din("logdtT", [128, 32])
    BT_re = din("BT_re", [128, 32, 16]); BT_im = din("BT_im", [128, 32, 16])
    CT_re = din("CT_re", [128, 32, 16]); CT_im = din("CT_im", [128, 32, 16])
    d_lay = din("d_lay", [128, 64])
    sprev_re = din("sprev_re", [128, 32, 4]); sprev_im = din("sprev_im", [128, 32, 4])
    w_up = din("w_up", [2, D, 2 * DFF]); w_down = din("w_down", [2, DFF, D])
    conv_wl = din("conv_wl", [2, 128, NFC, 3]); conv_bl = din("conv_bl", [2, 128, NFC])
    conv_prevT = din("conv_prevT", [2, 128, NFC, 4, 2])
    c_ident = din("c_ident", [128, 128]); c_trim8 = din("c_trim8", [128, 128]); c_onesm8 = din("c_onesm8", [128, 128])
    c_maskd = din("c_maskd", [128, 128]); c_masknew = din("c_masknew", [128, 4, 32]); c_maskM = din("c_maskM", [128, 128])
    c_sel = din("c_sel", [128, 64, 128]); c_iotap = din("c_iotap", [128, 1]); c_iotan = din("c_iotan", [128, 256])

    y_p = dout("y_p", [SEQ, D]); y_s = dout("y_s", [NS, D])
    kv_dev = dout("kv_dev", [8, 128, 17, 256])
    sre_p = dout("sre_p", [128, 32]); sim_p = dout("sim_p", [128, 32])
    sre_s = dout("sre_s", [128, 32, 4]); sim_s = dout("sim_s", [128, 32, 4])
    conv_p = dout("conv_p", [2, 128, NFC, 2]); conv_s = dout("conv_s", [2, 128, NFC, 4, 2])

    with ExitStack() as es:
        S = Sched(nc, es)

        _nm = [0]

        def sb(scope, name, shape, dt):
            _nm[0] += 1
            return scope.enter_context(nc.sbuf_tensor("%s_%d" % (name, _nm[0]), list(shape), dt))

        psF = [es.enter_context(nc.psum_tensor("psF%d" % i, [128, 512], F32)) for i in range(6)]
        psFb = [Buf() for _ in range(6)]
        psB32 = [es.enter_context(nc.psum_tensor("psB%d" % i, [128, 512], F32)) for i in range(2)]
        psB = [t[:].bitcast(BF16) for t in psB32]
        psBb = [Buf() for _ in range(2)]

        x_sb = sb(es, "x_sb", [128, 17, D], F32)
        xb = [Buf() for _ in range(17)]
        ident = sb(es, "ident", [128, 128], BF16)
        trim8 = sb(es, "trim8", [128, 128], BF16)
        onesm8 = sb(es, "onesm8", [128, 128], BF16)
        maskd = sb(es, "maskd", [128, 128], F32)
        bias_rep = sb(es, "bias_rep", [128, 1, 16], F32)
        bias_hq = sb(es, "bias_hq", [128, 1], F32)
        qg_rep = sb(es, "qg_rep", [128, 1, 64], F32)
        kg_rep = sb(es, "kg_rep", [128, 1, 64], F32)
        ssn = sb(es, "ssn", [128, 17], F32)
        rstd = sb(es, "rstd", [128, 17], F32)
        constb = Buf(); g_repb = Buf(); ssb = [Buf() for _ in range(17)]; xnb = [Buf(), Buf()]

        ch_x = S.chan("x"); ch_c = S.chan("const"); ch_g = S.chan("gain")

        for hf in range(2):
            S.dma("sp", ch_x, x_sb[:, hf * 8:(hf + 1) * 8, :],
                  xp[hf * 1024:(hf + 1) * 1024, :].rearrange("(p j) d -> p j d", j=8),
                  writes=xb[hf * 8:(hf + 1) * 8])
        S.dma("sp", ch_x, x_sb[0:NS, 16, :], xs[:, :], writes=[xb[16]])
        S.dma("pool", ch_c, ident[:], c_ident[:, :], writes=[constb])
        S.dma("pool", ch_c, trim8[:], c_trim8[:, :], writes=[constb])
        S.dma("pool", ch_c, onesm8[:], c_onesm8[:, :], writes=[constb])
        S.dma("sp", ch_c, maskd[:], c_maskd[:, :], writes=[constb])
        S.dma("sp", ch_c, bias_rep[:], lbias[0:1, :].partition_broadcast(128), writes=[constb])
        S.dma("sp", ch_c, bias_hq[:], lbias_hq[:, :], writes=[constb])
        S.dma("sp", ch_c, qg_rep[:], q_gain[0:1, :].partition_broadcast(128), writes=[constb])
        S.dma("sp", ch_c, kg_rep[:], k_gain[0:1, :].partition_broadcast(128), writes=[constb])

        def tcols(s):
            if s < 16:
                hf, j = divmod(s, 8)
                return slice(hf * 1024 + j, (hf + 1) * 1024, 8)
            return slice(SEQ, NT)

        def tp(s):
            return 128 if s < 16 else NS

        evac_rr = [0]

        def evac(out, in_, reads, writes, eng=None):
            if eng is None:
                eng = ("act", "dve")[evac_rr[0] % 2]
                evac_rr[0] += 1
            if eng == "act":
                S.op("act", lambda h: h.activation(out=out, in_=in_, func=AF.Copy), reads, writes)
            else:
                S.op(eng, lambda h: h.tensor_copy(out=out, in_=in_), reads, writes)

        def rms_to_hT(gain_row, hT, hTb):
          with ExitStack() as scn:
            g_rep = sb(scn, "g_rep", [128, 1, D], F32)
            junk = sb(scn, "junk", [128, D], BF16)
            xn = [sb(scn, "xn%d" % i, [128, D], BF16) for i in range(2)]
            S.dma("sp", ch_g, g_rep[:], gain_row.partition_broadcast(128), writes=[g_repb])
            for s in ([] if "norm" in DEBUG_SKIP else range(17)):
                P = tp(s)
                S.op("act", lambda h: h.activation(out=junk[:P, :], in_=x_sb[:P, s, :], func=AF.Square,
                                                   accum_out=ssn[:P, s:s + 1]), [xb[s]], [ssb[s]])
                S.op("act", lambda h: h.activation(out=rstd[:P, s:s + 1], in_=ssn[:P, s:s + 1], func=AF.Ln, bias=EPS, scale=1.0 / D),
                     [ssb[s]], [ssb[s]])
                S.op("act", lambda h: h.activation(out=rstd[:P, s:s + 1], in_=rstd[:P, s:s + 1], func=AF.Exp, scale=-0.5),
                     [ssb[s]], [ssb[s]])
                xt = xn[s % 2]
                S.op("dve", lambda h: h.scalar_tensor_tensor(out=xt[:P, :], in0=x_sb[:P, s, :], scalar=rstd[:P, s:s + 1],
                                                             in1=g_rep[:P, 0, :], op0=ALU.mult, op1=ALU.mult),
                     [xb[s], ssb[s], g_repb], [xnb[s % 2]])
                pb = psB[s % 2]
                for c in range(8):
                    S.op("pe", lambda h: h.transpose(pb[:, c * 128:c * 128 + P], xt[:P, c * 128:(c + 1) * 128], ident[:P, :P]),
                         [xnb[s % 2], constb], [psBb[s % 2]], inc=(c == 7))
                evac(hT[:, :, tcols(s)], pb.rearrange("p (c t) -> p c t", c=8)[:, :, 0:P], [psBb[s % 2]], [hTb])
            S.barrier()

        with ExitStack() as sc_attn:
            OT = sb(sc_attn, "OT", [128, 8, NT], BF16)
            OTb = Buf()
            Qbd = sb(sc_attn, "Qbd", [128, 4, 8, 128], BF16)
            KTs = sb(sc_attn, "KTs", [128, 8, NS], BF16)
            Vs16 = sb(sc_attn, "Vs16", [NS, D], BF16)
            smpb = Buf()
            S.op("pool", lambda h: h.memset(Qbd[:], 0.0), [], [smpb])

            with ExitStack() as sc1:
                kv32 = sb(sc1, "kv32", [128, 9, 256], F32)
                hT = sb(sc1, "hT", [128, 8, NT], BF16)
                hTb = Buf()
                rms_to_hT(norm_mix[0:1, :], hT, hTb)

                wq_sb = [sb(sc1, "wq%d" % i, [128, 8, 384], BF16) for i in range(2)]
                wqb = [Buf(), Buf()]
                ch_wq = [S.chan("wq0"), S.chan("wq1")]
                QKT = sb(sc1, "QKT", [128, 2, NT], BF16)
                V16 = sb(sc1, "V16", [128, 17, 128], BF16)
                QKTb = Buf(); V16b = Buf()
                k32b = Buf(); v32b = k32b
                ch_ko = S.chan("kvout")
                sq = [sb(sc1, "sq%d" % i, [128, 256], F32) for i in range(2)]
                sqb = [Buf(), Buf()]
                ss4 = [sb(sc1, "ss4%d" % i, [128, 4], F32) for i in range(2)]
                ss4b = [Buf(), Buf()]
                q16 = [sb(sc1, "q16%d" % i, [128, 128], BF16) for i in range(2)]
                k16 = [sb(sc1, "k16%d" % i, [128, 128], BF16) for i in range(2)]
                q16b = [Buf(), Buf()]; k16b = [Buf(), Buf()]
                e32 = [sb(sc1, "e32%d" % i, [128, 512], F32) for i in range(2)]
                spb16 = [sb(sc1, "spb%d" % i, [128, 512], BF16) for i in range(2)]
                wb16 = [sb(sc1, "wb%d" % i, [128, 512], BF16) for i in range(2)]
                e32b = [Buf(), Buf()]; spbb = [Buf(), Buf()]; wbb = [Buf(), Buf()]
                sps32 = [sb(sc1, "sps32%d" % i, [128, 512], F32) for i in range(2)]
                sps16 = [sb(sc1, "sps16%d" % i, [128, 512], BF16) for i in range(2)]
                sps32b = [Buf(), Buf()]; sps16b = [Buf(), Buf()]

                w_qkv_v = w_qkv.rearrange("(c p) n -> p c n", p=128)

                def load_wq(hp):
                    i = hp % 2
                    for part in range(3):
                        S.dma("pool", ch_wq[i], wq_sb[i][:, :, part * 128:(part + 1) * 128],
                              w_qkv_v[:, :, part * D + hp * 128: part * D + (hp + 1) * 128], writes=[wqb[i]])

                load_wq(0)
                blk = [0]
                grp = [0]
                for hp in ([] if "qkv" in DEBUG_SKIP else range(1 if "qkv1" in DEBUG_SKIP else 8)):
                    if hp + 1 < 8:
                        load_wq(hp + 1)
                    wq = wq_sb[hp % 2]
                    for t in range(17):
                        P = 128 if t < 16 else NS
                        cols = slice(t * 128, t * 128 + P)
                        ps = psF[t % 2]; psb_ = psFb[t % 2]
                        for c in range(8):
                            S.op("pe", lambda h: h.matmul(ps[:P, 0:384], lhsT=hT[:, c, cols], rhs=wq[:, c, :],
                                                          start=(c == 0), stop=(c == 7)),
                                 [hTb, wqb[hp % 2]], [psb_], inc=(c == 7))
                        i2 = t % 2
                        slot = t if t < 9 else t - 9
                        S.op("act", lambda h: h.activation(out=sq[i2][:P, :], in_=ps[:P, 0:256], func=AF.Square),
                             [psb_], [sqb[i2]])
                        S.op("dve", lambda h: h.tensor_reduce(out=ss4[i2][:P, :], in_=sq[i2][:P, :].rearrange("p (g d) -> p g d", d=64),
                                                              axis=AX.X, op=ALU.add), [sqb[i2]], [ss4b[i2]])
                        S.op("act", lambda h: h.activation(out=ss4[i2][:P, :], in_=ss4[i2][:P, :], func=AF.Ln, bias=EPS, scale=1.0 / 64),
                             [ss4b[i2]], [ss4b[i2]])
                        S.op("act", lambda h: h.activation(out=ss4[i2][:P, :], in_=ss4[i2][:P, :], func=AF.Exp, scale=-0.5),
                             [ss4b[i2]], [ss4b[i2]])
                        for g in range(2):
                            S.op("dve", lambda h: h.scalar_tensor_tensor(out=q16[i2][:P, g * 64:(g + 1) * 64], in0=ps[:P, g * 64:(g + 1) * 64],
                                                                         scalar=ss4[i2][:P, g:g + 1], in1=qg_rep[:P, 0, :],
                                                                         op0=ALU.mult, op1=ALU.mult),
                                 [psb_, ss4b[i2], constb], [q16b[i2]])
                        for g in range(2):
                            S.op("dve", lambda h: h.scalar_tensor_tensor(out=kv32[:P, slot, g * 64:(g + 1) * 64],
                                                                         in0=ps[:P, 128 + g * 64:128 + (g + 1) * 64],
                                                                         scalar=ss4[i2][:P, 2 + g:3 + g], in1=kg_rep[:P, 0, :],
                                                                         op0=ALU.mult, op1=ALU.mult),
                                 [psb_, ss4b[i2], constb], [k32b])
                        S.op("act", lambda h: h.activation(out=kv32[:P, slot, 128:256], in_=ps[:P, 256:384], func=AF.Copy),
                             [psb_], [v32b])
                        ceng = "dve" if "nopool" in DEBUG_SKIP else "pool"
                        S.op(ceng, lambda h: h.tensor_copy(out=k16[i2][:P, :], in_=kv32[:P, slot, 0:128]), [k32b], [k16b[i2]])
                        S.op(ceng, lambda h: h.tensor_copy(out=V16[:P, t, :], in_=kv32[:P, slot, 128:256]), [v32b], [V16b])
                        pb = psB[t % 2]
                        S.op("pe", lambda h: h.transpose(pb[:, 0:P], q16[i2][:P, :], ident[:P, :P]), [q16b[i2], constb], [psBb[t % 2]], inc=False)
                        S.op("pe", lambda h: h.transpose(pb[:, 128:128 + P], k16[i2][:P, :], ident[:P, :P]), [k16b[i2], constb], [psBb[t % 2]])
                        evac(QKT[:, :, cols], pb[:, 0:256].rearrange("p (a t) -> p a t", a=2)[:, :, 0:P], [psBb[t % 2]], [QKTb])
                        if t == 8 and "kvout" not in DEBUG_SKIP:
                            S.dma("sp", ch_ko, kv_dev[hp, :, 0:9, :], kv32[:, 0:9, :], reads=[k32b])
                        if t == 16 and "kvout" not in DEBUG_SKIP:
                            S.dma("sp", ch_ko, kv_dev[hp, :, 9:17, :], kv32[:, 0:8, :], reads=[k32b])
                    ceng = "dve" if "nopool" in DEBUG_SKIP else "pool"
                    for hh in range(2):
                        r = slice(hh * 64, (hh + 1) * 64)
                        hq = (2 * hp + hh) * 8
                        S.op(ceng, lambda h: h.tensor_copy(out=Qbd[r, :, hp, hq:hq + 8],
                                                             in_=QKT[r, 0, SEQ:NT].rearrange("p (b t) -> p b t", t=8)),
                             [QKTb], [smpb])
                    S.op(ceng, lambda h: h.tensor_copy(out=KTs[:, hp, :], in_=QKT[:, 1, SEQ:NT]), [QKTb], [smpb])
                    S.op(ceng, lambda h: h.tensor_copy(out=Vs16[:, hp * 128:(hp + 1) * 128], in_=V16[0:NS, 16, :]), [V16b], [smpb])

                    for hh in range(2):
                        hd = 2 * hp + hh
                        r = slice(hh * 64, (hh + 1) * 64)
                        for sbk in ([] if "attn" in DEBUG_SKIP else range(4)):
                            Q0 = 512 * sbk
                            gi = grp[0] % 2; grp[0] += 1
                            po = psF[2 + gi]; pob = psFb[2 + gi]
                            S.op("pool", lambda h: h.memset(sps32[gi][:], 0.0), [], [sps32b[gi]])
                            kbs = list(range(4 * sbk + 3, -1, -1))
                            info = []
                            for ki, kb in enumerate(kbs):
                                diag = kb >= 4 * sbk
                                qlo = (kb - 4 * sbk) * 128 if diag else 0
                                info.append((kb, diag, qlo, 512 - qlo, blk[0] % 2))
                                blk[0] += 1

                            def stage_a(ki):
                                kb, diag, qlo, N, bi = info[ki]
                                pz = psF[4 + bi]; pzb = psFb[4 + bi]
                                S.op("pe", lambda h: h.matmul(pz[:, 0:N], lhsT=QKT[r, 1, kb * 128:(kb + 1) * 128],
                                                              rhs=QKT[r, 0, Q0 + qlo:Q0 + 512], start=True, stop=False,
                                                              skip_group_check=True), [QKTb], [pzb])
                                S.op("act", lambda h: h.activation(out=e32[bi][:, 0:N], in_=pz[:, 0:N], func=AF.Exp,
                                                                   bias=bias_rep[:, 0, hd:hd + 1], scale=0.125),
                                     [pzb, constb], [e32b[bi]])
                                if diag:
                                    S.op("dve", lambda h: h.tensor_tensor(out=e32[bi][:, 0:128], in0=e32[bi][:, 0:128], in1=maskd[:, :],
                                                                          op=ALU.mult), [e32b[bi], constb], [e32b[bi]])
                                S.op("act", lambda h: h.activation(out=spb16[bi][:, 0:N], in_=e32[bi][:, 0:N], func=AF.Ln,
                                                                   bias=1.0, scale=1.0), [e32b[bi]], [spbb[bi]])

                            def stage_b(ki):
                                kb, diag, qlo, N, bi = info[ki]
                                first = ki == 0
                                last = ki == len(kbs) - 1
                                pz = psF[4 + bi]; pzb = psFb[4 + bi]
                                S.op("pe", lambda h: h.matmul(pz[:, 0:N], lhsT=trim8[:, :], rhs=spb16[bi][:, 0:N], start=False,
                                                              stop=first, skip_group_check=True), [spbb[bi], constb, pzb], [pzb],
                                     inc=first)
                                if not first:
                                    S.op("pe", lambda h: h.matmul(pz[:, 0:N], lhsT=onesm8[:, :], rhs=sps16[gi][:, qlo:512], start=False,
                                                                  stop=True, skip_group_check=True), [sps16b[gi], constb, pzb], [pzb])
                                S.op("act", lambda h: h.activation(out=wb16[bi][:, 0:N], in_=pz[:, 0:N], func=AF.Exp,
                                                                   bias=bias_rep[:, 0, hd:hd + 1], scale=0.125),
                                     [pzb, constb], [wbb[bi]])
                                if diag:
                                    S.op("dve", lambda h: h.tensor_tensor(out=wb16[bi][:, 0:128], in0=wb16[bi][:, 0:128], in1=maskd[:, :],
                                                                          op=ALU.mult), [wbb[bi], constb], [wbb[bi]])
                                S.op("pe", lambda h: h.matmul(po[:, qlo:512], lhsT=V16[:, kb, :], rhs=wb16[bi][:, 0:N], start=first,
                                                              stop=last, skip_group_check=True), [V16b, wbb[bi], pob], [pob])
                                if not last:
                                    kn = info[ki + 1]
                                    qn = kn[2]
                                    S.op("dve", lambda h: h.tensor_tensor(out=sps32[gi][:, qlo:512], in0=sps32[gi][:, qlo:512],
                                                                          in1=spb16[bi][:, 0:N], op=ALU.add),
                                         [sps32b[gi], spbb[bi]], [sps32b[gi]])
                                    S.op("pool", lambda h: h.tensor_copy(out=sps16[gi][:, qn:512], in_=sps32[gi][:, qn:512]),
                                         [sps32b[gi]], [sps16b[gi]])

                            stage_a(0)
                            for ki in range(len(kbs)):
                                if ki + 1 < len(kbs):
                                    stage_a(ki + 1)
                                stage_b(ki)
                            evac(OT[r, hp, Q0:Q0 + 512], po[r, 0:512], [pob], [OTb])
                S.barrier()

            with ExitStack() as sc2:
                NRING = 8
                Kr = [sb(sc2, "Kr%d" % i, [128, D], BF16) for i in range(NRING)]
                Vr = [sb(sc2, "Vr%d" % i, [128, D], BF16) for i in range(NRING)]
                Krb = [Buf() for _ in range(NRING)]; Vrb = [Buf() for _ in range(NRING)]
                ch_K = [S.chan("K%d" % i) for i in range(NRING)]
                ch_V = [S.chan("V%d" % i) for i in range(NRING)]
                KTb_sb = [sb(sc2, "KTb%d" % i, [128, 8, 512], BF16) for i in range(2)]
                KTbb = [Buf(), Buf()]
                es32 = [sb(sc2, "es32%d" % i, [128, 512], F32) for i in range(2)]
                sp32 = [sb(sc2, "sp32%d" % i, [128, 512], F32) for i in range(2)]
                cs32 = [sb(sc2, "cs32%d" % i, [128, 513], F32) for i in range(2)]
                ec32 = [sb(sc2, "ec32%d" % i, [128, 512], F32) for i in range(2)]
                w16 = [sb(sc2, "w16%d" % i, [128, 512], BF16) for i in range(2)]
                wT = [sb(sc2, "wT%d" % i, [128, 4, 128], BF16) for i in range(2)]
                esb = [Buf(), Buf()]; spb_ = [Buf(), Buf()]; csb = [Buf(), Buf()]; ecb = [Buf(), Buf()]
                w16b = [Buf(), Buf()]; wTb = [Buf(), Buf()]
                ones32 = sb(sc2, "ones32", [128, 512], F32)
                masknew = sb(sc2, "masknew", [128, 4, 32], F32)
                negR = sb(sc2, "negR", [128, 1], F32)
                negRb = Buf()
                Of16 = sb(sc2, "Of16", [128, D], BF16)
                Ofb = Buf()
                pt_i = sb(sc2, "pt_i", [128, 1, 256], I32)
                pt_f = sb(sc2, "pt_f", [128, 256], F32)
                idx_i = sb(sc2, "idx_i", [128, 256], I32)
                iotap = sb(sc2, "iotap", [128, 1], F32)
                idxb = Buf(); c2b = Buf()
                ch_c2 = S.chan("c2")
                S.dma("sp", ch_c2, pt_i[:], ptab[0:1, :].partition_broadcast(128), writes=[idxb])
                S.dma("sp", ch_c2, iotap[:], c_iotap[:, :], writes=[idxb])
                S.dma("sp", ch_c2, masknew[:], c_masknew[:, :, :], writes=[c2b])
                S.op("pool", lambda h: h.memset(ones32[:], 1.0), [], [c2b])
                S.op("pool", lambda h: h.memset(cs32[0][:, 0:1], 0.0), [], [csb[0]])
                S.op("pool", lambda h: h.memset(cs32[1][:, 0:1], 0.0), [], [csb[1]])
                S.op("dve", lambda h: h.tensor_copy(out=pt_f[:, :], in_=pt_i[:, 0, :]), [idxb], [idxb])
                S.op("dve", lambda h: h.tensor_scalar(out=pt_f[:, :], in0=pt_f[:, :], scalar1=128.0, scalar2=iotap[:, 0:1],
                                                      op0=ALU.mult, op1=ALU.add), [idxb], [idxb])
                S.op("dve", lambda h: h.tensor_copy(out=idx_i[:, :], in_=pt_f[:, :]), [idxb], [idxb])

                ring = [0]; sblk = [0]; tb = [0]

                def stick_s(ps, psb_, Wd, bl, mask_ap):
                    i = sblk[0] % 2; sblk[0] += 1
                    S.op("act", lambda h: h.activation(out=es32[i][:, 0:Wd], in_=ps[:, 0:Wd], func=AF.Exp, bias=bias_hq[:, 0:1], scale=0.125),
                         [psb_, constb], [esb[i]])
                    if mask_ap is not None:
                        S.op("dve", lambda h: h.tensor_tensor(out=es32[i][:, 0:Wd], in0=es32[i][:, 0:Wd], in1=mask_ap, op=ALU.mult),
                             [esb[i], c2b], [esb[i]])
                    S.op("act", lambda h: h.activation(out=sp32[i][:, 0:Wd], in_=es32[i][:, 0:Wd], func=AF.Ln, bias=1.0, scale=1.0),
                         [esb[i]], [spb_[i]])
                    S.op("dve", lambda h: h.tensor_tensor_scan(out=cs32[i][:, 1:Wd + 1], data0=ones32[:, 0:Wd], data1=sp32[i][:, 0:Wd],
                                                               initial=0.0, op0=ALU.mult, op1=ALU.add), [spb_[i], c2b], [csb[i]])
                    S.op("dve", lambda h: h.tensor_tensor(out=negR[:, :], in0=negR[:, :], in1=cs32[i][:, Wd:Wd + 1], op=ALU.subtract),
                         [csb[i], negRb], [negRb])
                    S.op("act", lambda h: h.activation(out=ec32[i][:, 0:Wd], in_=cs32[i][:, 0:Wd], func=AF.Exp, bias=negR[:, 0:1], scale=1.0),
                         [csb[i], negRb], [ecb[i]])
                    S.op("dve", lambda h: h.tensor_tensor(out=w16[i][:, 0:Wd], in0=es32[i][:, 0:Wd], in1=ec32[i][:, 0:Wd], op=ALU.mult),
                         [esb[i], ecb[i]], [w16b[i]])
                    return i

                for bl in ([] if "sample_attn" in DEBUG_SKIP else range(4)):
                    pO = [psF[2], psF[3]]; pOb = [psFb[2], psFb[3]]
                    S.op("pool", lambda h: h.memset(negR[:], 0.0), [], [negRb])
                    ps = psF[0]; psb_ = psFb[0]
                    for c in range(8):
                        S.op("pe", lambda h: h.matmul(ps[:, 0:NS], lhsT=Qbd[:, bl, c, :], rhs=KTs[:, c, :], start=(c == 0), stop=(c == 7)),
                             [smpb], [psb_], inc=(c == 7))
                    i = stick_s(ps, psb_, NS, bl, masknew[:, bl, :])
                    ti = tb[0] % 2; tb[0] += 1
                    S.op("pe", lambda h: h.transpose(psB[ti][0:NS, 0:128], w16[i][:, 0:NS], ident[:, :]), [w16b[i], constb], [psBb[ti]])
                    evac(wT[ti][0:NS, 0, :], psB[ti][0:NS, 0:128], [psBb[ti]], [wTb[ti]])
                    for hv in range(2):
                        S.op("pe", lambda h: h.matmul(pO[hv][:, :], lhsT=wT[ti][0:NS, 0, :], rhs=Vs16[:, hv * 512:(hv + 1) * 512],
                                                      start=True, stop=False, skip_group_check=True), [wTb[ti], smpb], [pOb[hv]])
                    for jb in range(15, -1, -1):
                        kt = jb % 2
                        slots = []
                        for pgi in range(4):
                            pg = 4 * jb + pgi
                            sl = ring[0] % NRING; ring[0] += 1
                            slots.append(sl)
                            col = bl * 64 + pg
                            S.idma(ch_K[sl], Kr[sl][:, :], ck[:, :], idx_i[:, col:col + 1], reads=[idxb], writes=[Krb[sl]])
                            S.idma(ch_V[sl], Vr[sl][:, :], cv[:, :], idx_i[:, col:col + 1], reads=[idxb], writes=[Vrb[sl]])
                        for pgi in range(4):
                            sl = slots[pgi]
                            ti = tb[0] % 2; tb[0] += 1
                            for c in range(8):
                                S.op("pe", lambda h: h.transpose(psB[ti][:, c * 128:(c + 1) * 128], Kr[sl][:, c * 128:(c + 1) * 128], ident[:, :]),
                                     [Krb[sl], constb], [psBb[ti]], inc=(c == 7))
                            evac(KTb_sb[kt][:, :, pgi * 128:(pgi + 1) * 128], psB[ti].rearrange("p (c t) -> p c t", c=8),
                                 [psBb[ti]], [KTbb[kt]])
                        pi = jb % 2
                        ps = psF[pi]; psb_ = psFb[pi]
                        for c in range(8):
                            S.op("pe", lambda h: h.matmul(ps[:, 0:512], lhsT=Qbd[:, bl, c, :], rhs=KTb_sb[kt][:, c, :], start=(c == 0), stop=(c == 7)),
                                 [smpb, KTbb[kt]], [psb_], inc=(c == 7))
                        i = stick_s(ps, psb_, 512, bl, None)
                        ti = tb[0] % 2; tb[0] += 1
                        for pgi in range(4):
                            S.op("pe", lambda h: h.transpose(psB[ti][:, pgi * 128:(pgi + 1) * 128], w16[i][:, pgi * 128:(pgi + 1) * 128], ident[:, :]),
                                 [w16b[i], constb], [psBb[ti]], inc=(pgi == 3))
                        evac(wT[ti][:, :, :], psB[ti][:, 0:512].rearrange("p (a t) -> p a t", a=4), [psBb[ti]], [wTb[ti]])
                        for pgi in range(4):
                            sl = slots[pgi]
                            for hv in range(2):
                                lastmm = (jb == 0 and pgi == 3)
                                S.op("pe", lambda h: h.matmul(pO[hv][:, :], lhsT=wT[ti][:, pgi, :], rhs=Vr[sl][:, hv * 512:(hv + 1) * 512],
                                                              start=False, stop=lastmm, skip_group_check=True),
                                     [wTb[ti], Vrb[sl], pOb[hv]], [pOb[hv]])
                    for hv in range(2):
                        evac(Of16[:, hv * 512:(hv + 1) * 512], pO[hv][:, :], [pOb[hv]], [Ofb])
                    ti = tb[0] % 2; tb[0] += 1
                    for c in range(8):
                        S.op("pe", lambda h: h.transpose(psB[ti][:, c * 128:(c + 1) * 128], Of16[:, c * 128:(c + 1) * 128], ident[:, :]),
                             [Ofb, constb], [psBb[ti]], inc=(c == 7))
                    for c in range(8):
                        for hh in range(2):
                            r = slice(hh * 64, (hh + 1) * 64)
                            cc = c * 128 + (2 * c + hh) * 8
                            evac(OT[r, c, SEQ + bl * 8:SEQ + bl * 8 + 8], psB[ti][r, cc:cc + 8], [psBb[ti]], [OTb])
                S.barrier()

            with ExitStack() as sc3:
                wo_sb = sb(sc3, "wo_sb", [128, 8, D], BF16)
                wob = Buf(); ch_wo = S.chan("wo")
                S.dma("pool", ch_wo, wo_sb[:], w_o.rearrange("(c p) n -> p c n", p=128), writes=[wob])
                for s in ([] if "wo" in DEBUG_SKIP else range(17)):
                    P = tp(s)
                    for hv in range(2):
                        pi = (2 * s + hv) % 4
                        ps = psF[pi]; psb_ = psFb[pi]
                        for c in range(8):
                            S.op("pe", lambda h: h.matmul(ps[:P, :], lhsT=OT[:, c, tcols(s)], rhs=wo_sb[:, c, hv * 512:(hv + 1) * 512],
                                                          start=(c == 0), stop=(c == 7)), [OTb, wob], [psb_], inc=(c == 7))
                        S.op("dve", lambda h: h.tensor_tensor(out=x_sb[:P, s, hv * 512:(hv + 1) * 512], in0=x_sb[:P, s, hv * 512:(hv + 1) * 512],
                                                              in1=ps[:P, :], op=ALU.add), [psb_, xb[s]], [xb[s]])
                S.barrier()

        def conv_ffn(li):
            if "ffn" in DEBUG_SKIP:
                return
            with ExitStack() as sc:
                hT = sb(sc, "hTf", [128, 8, NT], BF16)
                hTb = Buf()
                rms_to_hT(norm_ffn[li:li + 1, :], hT, hTb)
                gT = sb(sc, "gT", [128, 8, NT], BF16)
                gTb = Buf()
                wu = [sb(sc, "wu%d" % i, [128, 8, 256], BF16) for i in range(2)]
                wub = [Buf(), Buf()]; ch_wu = [S.chan("wu%d_%d" % (li, i)) for i in range(2)]
                wd = sb(sc, "wd", [128, 8, D], BF16)
                wdb = Buf(); ch_wd = S.chan("wd%d" % li)
                cw = sb(sc, "cw", [128, NFC, 3], F32)
                cb = sb(sc, "cb", [128, NFC], F32)
                prevT = sb(sc, "prevT", [128, NFC, 4, 2], F32)
                cpb = Buf(); ch_cp = S.chan("cp%d" % li)
                a32 = [sb(sc, "a32%d" % i, [128, 2 + SEQ], F32) for i in range(2)]
                a32b = [Buf(), Buf()]
                as32 = [sb(sc, "as32%d" % i, [128, 4, 10], F32) for i in range(2)]
                as32b = [Buf(), Buf()]
                c32 = [sb(sc, "c32%d" % i, [128, 512], F32) for i in range(2)]
                s32 = [sb(sc, "s32%d" % i, [128, 512], F32) for i in range(2)]
                c32b = [Buf(), Buf()]; s32b = [Buf(), Buf()]
                cst = sb(sc, "cst", [128, NFC, 2], F32)
                csts = sb(sc, "csts", [128, NFC, 4, 2], F32)
                cstb = Buf(); ch_co = S.chan("co%d" % li)
                S.dma("sp", ch_cp, cw[:], conv_wl[li], writes=[cpb])
                S.dma("sp", ch_cp, cb[:], conv_bl[li], writes=[cpb])
                S.dma("sp", ch_cp, prevT[:], conv_prevT[li], writes=[cpb])
                for i in range(2):
                    S.op("pool", lambda h: h.memset(a32[i][:, 0:2], 0.0), [], [a32b[i]])
                w_up_v = w_up[li].rearrange("(c p) n -> p c n", p=128)

                def load_wu(fc, i):
                    S.dma("pool", ch_wu[i], wu[i][:, :, 0:128], w_up_v[:, :, fc * 128:(fc + 1) * 128], writes=[wub[i]])
                    S.dma("pool", ch_wu[i], wu[i][:, :, 128:256], w_up_v[:, :, DFF + fc * 128:DFF + (fc + 1) * 128], writes=[wub[i]])

                cnt = [0]
                load_wu(0, 0)
                fcn = 0
                for grp_fcs in FFN_GROUPS:
                    for k, fc in enumerate(grp_fcs):
                        wi = fcn % 2
                        if fc + 1 < NFC:
                            load_wu(fc + 1, (fcn + 1) % 2)
                        ai = fcn % 2
                        fcn += 1
                        for tbk in range(5):
                            T0 = tbk * 512
                            Nt = 512 if tbk < 4 else NS
                            pa = psF[(2 * cnt[0]) % 4]; pab = psFb[(2 * cnt[0]) % 4]
                            pb_ = psF[(2 * cnt[0] + 1) % 4]; pbb = psFb[(2 * cnt[0] + 1) % 4]
                            ci = cnt[0] % 2
                            cnt[0] += 1
                            for c in range(8):
                                S.op("pe", lambda h: h.matmul(pa[:, 0:Nt], lhsT=wu[wi][:, c, 0:128], rhs=hT[:, c, T0:T0 + Nt],
                                                              start=(c == 0), stop=(c == 7)), [wub[wi], hTb], [pab], inc=(c == 7))
                            for c in range(8):
                                S.op("pe", lambda h: h.matmul(pb_[:, 0:Nt], lhsT=wu[wi][:, c, 128:256], rhs=hT[:, c, T0:T0 + Nt],
                                                              start=(c == 0), stop=(c == 7)), [wub[wi], hTb], [pbb], inc=(c == 7))
                            if tbk < 4:
                                A = a32[ai]
                                S.op("act", lambda h: h.activation(out=A[:, 2 + T0:2 + T0 + Nt], in_=pa[:, 0:Nt], func=AF.Copy),
                                     [pab], [a32b[ai]])
                                srcs = [A[:, T0 + j:T0 + j + Nt] for j in range(3)]
                                cdst = c32[ci][:, 0:Nt]; sdst = s32[ci][:, 0:Nt]
                                gdst = gT[:, k, T0:T0 + Nt]
                                rd = [a32b[ai], cpb]
                            else:
                                A = as32[ai]
                                S.op("pool", lambda h: h.tensor_copy(out=A[:, :, 0:2], in_=prevT[:, fc, :, :]), [cpb], [as32b[ai]])
                                S.op("act", lambda h: h.activation(out=A[:, :, 2:10], in_=pa[:, 0:Nt].rearrange("p (b t) -> p b t", t=8),
                                                                   func=AF.Copy), [pab], [as32b[ai]])
                                srcs = [A[:, :, j:j + 8] for j in range(3)]
                                cdst = c32[ci][:, 0:Nt].rearrange("p (b t) -> p b t", t=8)
                                sdst = s32[ci][:, 0:Nt]
                                gdst = gT[:, k, T0:T0 + Nt]
                                rd = [as32b[ai], cpb]
                            S.op("dve", lambda h: h.tensor_scalar(out=cdst, in0=srcs[0], scalar1=cw[:, fc, 0:1], scalar2=cb[:, fc:fc + 1],
                                                                  op0=ALU.mult, op1=ALU.add), rd, [c32b[ci]])
                            for j in (1, 2):
                                S.op("dve", lambda h: h.scalar_tensor_tensor(out=cdst, in0=srcs[j], scalar=cw[:, fc, j:j + 1], in1=cdst,
                                                                             op0=ALU.mult, op1=ALU.add), rd + [c32b[ci]], [c32b[ci]])
                            S.op("act", lambda h: h.activation(out=sdst, in_=c32[ci][:, 0:Nt], func=AF.Silu), [c32b[ci]], [s32b[ci]])
                            S.op("dve", lambda h: h.tensor_tensor(out=gdst, in0=s32[ci][:, 0:Nt], in1=pb_[:, 0:Nt], op=ALU.mult),
                                 [s32b[ci], pbb], [gTb])
                        S.op("pool", lambda h: h.tensor_copy(out=cst[:, fc, :], in_=a32[ai][:, SEQ:SEQ + 2]), [a32b[ai]], [cstb])
                        S.op("pool", lambda h: h.tensor_copy(out=csts[:, fc, :, :], in_=as32[ai][:, :, 8:10]), [as32b[ai]], [cstb])
                        S.dma("pool", ch_wd, wd[:, k, :], w_down[li, fc * 128:(fc + 1) * 128, :], writes=[wdb])
                    ng = len(grp_fcs)
                    for s in range(17):
                        P = tp(s)
                        for hv in range(2):
                            pi = 4 + (2 * s + hv) % 2
                            ps = psF[pi]; psb_ = psFb[pi]
                            for k in range(ng):
                                S.op("pe", lambda h: h.matmul(ps[:P, :], lhsT=gT[:, k, tcols(s)], rhs=wd[:, k, hv * 512:(hv + 1) * 512],
                                                              start=(k == 0), stop=(k == ng - 1)), [gTb, wdb], [psb_], inc=(k == ng - 1))
                            S.op("dve", lambda h: h.tensor_tensor(out=x_sb[:P, s, hv * 512:(hv + 1) * 512],
                                                                  in0=x_sb[:P, s, hv * 512:(hv + 1) * 512], in1=ps[:P, :], op=ALU.add),
                                 [psb_, xb[s]], [xb[s]])
                S.dma("sp", ch_co, conv_p[li], cst[:], reads=[cstb])
                S.dma("sp", ch_co, conv_s[li], csts[:], reads=[cstb])
                S.barrier()

        conv_ffn(0)

        NCH = 260
        with ExitStack() as sc_s5:
          if "s5" not in DEBUG_SKIP:
              U = sb(sc_s5, "U", [128, 64, NCH], BF16)
              Ub = Buf()
              with ExitStack() as scu:
                  u_tok = sb(scu, "u_tok", [128, 2, 64, 128], BF16)
                  u_toks = sb(scu, "u_toks", [4, 64, 128], BF16)
                  utb = Buf()
                  with ExitStack() as sch:
                      hT = sb(sch, "hTs", [128, 8, NT], BF16)
                      hTb = Buf()
                      rms_to_hT(norm_mix[1:2, :], hT, hTb)
                      win = sb(sch, "win", [128, 8, 512], BF16)
                      winb = Buf(); ch_win = S.chan("win")
                      w_in_v = w_in.rearrange("(c p) n -> p c n", p=128)
                      n = 0
                      for hv in range(2):
                          S.dma("pool", ch_win, win[:], w_in_v[:, :, hv * 512:(hv + 1) * 512], writes=[winb])
                          for s in range(16):
                              ps = psF[n % 4]; psb_ = psFb[n % 4]; n += 1
                              for c in range(8):
                                  S.op("pe", lambda h: h.matmul(ps[:, :], lhsT=hT[:, c, tcols(s)], rhs=win[:, c, :],
                                                                start=(c == 0), stop=(c == 7)), [hTb, winb], [psb_], inc=(c == 7))
                              evac(u_tok[:, s // 8, hv * 32:(hv + 1) * 32, (s % 8) * 16:(s % 8 + 1) * 16], ps[:, :].rearrange("p (g c) -> p g c", c=16), [psb_], [utb])
                          for j in range(8):
                              ps = psF[n % 4]; psb_ = psFb[n % 4]; n += 1
                              for c in range(8):
                                  S.op("pe", lambda h: h.matmul(ps[0:4, :], lhsT=hT[:, c, SEQ + j:NT:8], rhs=win[:, c, :],
                                                                start=(c == 0), stop=(c == 7)), [hTb, winb], [psb_], inc=(c == 7))
                              evac(u_toks[0:4, hv * 32:(hv + 1) * 32, j * 16:(j + 1) * 16], ps[0:4, :].rearrange("p (g c) -> p g c", c=16), [psb_], [utb])
                      S.barrier()
                  for g0 in range(0, 64, 3):
                      gs = list(range(g0, min(g0 + 3, 64)))
                      ti = (g0 // 3) % 2
                      for gi, g in enumerate(gs):
                          base = gi * NCH
                          for hf in range(2):
                              S.op("pe", lambda h: h.transpose(psB[ti][:, base + hf * 128:base + (hf + 1) * 128],
                                                               u_tok[:, hf, g, :], ident[:, :]),
                                   [utb, constb], [psBb[ti]], inc=False)
                          S.op("pe", lambda h: h.transpose(psB[ti][:, base + 256:base + 260], u_toks[0:4, g, :], ident[0:4, 0:4]),
                               [utb, constb], [psBb[ti]], inc=(gi == len(gs) - 1))
                      evac(U[:, g0:g0 + len(gs), :], psB[ti][:, 0:len(gs) * NCH].rearrange("p (g n) -> p g n", n=NCH), [psBb[ti]], [Ub])
                  S.barrier()

              with ExitStack() as scw:
                  Mw = sb(scw, "Mw", [128, 64, 128], BF16)
                  WBre = sb(scw, "WBre", [128, 32, 128], BF16)
                  WBim = sb(scw, "WBim", [128, 32, 128], BF16)
                  WCre = sb(scw, "WCre", [128, 32, 128], BF16)
                  nWCim = sb(scw, "nWCim", [128, 32, 128], BF16)
                  a8re = sb(scw, "a8re", [128, 32], F32); a8im = sb(scw, "a8im", [128, 32], F32); na8im = sb(scw, "na8im", [128, 32], F32)
                  r8 = sb(scw, "r8", [128, 32], F32); phir = sb(scw, "phir", [128, 32], F32)
                  dlay = sb(scw, "dlay", [128, 64], F32)
                  spr = sb(scw, "spr", [128, 32, 4], F32); spi = sb(scw, "spi", [128, 32, 4], F32)
                  spr16 = sb(scw, "spr16", [128, 32, 4], BF16); spi16 = sb(scw, "spi16", [128, 32, 4], BF16)
                  fin_re = sb(scw, "fin_re", [128, 32], F32); fin_im = sb(scw, "fin_im", [128, 32], F32)
                  fins_re = sb(scw, "fins_re", [128, 32, 4], F32); fins_im = sb(scw, "fins_im", [128, 32, 4], F32)
                  iotan = sb(scw, "iotan", [128, 256], F32)
                  maskM = sb(scw, "maskM", [128, 128], F32)
                  Wb = Buf(); prmb = Buf(); finb = Buf()
                  ch_p = S.chan("s5p"); ch_fo = S.chan("s5o")
                  with ExitStack() as scp:
                      def t32(name, shape):
                          return sb(scp, name, shape, F32)
                      lr = t32("lr", [128, 32]); lim = t32("lim", [128, 32]); ldt = t32("ldt", [128, 32])
                      Bre = t32("Bre", [128, 32, 16]); Bim = t32("Bim", [128, 32, 16])
                      Cre = t32("Cre", [128, 32, 16]); Cim = t32("Cim", [128, 32, 16])
                      for dst, src in ((lr, lamT_re), (lim, lamT_im), (ldt, logdtT), (Bre, BT_re), (Bim, BT_im), (Cre, CT_re), (Cim, CT_im),
                                       (dlay, d_lay), (spr, sprev_re), (spi, sprev_im), (iotan, c_iotan), (maskM, c_maskM)):
                          S.dma("sp", ch_p, dst[:], src, writes=[prmb])
                      dt_ = t32("dt_", [128, 32]); lrdt = t32("lrdt", [128, 32]); lidt = t32("lidt", [128, 32]); mag = t32("mag", [128, 32])
                      ang = t32("ang", [128, 32]); sinv = t32("sinv", [128, 32]); cosv = t32("cosv", [128, 32])
                      are = t32("are", [128, 32]); aim = t32("aim", [128, 32]); den = t32("den", [128, 32]); tmp = t32("tmp", [128, 32]); tmp2 = t32("tmp2", [128, 32])
                      zr = t32("zr", [128, 32]); zi = t32("zi", [128, 32]); am1 = t32("am1", [128, 32])
                      Apr = t32("Apr", [128, 32, 9]); Api = t32("Api", [128, 32, 9])
                      Bbr = t32("Bbr", [128, 32, 16]); Bbi = t32("Bbi", [128, 32, 16])
                      T1 = t32("T1", [128, 32, 16]); T2 = t32("T2", [128, 32, 16])
                      Aipr = t32("Aipr", [128, 32, 8]); Aipi = t32("Aipi", [128, 32, 8])
                      ivr = t32("ivr", [128, 32]); ivi = t32("ivi", [128, 32]); m2 = t32("m2", [128, 32])

                      def P_(eng, fn):
                          S.op(eng, fn, [prmb], [prmb])

                      def tt(out, a, b, op, eng="dve"):
                          P_(eng, lambda h: h.tensor_tensor(out=out, in0=a, in1=b, op=op))

                      def ts(out, a, s1, s2, op0, op1=None, eng="dve"):
                          if op1 is None:
                              P_(eng, lambda h: h.tensor_scalar(out=out, in0=a, scalar1=s1, scalar2=None, op0=op0))
                          else:
                              P_(eng, lambda h: h.tensor_scalar(out=out, in0=a, scalar1=s1, scalar2=s2, op0=op0, op1=op1))

                      def act(out, a, func, bias=0.0, scale=1.0):
                          P_("act", lambda h: h.activation(out=out, in_=a, func=func, bias=bias, scale=scale))

                      def cmul(o_re, o_im, ar, ai, br, bi, t1, t2, neg_im=False):
                          tt(t1, ar, br, ALU.mult); tt(t2, ai, bi, ALU.mult)
                          tt(o_re, t1, t2, ALU.subtract)
                          tt(t1, ar, bi, ALU.mult); tt(t2, ai, br, ALU.mult)
                          if neg_im:
                              tt(t1, t1, t2, ALU.add)
                              ts(o_im, t1, -1.0, None, ALU.mult)
                          else:
                              tt(o_im, t1, t2, ALU.add)

                      act(dt_[:], ldt[:], AF.Exp)
                      tt(lrdt[:], lr[:], dt_[:], ALU.mult); tt(lidt[:], lim[:], dt_[:], ALU.mult)
                      act(mag[:], lrdt[:], AF.Exp)
                      ki = sb(scp, "ki", [128, 32], I32)

                      def reduce_pi(dst, x):
                          ts(ki[:], x, 1.0 / (2 * PI), None, ALU.mult)
                          P_("dve", lambda h: h.scalar_tensor_tensor(out=dst, in0=ki[:], scalar=-2 * PI, in1=x, op0=ALU.mult, op1=ALU.add))
                          ts(dst, dst, -PI_SAFE, PI_SAFE, ALU.max, ALU.min)

                      reduce_pi(ang[:], lidt[:])
                      act(sinv[:], ang[:], AF.Sin)
                      act(tmp[:], ang[:], AF.Abs)
                      act(cosv[:], tmp[:], AF.Sin, bias=PI / 2, scale=-1.0)
                      tt(are[:], mag[:], cosv[:], ALU.mult); tt(aim[:], mag[:], sinv[:], ALU.mult)
                      tt(den[:], lr[:], lr[:], ALU.mult); tt(tmp[:], lim[:], lim[:], ALU.mult); tt(den[:], den[:], tmp[:], ALU.add)
                      P_("dve", lambda h: h.reciprocal(out=den[:], in_=den[:]))
                      ts(am1[:], are[:], -1.0, None, ALU.add)
                      tt(tmp[:], am1[:], lr[:], ALU.mult); tt(tmp2[:], aim[:], lim[:], ALU.mult); tt(tmp[:], tmp[:], tmp2[:], ALU.add)
                      tt(zr[:], tmp[:], den[:], ALU.mult)
                      tt(tmp[:], aim[:], lr[:], ALU.mult); tt(tmp2[:], am1[:], lim[:], ALU.mult); tt(tmp[:], tmp[:], tmp2[:], ALU.subtract)
                      tt(zi[:], tmp[:], den[:], ALU.mult)
                      zrb = zr[:, :].unsqueeze(2).broadcast_to([128, 32, 16]); zib = zi[:, :].unsqueeze(2).broadcast_to([128, 32, 16])
                      cmul(Bbr[:], Bbi[:], zrb, zib, Bre[:], Bim[:], T1[:], T2[:])
                      P_("dve", lambda h: h.memset(Apr[:, :, 0], 1.0)); P_("dve", lambda h: h.memset(Api[:, :, 0], 0.0))
                      for k in range(1, 9):
                          cmul(Apr[:, :, k], Api[:, :, k], Apr[:, :, k - 1], Api[:, :, k - 1], are[:], aim[:], tmp[:], tmp2[:])
                      tt(m2[:], mag[:], mag[:], ALU.mult)
                      P_("dve", lambda h: h.reciprocal(out=m2[:], in_=m2[:]))
                      tt(ivr[:], are[:], m2[:], ALU.mult); tt(ivi[:], aim[:], m2[:], ALU.mult); ts(ivi[:], ivi[:], -1.0, None, ALU.mult)
                      P_("dve", lambda h: h.memset(Aipr[:, :, 0], 1.0)); P_("dve", lambda h: h.memset(Aipi[:, :, 0], 0.0))
                      for k in range(1, 8):
                          cmul(Aipr[:, :, k], Aipi[:, :, k], Aipr[:, :, k - 1], Aipi[:, :, k - 1], ivr[:], ivi[:], tmp[:], tmp2[:])
                      P_("dve", lambda h: h.tensor_copy(out=a8re[:], in_=Apr[:, :, 8])); P_("dve", lambda h: h.tensor_copy(out=a8im[:], in_=Api[:, :, 8]))
                      ts(na8im[:], a8im[:], -1.0, None, ALU.mult)
                      ts(tmp[:], lrdt[:], 8.0, None, ALU.mult)
                      act(r8[:], tmp[:], AF.Exp)
                      ts(tmp2[:], lidt[:], 8.0, None, ALU.mult)
                      reduce_pi(phir[:], tmp2[:])

                      def b16(t, gsl, k):
                          ng = gsl.stop - gsl.start
                          return t[:, gsl, k:k + 1].broadcast_to([128, ng, 16])

                      allg = slice(0, 32)
                      for i in range(8):
                          cmul(WCre[:, :, i * 16:(i + 1) * 16], nWCim[:, :, i * 16:(i + 1) * 16], b16(Apr, allg, i + 1), b16(Api, allg, i + 1),
                               Cre[:], Cim[:], T1[:], T2[:], neg_im=True)
                      P_("dve", lambda h: h.tensor_copy(out=spr16[:], in_=spr[:])); P_("dve", lambda h: h.tensor_copy(out=spi16[:], in_=spi[:]))
                      for half in range(2):
                          gsl = slice(16 * half, 16 * half + 16)
                          with ExitStack() as sch2:
                              Lre = sb(sch2, "Lre", [128, 16, 128], BF16); nLim = sb(sch2, "nLim", [128, 16, 128], BF16)
                              Rre = sb(sch2, "Rre", [128, 16, 128], BF16); Rim = sb(sch2, "Rim", [128, 16, 128], BF16)
                              WBr_qp = sb(sch2, "WBr_qp", [128, 16, 128], BF16); WBi_qp = sb(sch2, "WBi_qp", [128, 16, 128], BF16)
                              t1 = T1[:, 0:16, :]; t2 = T2[:, 0:16, :]
                              for j in range(8):
                                  js = slice(j * 16, (j + 1) * 16)
                                  cmul(Lre[:, :, js], nLim[:, :, js], b16(Aipr, gsl, j), b16(Aipi, gsl, j), Bbr[:, gsl, :], Bbi[:, gsl, :], t1, t2, neg_im=True)
                                  cmul(Rre[:, :, js], Rim[:, :, js], b16(Apr, gsl, j), b16(Api, gsl, j), Cre[:, gsl, :], Cim[:, gsl, :], t1, t2)
                                  cmul(WBr_qp[:, :, js], WBi_qp[:, :, js], b16(Apr, gsl, 7 - j), b16(Api, gsl, 7 - j), Bbr[:, gsl, :], Bbi[:, gsl, :], t1, t2)
                              for gl_ in range(16):
                                  gp = 16 * half + gl_
                                  ti = gp % 2
                                  S.op("pe", lambda h: h.transpose(psB[ti][:, 0:128], WBr_qp[:, gl_, :], ident[:, :]), [prmb, constb], [psBb[ti]], inc=False)
                                  S.op("pe", lambda h: h.transpose(psB[ti][:, 128:256], WBi_qp[:, gl_, :], ident[:, :]), [prmb, constb], [psBb[ti]])
                                  evac(WBre[:, gp, :], psB[ti][:, 0:128], [psBb[ti]], [Wb])
                                  evac(WBim[:, gp, :], psB[ti][:, 128:256], [psBb[ti]], [Wb])
                              for gg in range(32):
                                  gl_, q = divmod(gg, 2)
                                  g = 32 * half + gg
                                  r = slice(q * 64, (q + 1) * 64)
                                  pi = g % 4
                                  S.op("pe", lambda h: h.matmul(psF[pi][:, 0:128], lhsT=Lre[r, gl_, :], rhs=Rre[r, gl_, :], start=True, stop=False),
                                       [prmb], [psFb[pi]], inc=False)
                                  S.op("pe", lambda h: h.matmul(psF[pi][:, 0:128], lhsT=nLim[r, gl_, :], rhs=Rim[r, gl_, :], start=False, stop=True),
                                       [prmb], [psFb[pi]])
                                  S.op("dve", lambda h: h.tensor_tensor(out=Mw[:, g, :], in0=psF[pi][:, 0:128], in1=maskM[:, :], op=ALU.mult),
                                       [psFb[pi], prmb], [Wb])
                              S.barrier()
                      S.barrier()

                  with ExitStack() as scc:
                      def t32(name, shape):
                          return sb(scc, name, shape, F32)
                      NB = 2
                      angt = [t32("angt%d" % i, [128, 256]) for i in range(NB)]
                      angc = [t32("angc%d" % i, [128, 256]) for i in range(NB)]
                      kI = [sb(scc, "kI%d" % i, [128, 256], I32) for i in range(NB)]
                      sn = [t32("sn%d" % i, [128, 256]) for i in range(NB)]
                      cn = [t32("cn%d" % i, [128, 256]) for i in range(NB)]
                      Fre = [t32("Fre%d" % i, [128, NCH]) for i in range(NB)]
                      Fim = [t32("Fim%d" % i, [128, NCH]) for i in range(NB)]
                      Gre = [t32("Gre%d" % i, [128, 256]) for i in range(NB)]
                      Gim = [t32("Gim%d" % i, [128, 256]) for i in range(NB)]
                      Tre = [t32("Tre%d" % i, [128, 256]) for i in range(NB)]
                      Tim = [t32("Tim%d" % i, [128, 256]) for i in range(NB)]
                      q1 = [t32("q1%d" % i, [128, 256]) for i in range(NB)]
                      q2 = [t32("q2%d" % i, [128, 256]) for i in range(NB)]
                      Sre = [t32("Sre%d" % i, [128, 256]) for i in range(NB)]
                      Sim = [t32("Sim%d" % i, [128, 256]) for i in range(NB)]
                      Sxr = [sb(scc, "Sxr%d" % i, [128, 257], BF16) for i in range(NB)]
                      Sxi = [sb(scc, "Sxi%d" % i, [128, 257], BF16) for i in range(NB)]
                      y32 = [t32("y32%d" % i, [128, NCH]) for i in range(2)]
                      z32 = [t32("z32%d" % i, [128, NCH]) for i in range(2)]
                      tb_ = [Buf() for _ in range(NB)]
                      yb = [Buf(), Buf()]
                      for i in range(NB):
                          S.op("pool", lambda h: h.memset(Sxr[i][:, 0:1], 0.0), [], [tb_[i]])
                          S.op("pool", lambda h: h.memset(Sxi[i][:, 0:1], 0.0), [], [tb_[i]])
                      GC = math.sqrt(2.0 / PI)
                      for gp in range(32):
                          i = gp % NB
                          B_ = tb_[i]
                          S.op("pool", lambda h: h.tensor_scalar(out=angc[i][:], in0=iotan[:], scalar1=phir[:, gp:gp + 1], scalar2=None,
                                                                 op0=ALU.mult), [prmb, B_], [B_])
                          S.op("pool", lambda h: h.tensor_scalar(out=kI[i][:], in0=angc[i][:], scalar1=1.0 / (2 * PI), scalar2=None,
                                                                 op0=ALU.mult), [B_], [B_])
                          S.op("dve", lambda h: h.scalar_tensor_tensor(out=angt[i][:], in0=kI[i][:], scalar=-2 * PI, in1=angc[i][:],
                                                                        op0=ALU.mult, op1=ALU.add), [B_], [B_])
                          S.op("pool", lambda h: h.tensor_scalar(out=angt[i][:], in0=angt[i][:], scalar1=-PI_SAFE, scalar2=PI_SAFE,
                                                                 op0=ALU.max, op1=ALU.min), [B_], [B_])
                          S.op("act", lambda h: h.activation(out=sn[i][:], in_=angt[i][:], func=AF.Sin), [B_], [B_])
                          S.op("act", lambda h: h.activation(out=angc[i][:], in_=angt[i][:], func=AF.Abs), [B_], [B_])
                          S.op("act", lambda h: h.activation(out=cn[i][:], in_=angc[i][:], func=AF.Sin, bias=PI / 2, scale=-1.0), [B_], [B_])
                          for q in range(2):
                              g = 2 * gp + q
                              r = slice(q * 64, (q + 1) * 64)
                              S.op("pe", lambda h: h.matmul(psF[0][r, 0:NCH], lhsT=WBre[:, gp, r], rhs=U[:, g, :], start=True, stop=True),
                                   [Wb, Ub], [psFb[0]])
                              S.op("pe", lambda h: h.matmul(psF[1][r, 0:NCH], lhsT=WBim[:, gp, r], rhs=U[:, g, :], start=True, stop=True),
                                   [Wb, Ub], [psFb[1]])
                          S.op("act", lambda h: h.activation(out=Fre[i][:], in_=psF[0][:, 0:NCH], func=AF.Copy), [psFb[0], B_], [B_])
                          S.op("act", lambda h: h.activation(out=Fim[i][:], in_=psF[1][:, 0:NCH], func=AF.Copy), [psFb[1], B_], [B_])

                          def D_(fn, eng="dve"):
                              S.op(eng, fn, [B_, prmb], [B_])
                          D_(lambda h: h.tensor_tensor(out=q1[i][:], in0=cn[i][:], in1=Fre[i][:, 0:256], op=ALU.mult))
                          D_(lambda h: h.tensor_tensor(out=q2[i][:], in0=sn[i][:], in1=Fim[i][:, 0:256], op=ALU.mult), "pool")
                          D_(lambda h: h.tensor_tensor(out=Gre[i][:], in0=q1[i][:], in1=q2[i][:], op=ALU.add))
                          D_(lambda h: h.tensor_tensor(out=q1[i][:], in0=cn[i][:], in1=Fim[i][:, 0:256], op=ALU.mult))
                          D_(lambda h: h.tensor_tensor(out=q2[i][:], in0=sn[i][:], in1=Fre[i][:, 0:256], op=ALU.mult), "pool")
                          D_(lambda h: h.tensor_tensor(out=Gim[i][:], in0=q1[i][:], in1=q2[i][:], op=ALU.subtract))
                          D_(lambda h: h.tensor_tensor_scan(out=Tre[i][:], data0=r8[:, gp:gp + 1].broadcast_to([128, 256]), data1=Gre[i][:], initial=0.0, op0=ALU.mult, op1=ALU.add))
                          D_(lambda h: h.tensor_tensor_scan(out=Tim[i][:], data0=r8[:, gp:gp + 1].broadcast_to([128, 256]), data1=Gim[i][:], initial=0.0, op0=ALU.mult, op1=ALU.add))
                          D_(lambda h: h.tensor_tensor(out=q1[i][:], in0=cn[i][:], in1=Tre[i][:], op=ALU.mult))
                          D_(lambda h: h.tensor_tensor(out=q2[i][:], in0=sn[i][:], in1=Tim[i][:], op=ALU.mult), "pool")
                          D_(lambda h: h.tensor_tensor(out=Sre[i][:], in0=q1[i][:], in1=q2[i][:], op=ALU.subtract))
                          D_(lambda h: h.tensor_tensor(out=q1[i][:], in0=cn[i][:], in1=Tim[i][:], op=ALU.mult))
                          D_(lambda h: h.tensor_tensor(out=q2[i][:], in0=sn[i][:], in1=Tre[i][:], op=ALU.mult), "pool")
                          D_(lambda h: h.tensor_tensor(out=Sim[i][:], in0=q1[i][:], in1=q2[i][:], op=ALU.add))
                          D_(lambda h: h.tensor_copy(out=Sxr[i][:, 1:257], in_=Sre[i][:]), "pool")
                          D_(lambda h: h.tensor_copy(out=Sxi[i][:, 1:257], in_=Sim[i][:]), "pool")
                          S.op("pool", lambda h: h.tensor_copy(out=fin_re[:, gp:gp + 1], in_=Sre[i][:, 255:256]), [B_], [finb])
                          S.op("pool", lambda h: h.tensor_copy(out=fin_im[:, gp:gp + 1], in_=Sim[i][:, 255:256]), [B_], [finb])
                          S.op("dve", lambda h: h.scalar_tensor_tensor(out=q1[i][:, 0:4], in0=spr[:, gp, :], scalar=a8re[:, gp:gp + 1], in1=Fre[i][:, 256:260],
                                                                       op0=ALU.mult, op1=ALU.add), [B_, prmb], [B_])
                          S.op("dve", lambda h: h.scalar_tensor_tensor(out=fins_re[:, gp, :], in0=spi[:, gp, :], scalar=na8im[:, gp:gp + 1], in1=q1[i][:, 0:4],
                                                                       op0=ALU.mult, op1=ALU.add), [B_, prmb], [finb])
                          S.op("dve", lambda h: h.scalar_tensor_tensor(out=q2[i][:, 0:4], in0=spi[:, gp, :], scalar=a8re[:, gp:gp + 1], in1=Fim[i][:, 256:260],
                                                                       op0=ALU.mult, op1=ALU.add), [B_, prmb], [B_])
                          S.op("dve", lambda h: h.scalar_tensor_tensor(out=fins_im[:, gp, :], in0=spr[:, gp, :], scalar=a8im[:, gp:gp + 1], in1=q2[i][:, 0:4],
                                                                       op0=ALU.mult, op1=ALU.add), [B_, prmb], [finb])
                          for q in range(2):
                              g = 2 * gp + q
                              r = slice(q * 64, (q + 1) * 64)
                              pi = 2 + g % 2
                              py = psF[pi]; pyb = psFb[pi]
                              S.op("pe", lambda h: h.matmul(py[:, 0:NCH], lhsT=Mw[:, g, :], rhs=U[:, g, :], start=True, stop=False, skip_group_check=True),
                                   [Wb, Ub], [pyb], inc=False)
                              S.op("pe", lambda h: h.matmul(py[:, 0:256], lhsT=WCre[r, gp, :], rhs=Sxr[i][r, 0:256], start=False, stop=False, skip_group_check=True),
                                   [prmb, B_], [pyb], inc=False)
                              S.op("pe", lambda h: h.matmul(py[:, 0:256], lhsT=nWCim[r, gp, :], rhs=Sxi[i][r, 0:256], start=False, stop=False, skip_group_check=True),
                                   [prmb, B_], [pyb], inc=False)
                              S.op("pe", lambda h: h.matmul(py[:, 256:260], lhsT=WCre[r, gp, :], rhs=spr16[r, gp, :], start=False, stop=False, skip_group_check=True),
                                   [prmb], [pyb], inc=False)
                              S.op("pe", lambda h: h.matmul(py[:, 256:260], lhsT=nWCim[r, gp, :], rhs=spi16[r, gp, :], start=False, stop=True, skip_group_check=True),
                                   [prmb], [pyb])
                              yi = g % 2
                              S.op("dve", lambda h: h.scalar_tensor_tensor(out=y32[yi][:], in0=U[:, g, :], scalar=dlay[:, g:g + 1], in1=py[:, 0:NCH],
                                                                           op0=ALU.mult, op1=ALU.add), [pyb, Ub, prmb], [yb[yi]])
                              S.op("pool", lambda h: h.tensor_tensor(out=z32[yi][:], in0=y32[yi][:], in1=y32[yi][:], op=ALU.mult), [yb[yi]], [yb[yi]])
                              S.op("pool", lambda h: h.tensor_scalar(out=z32[yi][:], in0=z32[yi][:], scalar1=0.044715, scalar2=1.0, op0=ALU.mult, op1=ALU.add),
                                   [yb[yi]], [yb[yi]])
                              S.op("pool", lambda h: h.tensor_tensor(out=z32[yi][:], in0=z32[yi][:], in1=y32[yi][:], op=ALU.mult), [yb[yi]], [yb[yi]])
                              S.op("pool", lambda h: h.tensor_scalar(out=y32[yi][:], in0=y32[yi][:], scalar1=0.5, scalar2=None, op0=ALU.mult),
                                   [yb[yi]], [yb[yi]])
                              S.op("act", lambda h: h.activation(out=z32[yi][:], in_=z32[yi][:], func=AF.Tanh, scale=GC), [yb[yi]], [yb[yi]])
                              S.op("dve", lambda h: h.scalar_tensor_tensor(out=U[:, g, :], in0=z32[yi][:], scalar=1.0, in1=y32[yi][:],
                                                                           op0=ALU.add, op1=ALU.mult), [yb[yi], Ub], [Ub])
                      S.dma("sp", ch_fo, sre_p[:, :], fin_re[:], reads=[finb])
                      S.dma("sp", ch_fo, sim_p[:, :], fin_im[:], reads=[finb])
                      S.dma("sp", ch_fo, sre_s[:, :, :], fins_re[:], reads=[finb])
                      S.dma("sp", ch_fo, sim_s[:, :, :], fins_im[:], reads=[finb])
                      S.barrier()

              with ExitStack() as scg:
                  sel = sb(scg, "sel", [128, 64, 128], BF16)
                  selb = Buf(); ch_sel = S.chan("sel")
                  S.dma("pool", ch_sel, sel[:], c_sel[:, :, :], writes=[selb])
                  gTs = sb(scg, "gTs", [128, 8, NT], BF16)
                  gTsb = Buf()
                  n = 0
                  for fcn in range(8):
                      for i in range(8):
                          ps = psF[n % 4]; psb_ = psFb[n % 4]; n += 1
                          for gl in range(8):
                              S.op("pe", lambda h: h.matmul(ps[:, 0:NCH], lhsT=sel[:, gl * 8 + i, :], rhs=U[:, 8 * fcn + gl, :],
                                                            start=(gl == 0), stop=(gl == 7)), [selb, Ub], [psb_], inc=(gl == 7))
                          evac(gTs[:, fcn, i:NT:8], ps[:, 0:NCH], [psb_], [gTsb])
                  wg = [sb(scg, "wg%d" % i, [128, 8, 1024], BF16) for i in range(2)]
                  wgb = [Buf(), Buf()]; ch_wg = [S.chan("wg0"), S.chan("wg1")]
                  sg32 = [sb(scg, "sg32%d" % i, [128, 512], F32) for i in range(2)]
                  sgb = [Buf(), Buf()]
                  w_glu_v = w_glu.rearrange("(c p) n -> p c n", p=128)
                  for hv in range(2):
                      S.dma("pool", ch_wg[hv], wg[hv][:, :, 0:512], w_glu_v[:, :, hv * 512:(hv + 1) * 512], writes=[wgb[hv]])
                      S.dma("pool", ch_wg[hv], wg[hv][:, :, 512:1024], w_glu_v[:, :, D + hv * 512:D + (hv + 1) * 512], writes=[wgb[hv]])
                  n = 0
                  for hv in range(2):
                      for s in range(17):
                          P = tp(s)
                          pa = psF[(2 * n) % 4]; pab = psFb[(2 * n) % 4]
                          pb_ = psF[(2 * n + 1) % 4]; pbb = psFb[(2 * n + 1) % 4]
                          si = n % 2
                          n += 1
                          for c in range(8):
                              S.op("pe", lambda h: h.matmul(pa[:P, :], lhsT=gTs[:, c, tcols(s)], rhs=wg[hv][:, c, 0:512], start=(c == 0), stop=(c == 7)),
                                   [gTsb, wgb[hv]], [pab], inc=(c == 7))
                          for c in range(8):
                              S.op("pe", lambda h: h.matmul(pb_[:P, :], lhsT=gTs[:, c, tcols(s)], rhs=wg[hv][:, c, 512:1024], start=(c == 0), stop=(c == 7)),
                                   [gTsb, wgb[hv]], [pbb], inc=(c == 7))
                          S.op("act", lambda h: h.activation(out=sg32[si][:P, :], in_=pb_[:P, :], func=AF.Sigmoid), [pbb], [sgb[si]])
                          S.op("dve", lambda h: h.tensor_tensor(out=sg32[si][:P, :], in0=sg32[si][:P, :], in1=pa[:P, :], op=ALU.mult), [sgb[si], pab], [sgb[si]])
                          S.op("dve", lambda h: h.tensor_tensor(out=x_sb[:P, s, hv * 512:(hv + 1) * 512], in0=x_sb[:P, s, hv * 512:(hv + 1) * 512],
                                                                in1=sg32[si][:P, :], op=ALU.add), [sgb[si], xb[s]], [xb[s]])
                  S.barrier()

        conv_ffn(1)

        ch_y = S.chan("yout")
        for hf in range(2):
            S.dma("sp", ch_y, y_p[hf * 1024:(hf + 1) * 1024, :].rearrange("(p j) d -> p j d", j=8), x_sb[:, hf * 8:(hf + 1) * 8, :],
                  reads=xb[hf * 8:(hf + 1) * 8])
        S.dma("sp", ch_y, y_s[:, :], x_sb[0:NS, 16, :], reads=[xb[16]])
        S.barrier()
    return nc


def _consts():
    c = {}
    c["c_ident"] = np.eye(128, dtype=np.float32)
    j = np.arange(128)
    c["c_trim8"] = np.where(j[:, None] >= j[None, :], -8.0, 0.0).astype(np.float32)
    c["c_onesm8"] = np.full((128, 128), -8.0, np.float32)
    c["c_maskd"] = (j[:, None] < j[None, :]).astype(np.float32)
    hq = np.arange(128)
    tq = hq % 8
    mn = np.zeros((128, 4, 32), np.float32)
    for bl in range(4):
        for t in range(8):
            mn[:, bl, bl * 8 + t] = (t < tq).astype(np.float32)
    c["c_masknew"] = mn
    jj = np.arange(128) // 16
    c["c_maskM"] = (jj[None, :] >= jj[:, None]).astype(np.float32)
    sel = np.zeros((128, 64, 128), np.float32)
    for gl in range(8):
        for i in range(8):
            for cc in range(16):
                sel[i * 16 + cc, gl * 8 + i, gl * 16 + cc] = 1.0
    c["c_sel"] = sel
    c["c_iotap"] = np.arange(128, dtype=np.float32).reshape(128, 1)
    c["c_iotan"] = np.tile(np.arange(256, dtype=np.float32)[None, :], (128, 1))
    return c


def _qp(a):
    rest = a.shape[2:]
    return np.ascontiguousarray(a.reshape((32, 2, 64) + rest).transpose((1, 2, 0) + tuple(range(3, 3 + len(rest)))).reshape((128, 32) + rest))


def _qp_inv(a):
    rest = a.shape[2:]
    return a.reshape((2, 64, 32) + rest).transpose((2, 0, 1) + tuple(range(3, 3 + len(rest)))).reshape((64, 64) + rest)


def _host_prep(x_prompt, x_sample, cache_k, cache_v, state_ssm_re, state_ssm_im, state_ffn_conv, page_table,
               norm_mix, norm_ffn, attn_w_qkv, attn_q_gain, attn_k_gain, attn_logit_bias, attn_w_o,
               ssm_w_in, ssm_lambda_re, ssm_lambda_im, ssm_log_dt, ssm_b_re, ssm_b_im, ssm_c_re, ssm_c_im,
               ssm_d, ssm_w_glu, ffn_w_up, ffn_conv_w, ffn_conv_b, ffn_w_down, cores=None):
    f = lambda a: np.ascontiguousarray(np.asarray(a, dtype=np.float32))
    shared = dict(_consts())
    ckf = f(cache_k).reshape(2560 * 128, D)
    cvf = f(cache_v).reshape(2560 * 128, D)
    shared.update(
        ck=ckf, cv=cvf, norm_mix=f(norm_mix), norm_ffn=f(norm_ffn), w_qkv=f(attn_w_qkv)[0], q_gain=f(attn_q_gain), k_gain=f(attn_k_gain),
        lbias=f(attn_logit_bias), lbias_hq=f(np.repeat(np.asarray(attn_logit_bias)[0], 8).reshape(128, 1)),
        w_o=f(attn_w_o)[0], w_in=f(ssm_w_in)[0], w_glu=f(ssm_w_glu)[0],
        lamT_re=_qp(f(ssm_lambda_re)[0]), lamT_im=_qp(f(ssm_lambda_im)[0]),
        logdtT=_qp(f(np.repeat(np.asarray(ssm_log_dt)[0][:, None], 64, axis=1))),
        BT_re=_qp(f(ssm_b_re)[0]), BT_im=_qp(f(ssm_b_im)[0]),
        CT_re=_qp(f(np.transpose(np.asarray(ssm_c_re)[0], (0, 2, 1)))), CT_im=_qp(f(np.transpose(np.asarray(ssm_c_im)[0], (0, 2, 1)))),
        d_lay=f(np.tile(np.asarray(ssm_d)[0].reshape(64, 16).T, (8, 1))),
        w_up=f(ffn_w_up), w_down=f(ffn_w_down),
        conv_wl=f(np.asarray(ffn_conv_w).reshape(2, 3, NFC, 128).transpose(0, 3, 2, 1)),
        conv_bl=f(np.asarray(ffn_conv_b).reshape(2, NFC, 128).transpose(0, 2, 1)),
    )
    in_maps = []
    sfc = np.asarray(state_ffn_conv, dtype=np.float32)
    for c in (cores if cores is not None else range(NCORES)):
        m = dict(shared)
        bs = slice(4 * c, 4 * c + 4)
        m["xp"] = f(x_prompt[c])
        m["xs"] = f(np.asarray(x_sample)[bs].reshape(NS, D))
        m["ptab"] = np.ascontiguousarray(np.asarray(page_table)[bs].reshape(1, 256).astype(np.int32))
        m["sprev_re"] = _qp(f(np.transpose(np.asarray(state_ssm_re)[0, bs], (1, 2, 0))))
        m["sprev_im"] = _qp(f(np.transpose(np.asarray(state_ssm_im)[0, bs], (1, 2, 0))))
        m["conv_prevT"] = f(sfc[:, bs].reshape(2, 4, 2, NFC, 128).transpose(0, 4, 3, 1, 2))
        in_maps.append(m)
    return in_maps


def _assemble(R):
    n = len(R)
    y_prompt = np.stack([R[c]["y_p"] for c in range(n)]).astype(np.float32)
    y_sample = np.concatenate([R[c]["y_s"].reshape(4, 8, D) for c in range(n)]).astype(np.float32)
    def kv_p(a):
        return np.ascontiguousarray(a[:, :, 0:16, :].transpose(2, 1, 0, 3)).reshape(SEQ, 16, 64)

    def kv_s(a):
        return np.ascontiguousarray(a[:, 0:NS, 16, :].transpose(1, 0, 2)).reshape(4, 8, 16, 64)

    k_prompt = np.stack([kv_p(R[c]["kv_dev"][..., 0:128]) for c in range(n)])[None].astype(np.float32)
    v_prompt = np.stack([kv_p(R[c]["kv_dev"][..., 128:256]) for c in range(n)])[None].astype(np.float32)
    k_sample = np.concatenate([kv_s(R[c]["kv_dev"][..., 0:128]) for c in range(n)])[None].astype(np.float32)
    v_sample = np.concatenate([kv_s(R[c]["kv_dev"][..., 128:256]) for c in range(n)])[None].astype(np.float32)
    sre_p = np.stack([_qp_inv(R[c]["sre_p"]) for c in range(n)])[None].astype(np.float32)
    sim_p = np.stack([_qp_inv(R[c]["sim_p"]) for c in range(n)])[None].astype(np.float32)
    sre_s = np.concatenate([np.transpose(_qp_inv(R[c]["sre_s"]), (2, 0, 1)) for c in range(n)])[None].astype(np.float32)
    sim_s = np.concatenate([np.transpose(_qp_inv(R[c]["sim_s"]), (2, 0, 1)) for c in range(n)])[None].astype(np.float32)
    conv_prompt = np.stack([R[c]["conv_p"].transpose(0, 3, 2, 1).reshape(2, 2, DFF) for c in range(n)], axis=1).astype(np.float32)
    conv_sample = np.concatenate([R[c]["conv_s"].transpose(0, 3, 4, 2, 1).reshape(2, 4, 2, DFF) for c in range(n)], axis=1).astype(np.float32)
    return (y_prompt, y_sample, k_prompt, v_prompt, k_sample, v_sample, sre_p, sim_p, sre_s, sim_s, conv_prompt, conv_sample)


def kernel(**inputs):
    in_maps = _host_prep(**inputs)
    nc = build_program()
    res = run_bass_kernel_spmd(nc, in_maps, core_ids=list(range(NCORES)))
    return _assemble(res.results)
```

```python
import math
from contextlib import ExitStack

import numpy as np
import concourse.bass as bass
import concourse.mybir as mybir
from concourse.bass_utils import run_bass_kernel_spmd

F32 = mybir.dt.float32
BF16 = mybir.dt.bfloat16
I32 = mybir.dt.int32
AF = mybir.ActivationFunctionType
ALU = mybir.AluOpType
AX = mybir.AxisListType

NCORES = 8
D = 1024
SEQ = 2048
NS = 32
NT = SEQ + NS
DFF = 2816
NFC = 22
EPS = 1e-6
PI = math.pi
PI_SAFE = 3.1415925
FFN_GROUPS = [list(range(0, 8)), list(range(8, 15)), list(range(15, 22))]
DEBUG_SKIP = set()


class Buf:
    __slots__ = ("w", "r")

    def __init__(self):
        self.w = None
        self.r = {}


class Sched:
    def __init__(self, nc, es):
        self.nc = nc
        self.es = es
        self.h = {"pe": nc.tensor, "act": nc.scalar, "dve": nc.vector, "pool": nc.gpsimd, "sp": nc.sync}
        self.semobj = {}
        self.cnt = {}
        for k in ("pe", "act", "dve", "pool"):
            self.semobj[k] = es.enter_context(nc.semaphore("s_" + k))
            self.cnt[k] = 0
        self.waited = {k: {} for k in self.h}
        self.chans = []
        self.pend = {k: ([], []) for k in self.cnt}

    def _deps(self, reads, writes):
        deps = {}

        def add(k, v):
            if deps.get(k, 0) < v:
                deps[k] = v

        for b in reads:
            if b.w is not None:
                add(*b.w)
        for b in writes:
            if b.w is not None:
                add(*b.w)
            for k, v in b.r.items():
                add(k, v)
        return deps

    def _wait(self, eng, deps):
        for k, v in deps.items():
            if eng == "pe" and k == "pe":
                continue
            if self.waited[eng].get(k, 0) >= v:
                continue
            self.h[eng].wait_ge(self.semobj[k], v)
            self.waited[eng][k] = v

    def _mark(self, ticket, reads, writes):
        k, v = ticket
        for b in reads:
            if b.r.get(k, 0) < v:
                b.r[k] = v
        for b in writes:
            b.w = ticket
            b.r = {}

    def op(self, eng, fn, reads=(), writes=(), inc=True):
        if eng == "pool":
            eng = "dve"
        self._wait(eng, self._deps(reads, writes))
        ins = fn(self.h[eng])
        if not inc:
            self.pend[eng][0].extend(reads)
            self.pend[eng][1].extend(writes)
            return
        self.cnt[eng] += 1
        ins.then_inc(self.semobj[eng], 1)
        pr, pw = self.pend[eng]
        self._mark((eng, self.cnt[eng]), list(reads) + pr, list(writes) + pw)
        self.pend[eng] = ([], [])

    def chan(self, name):
        c = {"key": "c_" + name, "n": 0}
        self.semobj[c["key"]] = self.es.enter_context(self.nc.semaphore("c_" + name))
        self.chans.append(c)
        return c

    def dma(self, q, ch, out, in_, reads=(), writes=(), **kw):
        self._wait(q, self._deps(reads, writes))
        ins = self.h[q].dma_start(out=out, in_=in_, **kw)
        ch["n"] += 16
        ins.then_inc(self.semobj[ch["key"]], 16)
        self._mark((ch["key"], ch["n"]), reads, writes)

    def idma(self, ch, out, in_, idx_ap, reads=(), writes=()):
        q = "pool"
        self._wait(q, self._deps(reads, writes))
        ins = self.h[q].indirect_dma_start(out=out, out_offset=None, in_=in_,
                                           in_offset=bass.IndirectOffsetOnAxis(ap=idx_ap, axis=0))
        ch["n"] += 16
        ins.then_inc(self.semobj[ch["key"]], 16)
        self._mark((ch["key"], ch["n"]), reads, writes)

    def barrier(self, engines=None):
        allt = {k: v for k, v in self.cnt.items() if v > 0}
        for c in self.chans:
            if c["n"]:
                allt[c["key"]] = c["n"]
        for eng in (engines or self.h.keys()):
            self._wait(eng, allt)


def build_program(pool_rows=2560 * 128):
    nc = bass.Bass("TRN2", target_bir_lowering=False)

    def din(name, shape, dt=F32):
        return nc.dram_tensor(name, list(shape), dt, kind="ExternalInput").ap()

    def dout(name, shape, dt=F32):
        return nc.dram_tensor(name, list(shape), dt, kind="ExternalOutput").ap()

    xp = din("xp", [SEQ, D]); xs = din("xs", [NS, D])
    ck = din("ck", [pool_rows, D]); cv = din("cv", [pool_rows, D])
    ptab = din("ptab", [1, 256], I32)
    norm_mix = din("norm_mix", [2, D]); norm_ffn = din("norm_ffn", [2, D])
    w_qkv = din("w_qkv", [D, 3 * D]); q_gain = din("q_gain", [1, 64]); k_gain = din("k_gain", [1, 64])
    lbias = din("lbias", [1, 16]); lbias_hq = din("lbias_hq", [128, 1])
    w_o = din("w_o", [D, D]); w_in = din("w_in", [D, D]); w_glu = din("w_glu", [D, 2 * D])
    lamT_re = din("lamT_re", [128, 32]); lamT_im = din("lamT_im", [128, 32]); logdtT = din("logdtT", [128, 32])
    BT_re = din("BT_re", [128, 32, 16]); BT_im = din("BT_im", [128, 32, 16])
    CT_re = din("CT_re", [128, 32, 16]); CT_im = din("CT_im", [128, 32, 16])
    d_lay = din("d_lay", [128, 64])
    sprev_re = din("sprev_re", [128, 32, 4]); sprev_im = din("sprev_im", [128, 32, 4])
    w_up = din("w_up", [2, D, 2 * DFF]); w_down = din("w_down", [2, DFF, D])
    conv_wl = din("conv_wl", [2, 128, NFC, 3]); conv_bl = din("conv_bl", [2, 128, NFC])
    conv_prevT = din("conv_prevT", [2, 128, NFC, 4, 2])
    c_ident = din("c_ident", [128, 128]); c_trim8 = din("c_trim8", [128, 128]); c_onesm8 = din("c_onesm8", [128, 128])
    c_maskd = din("c_maskd", [128, 128]); c_masknew = din("c_masknew", [128, 4, 32]); c_maskM = din("c_maskM", [128, 128])
    c_sel = din("c_sel", [128, 64, 128]); c_iotap = din("c_iotap", [128, 1]); c_iotan = din("c_iotan", [128, 256])

    y_p = dout("y_p", [SEQ, D]); y_s = dout("y_s", [NS, D])
    kv_dev = dout("kv_dev", [8, 128, 17, 256])
    sre_p = dout("sre_p", [128, 32]); sim_p = dout("sim_p", [128, 32])
    sre_s = dout("sre_s", [128, 32, 4]); sim_s = dout("sim_s", [128, 32, 4])
    conv_p = dout("conv_p", [2, 128, NFC, 2]); conv_s = dout("conv_s", [2, 128, NFC, 4, 2])

    with ExitStack() as es:
        S = Sched(nc, es)

        _nm = [0]

        def sb(scope, name, shape, dt):
            _nm[0] += 1
            return scope.enter_context(nc.sbuf_tensor("%s_%d" % (name, _nm[0]), list(shape), dt))

        psF = [es.enter_context(nc.psum_tensor("psF%d" % i, [128, 512], F32)) for i in range(6)]
        psFb = [Buf() for _ in range(6)]
        psB32 = [es.enter_context(nc.psum_tensor("psB%d" % i, [128, 512], F32)) for i in range(2)]
        psB = [t[:].bitcast(BF16) for t in psB32]
        psBb = [Buf() for _ in range(2)]

        x_sb = sb(es, "x_sb", [128, 17, D], F32)
        xb = [Buf() for _ in range(17)]
        ident = sb(es, "ident", [128, 128], BF16)
        trim8 = sb(es, "trim8", [128, 128], BF16)
        onesm8 = sb(es, "onesm8", [128, 128], BF16)
        maskd = sb(es, "maskd", [128, 128], F32)
        bias_rep = sb(es, "bias_rep", [128, 1, 16], F32)
        bias_hq = sb(es, "bias_hq", [128, 1], F32)
        qg_rep = sb(es, "qg_rep", [128, 1, 64], F32)
        kg_rep = sb(es, "kg_rep", [128, 1, 64], F32)
        ssn = sb(es, "ssn", [128, 17], F32)
        rstd = sb(es, "rstd", [128, 17], F32)
        constb = Buf(); g_repb = Buf(); ssb = [Buf() for _ in range(17)]; xnb = [Buf(), Buf()]

        ch_x = S.chan("x"); ch_c = S.chan("const"); ch_g = S.chan("gain")

        for hf in range(2):
            S.dma("sp", ch_x, x_sb[:, hf * 8:(hf + 1) * 8, :],
                  xp[hf * 1024:(hf + 1) * 1024, :].rearrange("(p j) d -> p j d", j=8),
                  writes=xb[hf * 8:(hf + 1) * 8])
        S.dma("sp", ch_x, x_sb[0:NS, 16, :], xs[:, :], writes=[xb[16]])
        S.dma("pool", ch_c, ident[:], c_ident[:, :], writes=[constb])
        S.dma("pool", ch_c, trim8[:], c_trim8[:, :], writes=[constb])
        S.dma("pool", ch_c, onesm8[:], c_onesm8[:, :], writes=[constb])
        S.dma("sp", ch_c, maskd[:], c_maskd[:, :], writes=[constb])
        S.dma("sp", ch_c, bias_rep[:], lbias[0:1, :].partition_broadcast(128), writes=[constb])
        S.dma("sp", ch_c, bias_hq[:], lbias_hq[:, :], writes=[constb])
        S.dma("sp", ch_c, qg_rep[:], q_gain[0:1, :].partition_broadcast(128), writes=[constb])
        S.dma("sp", ch_c, kg_rep[:], k_gain[0:1, :].partition_broadcast(128), writes=[constb])

        def tcols(s):
            if s < 16:
                hf, j = divmod(s, 8)
                return slice(hf * 1024 + j, (hf + 1) * 1024, 8)
            return slice(SEQ, NT)

        def tp(s):
            return 128 if s < 16 else NS

        evac_rr = [0]

        def evac(out, in_, reads, writes, eng=None):
            if eng is None:
                eng = ("act", "dve")[evac_rr[0] % 2]
                evac_rr[0] += 1
            if eng == "act":
                S.op("act", lambda h: h.activation(out=out, in_=in_, func=AF.Copy), reads, writes)
            else:
                S.op(eng, lambda h: h.tensor_copy(out=out, in_=in_), reads, writes)

        def rms_to_hT(gain_row, hT, hTb):
          with ExitStack() as scn:
            g_rep = sb(scn, "g_rep", [128, 1, D], F32)
            junk = sb(scn, "junk", [128, D], BF16)
            xn = [sb(scn, "xn%d" % i, [128, D], BF16) for i in range(2)]
            S.dma("sp", ch_g, g_rep[:], gain_row.partition_broadcast(128), writes=[g_repb])
            for s in ([] if "norm" in DEBUG_SKIP else range(17)):
                P = tp(s)
                S.op("act", lambda h: h.activation(out=junk[:P, :], in_=x_sb[:P, s, :], func=AF.Square,
                                                   accum_out=ssn[:P, s:s + 1]), [xb[s]], [ssb[s]])
                S.op("act", lambda h: h.activation(out=rstd[:P, s:s + 1], in_=ssn[:P, s:s + 1], func=AF.Ln, bias=EPS, scale=1.0 / D),
                     [ssb[s]], [ssb[s]])
                S.op("act", lambda h: h.activation(out=rstd[:P, s:s + 1], in_=rstd[:P, s:s + 1], func=AF.Exp, scale=-0.5),
                     [ssb[s]], [ssb[s]])
                xt = xn[s % 2]
                S.op("dve", lambda h: h.scalar_tensor_tensor(out=xt[:P, :], in0=x_sb[:P, s, :], scalar=rstd[:P, s:s + 1],
                                                             in1=g_rep[:P, 0, :], op0=ALU.mult, op1=ALU.mult),
                     [xb[s], ssb[s], g_repb], [xnb[s % 2]])
                pb = psB[s % 2]
                for c in range(8):
                    S.op("pe", lambda h: h.transpose(pb[:, c * 128:c * 128 + P], xt[:P, c * 128:(c + 1) * 128], ident[:P, :P]),
                         [xnb[s % 2], constb], [psBb[s % 2]], inc=(c == 7))
                evac(hT[:, :, tcols(s)], pb.rearrange("p (c t) -> p c t", c=8)[:, :, 0:P], [psBb[s % 2]], [hTb])
            S.barrier()

        with ExitStack() as sc_attn:
            OT = sb(sc_attn, "OT", [128, 8, NT], BF16)
            OTb = Buf()
            Qbd = sb(sc_attn, "Qbd", [128, 4, 8, 128], BF16)
            KTs = sb(sc_attn, "KTs", [128, 8, NS], BF16)
            Vs16 = sb(sc_attn, "Vs16", [NS, D], BF16)
            smpb = Buf()
            S.op("pool", lambda h: h.memset(Qbd[:], 0.0), [], [smpb])

            with ExitStack() as sc1:
                kv32 = sb(sc1, "kv32", [128, 9, 256], F32)
                hT = sb(sc1, "hT", [128, 8, NT], BF16)
                hTb = Buf()
                rms_to_hT(norm_mix[0:1, :], hT, hTb)

                wq_sb = [sb(sc1, "wq%d" % i, [128, 8, 384], BF16) for i in range(2)]
                wqb = [Buf(), Buf()]
                ch_wq = [S.chan("wq0"), S.chan("wq1")]
                QKT = sb(sc1, "QKT", [128, 2, NT], BF16)
                V16 = sb(sc1, "V16", [128, 17, 128], BF16)
                QKTb = Buf(); V16b = Buf()
                k32b = Buf(); v32b = k32b
                ch_ko = S.chan("kvout")
                sq = [sb(sc1, "sq%d" % i, [128, 256], F32) for i in range(2)]
                sqb = [Buf(), Buf()]
                ss4 = [sb(sc1, "ss4%d" % i, [128, 4], F32) for i in range(2)]
                ss4b = [Buf(), Buf()]
                q16 = [sb(sc1, "q16%d" % i, [128, 128], BF16) for i in range(2)]
                k16 = [sb(sc1, "k16%d" % i, [128, 128], BF16) for i in range(2)]
                q16b = [Buf(), Buf()]; k16b = [Buf(), Buf()]
                e32 = [sb(sc1, "e32%d" % i, [128, 512], F32) for i in range(2)]
                spb16 = [sb(sc1, "spb%d" % i, [128, 512], BF16) for i in range(2)]
                wb16 = [sb(sc1, "wb%d" % i, [128, 512], BF16) for i in range(2)]
                e32b = [Buf(), Buf()]; spbb = [Buf(), Buf()]; wbb = [Buf(), Buf()]
                sps32 = [sb(sc1, "sps32%d" % i, [128, 512], F32) for i in range(2)]
                sps16 = [sb(sc1, "sps16%d" % i, [128, 512], BF16) for i in range(2)]
                sps32b = [Buf(), Buf()]; sps16b = [Buf(), Buf()]

                w_qkv_v = w_qkv.rearrange("(c p) n -> p c n", p=128)

                def load_wq(hp):
                    i = hp % 2
                    for part in range(3):
                        S.dma("pool", ch_wq[i], wq_sb[i][:, :, part * 128:(part + 1) * 128],
                              w_qkv_v[:, :, part * D + hp * 128: part * D + (hp + 1) * 128], writes=[wqb[i]])

                load_wq(0)
                blk = [0]
                grp = [0]
                for hp in ([] if "qkv" in DEBUG_SKIP else range(1 if "qkv1" in DEBUG_SKIP else 8)):
                    if hp + 1 < 8:
                        load_wq(hp + 1)
                    wq = wq_sb[hp % 2]
                    for t in range(17):
                        P = 128 if t < 16 else NS
                        cols = slice(t * 128, t * 128 + P)
                        ps = psF[t % 2]; psb_ = psFb[t % 2]
                        for c in range(8):
                            S.op("pe", lambda h: h.matmul(ps[:P, 0:384], lhsT=hT[:, c, cols], rhs=wq[:, c, :],
                                                          start=(c == 0), stop=(c == 7)),
                                 [hTb, wqb[hp % 2]], [psb_], inc=(c == 7))
                        i2 = t % 2
                        slot = t if t < 9 else t - 9
                        S.op("act", lambda h: h.activation(out=sq[i2][:P, :], in_=ps[:P, 0:256], func=AF.Square),
                             [psb_], [sqb[i2]])
                        S.op("dve", lambda h: h.tensor_reduce(out=ss4[i2][:P, :], in_=sq[i2][:P, :].rearrange("p (g d) -> p g d", d=64),
                                                              axis=AX.X, op=ALU.add), [sqb[i2]], [ss4b[i2]])
                        S.op("act", lambda h: h.activation(out=ss4[i2][:P, :], in_=ss4[i2][:P, :], func=AF.Ln, bias=EPS, scale=1.0 / 64),
                             [ss4b[i2]], [ss4b[i2]])
                        S.op("act", lambda h: h.activation(out=ss4[i2][:P, :], in_=ss4[i2][:P, :], func=AF.Exp, scale=-0.5),
                             [ss4b[i2]], [ss4b[i2]])
                        for g in range(2):
                            S.op("dve", lambda h: h.scalar_tensor_tensor(out=q16[i2][:P, g * 64:(g + 1) * 64], in0=ps[:P, g * 64:(g + 1) * 64],
                                                                         scalar=ss4[i2][:P, g:g + 1], in1=qg_rep[:P, 0, :],
                                                                         op0=ALU.mult, op1=ALU.mult),
                                 [psb_, ss4b[i2], constb], [q16b[i2]])
                        for g in range(2):
                            S.op("dve", lambda h: h.scalar_tensor_tensor(out=kv32[:P, slot, g * 64:(g + 1) * 64],
                                                                         in0=ps[:P, 128 + g * 64:128 + (g + 1) * 64],
                                                                         scalar=ss4[i2][:P, 2 + g:3 + g], in1=kg_rep[:P, 0, :],
                                                                         op0=ALU.mult, op1=ALU.mult),
                                 [psb_, ss4b[i2], constb], [k32b])
                        S.op("act", lambda h: h.activation(out=kv32[:P, slot, 128:256], in_=ps[:P, 256:384], func=AF.Copy),
                             [psb_], [v32b])
                        ceng = "dve" if "nopool" in DEBUG_SKIP else "pool"
                        S.op(ceng, lambda h: h.tensor_copy(out=k16[i2][:P, :], in_=kv32[:P, slot, 0:128]), [k32b], [k16b[i2]])
                        S.op(ceng, lambda h: h.tensor_copy(out=V16[:P, t, :], in_=kv32[:P, slot, 128:256]), [v32b], [V16b])
                        pb = psB[t % 2]
                        S.op("pe", lambda h: h.transpose(pb[:, 0:P], q16[i2][:P, :], ident[:P, :P]), [q16b[i2], constb], [psBb[t % 2]], inc=False)
                        S.op("pe", lambda h: h.transpose(pb[:, 128:128 + P], k16[i2][:P, :], ident[:P, :P]), [k16b[i2], constb], [psBb[t % 2]])
                        evac(QKT[:, :, cols], pb[:, 0:256].rearrange("p (a t) -> p a t", a=2)[:, :, 0:P], [psBb[t % 2]], [QKTb])
                        if t == 8 and "kvout" not in DEBUG_SKIP:
                            S.dma("sp", ch_ko, kv_dev[hp, :, 0:9, :], kv32[:, 0:9, :], reads=[k32b])
                        if t == 16 and "kvout" not in DEBUG_SKIP:
                            S.dma("sp", ch_ko, kv_dev[hp, :, 9:17, :], kv32[:, 0:8, :], reads=[k32b])
                    ceng = "dve" if "nopool" in DEBUG_SKIP else "pool"
                    for hh in range(2):
                        r = slice(hh * 64, (hh + 1) * 64)
                        hq = (2 * hp + hh) * 8
                        S.op(ceng, lambda h: h.tensor_copy(out=Qbd[r, :, hp, hq:hq + 8],
                                                             in_=QKT[r, 0, SEQ:NT].rearrange("p (b t) -> p b t", t=8)),
                             [QKTb], [smpb])
                    S.op(ceng, lambda h: h.tensor_copy(out=KTs[:, hp, :], in_=QKT[:, 1, SEQ:NT]), [QKTb], [smpb])
                    S.op(ceng, lambda h: h.tensor_copy(out=Vs16[:, hp * 128:(hp + 1) * 128], in_=V16[0:NS, 16, :]), [V16b], [smpb])

                    def make_group(hh, sbk):
                        hd = 2 * hp + hh
                        r = slice(hh * 64, (hh + 1) * 64)
                        Q0 = 512 * sbk
                        gi = grp[0] % 2; grp[0] += 1
                        po = psF[2 + gi]; pob = psFb[2 + gi]
                        def start():
                            S.op("pool", lambda h: h.memset(sps32[gi][:], 0.0), [], [sps32b[gi]])
                        kbs = list(range(4 * sbk + 3, -1, -1))
                        info = []
                        for ki, kb in enumerate(kbs):
                            diag = kb >= 4 * sbk
                            qlo = (kb - 4 * sbk) * 128 if diag else 0
                            info.append((kb, diag, qlo, 512 - qlo, blk[0] % 2))
                            blk[0] += 1

                        def stage_a(ki):
                            kb, diag, qlo, N, bi = info[ki]
                            pz = psF[4 + bi]; pzb = psFb[4 + bi]
                            S.op("pe", lambda h: h.matmul(pz[:, 0:N], lhsT=QKT[r, 1, kb * 128:(kb + 1) * 128],
                                                          rhs=QKT[r, 0, Q0 + qlo:Q0 + 512], start=True, stop=False,
                                                          skip_group_check=True), [QKTb], [pzb])
                            S.op("act", lambda h: h.activation(out=e32[bi][:, 0:N], in_=pz[:, 0:N], func=AF.Exp,
                                                               bias=bias_rep[:, 0, hd:hd + 1], scale=0.125),
                                 [pzb, constb], [e32b[bi]])
                            if diag:
                                S.op("dve", lambda h: h.tensor_tensor(out=e32[bi][:, 0:128], in0=e32[bi][:, 0:128], in1=maskd[:, :],
                                                                      op=ALU.mult), [e32b[bi], constb], [e32b[bi]])
                            S.op("act", lambda h: h.activation(out=spb16[bi][:, 0:N], in_=e32[bi][:, 0:N], func=AF.Ln,
                                                               bias=1.0, scale=1.0), [e32b[bi]], [spbb[bi]])

                        def stage_b(ki):
                            kb, diag, qlo, N, bi = info[ki]
                            first = ki == 0
                            last = ki == len(kbs) - 1
                            pz = psF[4 + bi]; pzb = psFb[4 + bi]
                            S.op("pe", lambda h: h.matmul(pz[:, 0:N], lhsT=trim8[:, :], rhs=spb16[bi][:, 0:N], start=False,
                                                          stop=first, skip_group_check=True), [spbb[bi], constb, pzb], [pzb],
                                 inc=first)
                            if not first:
                                S.op("pe", lambda h: h.matmul(pz[:, 0:N], lhsT=onesm8[:, :], rhs=sps16[gi][:, qlo:512], start=False,
                                                              stop=True, skip_group_check=True), [sps16b[gi], constb, pzb], [pzb])
                            S.op("act", lambda h: h.activation(out=wb16[bi][:, 0:N], in_=pz[:, 0:N], func=AF.Exp,
                                                               bias=bias_rep[:, 0, hd:hd + 1], scale=0.125),
                                 [pzb, constb], [wbb[bi]])
                            if diag:
                                S.op("dve", lambda h: h.tensor_tensor(out=wb16[bi][:, 0:128], in0=wb16[bi][:, 0:128], in1=maskd[:, :],
                                                                      op=ALU.mult), [wbb[bi], constb], [wbb[bi]])
                            S.op("pe", lambda h: h.matmul(po[:, qlo:512], lhsT=V16[:, kb, :], rhs=wb16[bi][:, 0:N], start=first,
                                                          stop=last, skip_group_check=True), [V16b, wbb[bi], pob], [pob])
                            if not last:
                                kn = info[ki + 1]
                                qn = kn[2]
                                S.op("dve", lambda h: h.tensor_tensor(out=sps32[gi][:, qlo:512], in0=sps32[gi][:, qlo:512],
                                                                      in1=spb16[bi][:, 0:N], op=ALU.add),
                                     [sps32b[gi], spbb[bi]], [sps32b[gi]])
                                S.op("pool", lambda h: h.tensor_copy(out=sps16[gi][:, qn:512], in_=sps32[gi][:, qn:512]),
                                     [sps32b[gi]], [sps16b[gi]])

                        def finish():
                            evac(OT[r, hp, Q0:Q0 + 512], po[r, 0:512], [pob], [OTb])

                        return start, stage_a, stage_b, len(kbs), finish

                    groups = [] if "attn" in DEBUG_SKIP else [make_group(hh, sbk) for hh in range(2) for sbk in range(4)]
                    if groups:
                        groups[0][0]()
                        groups[0][1](0)
                    for gx, (g_start, g_a, g_b, g_n, g_fin) in enumerate(groups):
                        for ki in range(g_n):
                            if ki + 1 < g_n:
                                g_a(ki + 1)
                            elif gx + 1 < len(groups):
                                groups[gx + 1][0]()
                                groups[gx + 1][1](0)
                            g_b(ki)
                        g_fin()
                S.barrier()

            with ExitStack() as sc2:
                NRING = 8
                Kr = [sb(sc2, "Kr%d" % i, [128, D], BF16) for i in range(NRING)]
                Vr = [sb(sc2, "Vr%d" % i, [128, D], BF16) for i in range(NRING)]
                Krb = [Buf() for _ in range(NRING)]; Vrb = [Buf() for _ in range(NRING)]
                ch_K = [S.chan("K%d" % i) for i in range(NRING)]
                ch_V = [S.chan("V%d" % i) for i in range(NRING)]
                KTb_sb = [sb(sc2, "KTb%d" % i, [128, 8, 512], BF16) for i in range(2)]
                KTbb = [Buf(), Buf()]
                es32 = [sb(sc2, "es32%d" % i, [128, 512], F32) for i in range(2)]
                sp32 = [sb(sc2, "sp32%d" % i, [128, 512], F32) for i in range(2)]
                cs32 = [sb(sc2, "cs32%d" % i, [128, 513], F32) for i in range(2)]
                ec32 = [sb(sc2, "ec32%d" % i, [128, 512], F32) for i in range(2)]
                w16 = [sb(sc2, "w16%d" % i, [128, 512], BF16) for i in range(2)]
                wT = [sb(sc2, "wT%d" % i, [128, 4, 128], BF16) for i in range(2)]
                esb = [Buf(), Buf()]; spb_ = [Buf(), Buf()]; csb = [Buf(), Buf()]; ecb = [Buf(), Buf()]
                w16b = [Buf(), Buf()]; wTb = [Buf(), Buf()]
                ones32 = sb(sc2, "ones32", [128, 512], F32)
                masknew = sb(sc2, "masknew", [128, 4, 32], F32)
                negR = sb(sc2, "negR", [128, 1], F32)
                negRb = Buf()
                Of16 = sb(sc2, "Of16", [128, D], BF16)
                Ofb = Buf()
                pt_i = sb(sc2, "pt_i", [128, 1, 256], I32)
                pt_f = sb(sc2, "pt_f", [128, 256], F32)
                idx_i = sb(sc2, "idx_i", [128, 256], I32)
                iotap = sb(sc2, "iotap", [128, 1], F32)
                idxb = Buf(); c2b = Buf()
                ch_c2 = S.chan("c2")
                S.dma("sp", ch_c2, pt_i[:], ptab[0:1, :].partition_broadcast(128), writes=[idxb])
                S.dma("sp", ch_c2, iotap[:], c_iotap[:, :], writes=[idxb])
                S.dma("sp", ch_c2, masknew[:], c_masknew[:, :, :], writes=[c2b])
                S.op("pool", lambda h: h.memset(ones32[:], 1.0), [], [c2b])
                S.op("pool", lambda h: h.memset(cs32[0][:, 0:1], 0.0), [], [csb[0]])
                S.op("pool", lambda h: h.memset(cs32[1][:, 0:1], 0.0), [], [csb[1]])
                S.op("dve", lambda h: h.tensor_copy(out=pt_f[:, :], in_=pt_i[:, 0, :]), [idxb], [idxb])
                S.op("dve", lambda h: h.tensor_scalar(out=pt_f[:, :], in0=pt_f[:, :], scalar1=128.0, scalar2=iotap[:, 0:1],
                                                      op0=ALU.mult, op1=ALU.add), [idxb], [idxb])
                S.op("dve", lambda h: h.tensor_copy(out=idx_i[:, :], in_=pt_f[:, :]), [idxb], [idxb])

                ring = [0]; sblk = [0]; tb = [0]

                def stick_s(ps, psb_, Wd, bl, mask_ap):
                    i = sblk[0] % 2; sblk[0] += 1
                    S.op("act", lambda h: h.activation(out=es32[i][:, 0:Wd], in_=ps[:, 0:Wd], func=AF.Exp, bias=bias_hq[:, 0:1], scale=0.125),
                         [psb_, constb], [esb[i]])
                    if mask_ap is not None:
                        S.op("dve", lambda h: h.tensor_tensor(out=es32[i][:, 0:Wd], in0=es32[i][:, 0:Wd], in1=mask_ap, op=ALU.mult),
                             [esb[i], c2b], [esb[i]])
                    S.op("act", lambda h: h.activation(out=sp32[i][:, 0:Wd], in_=es32[i][:, 0:Wd], func=AF.Ln, bias=1.0, scale=1.0),
                         [esb[i]], [spb_[i]])
                    S.op("dve", lambda h: h.tensor_tensor_scan(out=cs32[i][:, 1:Wd + 1], data0=ones32[:, 0:Wd], data1=sp32[i][:, 0:Wd],
                                                               initial=0.0, op0=ALU.mult, op1=ALU.add), [spb_[i], c2b], [csb[i]])
                    S.op("dve", lambda h: h.tensor_tensor(out=negR[:, :], in0=negR[:, :], in1=cs32[i][:, Wd:Wd + 1], op=ALU.subtract),
                         [csb[i], negRb], [negRb])
                    S.op("act", lambda h: h.activation(out=ec32[i][:, 0:Wd], in_=cs32[i][:, 0:Wd], func=AF.Exp, bias=negR[:, 0:1], scale=1.0),
                         [csb[i], negRb], [ecb[i]])
                    S.op("dve", lambda h: h.tensor_tensor(out=w16[i][:, 0:Wd], in0=es32[i][:, 0:Wd], in1=ec32[i][:, 0:Wd], op=ALU.mult),
                         [esb[i], ecb[i]], [w16b[i]])
                    return i

                for bl in ([] if "sample_attn" in DEBUG_SKIP else range(4)):
                    pO = [psF[2], psF[3]]; pOb = [psFb[2], psFb[3]]
                    S.op("pool", lambda h: h.memset(negR[:], 0.0), [], [negRb])
                    ps = psF[0]; psb_ = psFb[0]
                    for c in range(8):
                        S.op("pe", lambda h: h.matmul(ps[:, 0:NS], lhsT=Qbd[:, bl, c, :], rhs=KTs[:, c, :], start=(c == 0), stop=(c == 7)),
                             [smpb], [psb_], inc=(c == 7))
                    i = stick_s(ps, psb_, NS, bl, masknew[:, bl, :])
                    ti = tb[0] % 2; tb[0] += 1
                    S.op("pe", lambda h: h.transpose(psB[ti][0:NS, 0:128], w16[i][:, 0:NS], ident[:, :]), [w16b[i], constb], [psBb[ti]])
                    evac(wT[ti][0:NS, 0, :], psB[ti][0:NS, 0:128], [psBb[ti]], [wTb[ti]])
                    for hv in range(2):
                        S.op("pe", lambda h: h.matmul(pO[hv][:, :], lhsT=wT[ti][0:NS, 0, :], rhs=Vs16[:, hv * 512:(hv + 1) * 512],
                                                      start=True, stop=False, skip_group_check=True), [wTb[ti], smpb], [pOb[hv]])
                    for jb in range(15, -1, -1):
                        kt = jb % 2
                        slots = []
                        for pgi in range(4):
                            pg = 4 * jb + pgi
                            sl = ring[0] % NRING; ring[0] += 1
                            slots.append(sl)
                            col = bl * 64 + pg
                            S.idma(ch_K[sl], Kr[sl][:, :], ck[:, :], idx_i[:, col:col + 1], reads=[idxb], writes=[Krb[sl]])
                            S.idma(ch_V[sl], Vr[sl][:, :], cv[:, :], idx_i[:, col:col + 1], reads=[idxb], writes=[Vrb[sl]])
                        for pgi in range(4):
                            sl = slots[pgi]
                            ti = tb[0] % 2; tb[0] += 1
                            for c in range(8):
                                S.op("pe", lambda h: h.transpose(psB[ti][:, c * 128:(c + 1) * 128], Kr[sl][:, c * 128:(c + 1) * 128], ident[:, :]),
                                     [Krb[sl], constb], [psBb[ti]], inc=(c == 7))
                            evac(KTb_sb[kt][:, :, pgi * 128:(pgi + 1) * 128], psB[ti].rearrange("p (c t) -> p c t", c=8),
                                 [psBb[ti]], [KTbb[kt]])
                        pi = jb % 2
                        ps = psF[pi]; psb_ = psFb[pi]
                        for c in range(8):
                            S.op("pe", lambda h: h.matmul(ps[:, 0:512], lhsT=Qbd[:, bl, c, :], rhs=KTb_sb[kt][:, c, :], start=(c == 0), stop=(c == 7)),
                                 [smpb, KTbb[kt]], [psb_], inc=(c == 7))
                        i = stick_s(ps, psb_, 512, bl, None)
                        ti = tb[0] % 2; tb[0] += 1
                        for pgi in range(4):
                            S.op("pe", lambda h: h.transpose(psB[ti][:, pgi * 128:(pgi + 1) * 128], w16[i][:, pgi * 128:(pgi + 1) * 128], ident[:, :]),
                                 [w16b[i], constb], [psBb[ti]], inc=(pgi == 3))
                        evac(wT[ti][:, :, :], psB[ti][:, 0:512].rearrange("p (a t) -> p a t", a=4), [psBb[ti]], [wTb[ti]])
                        for pgi in range(4):
                            sl = slots[pgi]
                            for hv in range(2):
                                lastmm = (jb == 0 and pgi == 3)
                                S.op("pe", lambda h: h.matmul(pO[hv][:, :], lhsT=wT[ti][:, pgi, :], rhs=Vr[sl][:, hv * 512:(hv + 1) * 512],
                                                              start=False, stop=lastmm, skip_group_check=True),
                                     [wTb[ti], Vrb[sl], pOb[hv]], [pOb[hv]])
                    for hv in range(2):
                        evac(Of16[:, hv * 512:(hv + 1) * 512], pO[hv][:, :], [pOb[hv]], [Ofb])
                    ti = tb[0] % 2; tb[0] += 1
                    for c in range(8):
                        S.op("pe", lambda h: h.transpose(psB[ti][:, c * 128:(c + 1) * 128], Of16[:, c * 128:(c + 1) * 128], ident[:, :]),
                             [Ofb, constb], [psBb[ti]], inc=(c == 7))
                    for c in range(8):
                        for hh in range(2):
                            r = slice(hh * 64, (hh + 1) * 64)
                            cc = c * 128 + (2 * c + hh) * 8
                            evac(OT[r, c, SEQ + bl * 8:SEQ + bl * 8 + 8], psB[ti][r, cc:cc + 8], [psBb[ti]], [OTb])
                S.barrier()

            with ExitStack() as sc3:
                wo_sb = sb(sc3, "wo_sb", [128, 8, D], BF16)
                wob = Buf(); ch_wo = S.chan("wo")
                S.dma("pool", ch_wo, wo_sb[:], w_o.rearrange("(c p) n -> p c n", p=128), writes=[wob])
                for s in ([] if "wo" in DEBUG_SKIP else range(17)):
                    P = tp(s)
                    for hv in range(2):
                        pi = (2 * s + hv) % 4
                        ps = psF[pi]; psb_ = psFb[pi]
                        for c in range(8):
                            S.op("pe", lambda h: h.matmul(ps[:P, :], lhsT=OT[:, c, tcols(s)], rhs=wo_sb[:, c, hv * 512:(hv + 1) * 512],
                                                          start=(c == 0), stop=(c == 7)), [OTb, wob], [psb_], inc=(c == 7))
                        S.op("dve", lambda h: h.tensor_tensor(out=x_sb[:P, s, hv * 512:(hv + 1) * 512], in0=x_sb[:P, s, hv * 512:(hv + 1) * 512],
                                                              in1=ps[:P, :], op=ALU.add), [psb_, xb[s]], [xb[s]])
                S.barrier()

        def conv_ffn(li):
            if "ffn" in DEBUG_SKIP:
                return
            with ExitStack() as sc:
                hT = sb(sc, "hTf", [128, 8, NT], BF16)
                hTb = Buf()
                rms_to_hT(norm_ffn[li:li + 1, :], hT, hTb)
                gT = sb(sc, "gT", [128, 8, NT], BF16)
                gTb = Buf()
                wu = [sb(sc, "wu%d" % i, [128, 8, 256], BF16) for i in range(2)]
                wub = [Buf(), Buf()]; ch_wu = [S.chan("wu%d_%d" % (li, i)) for i in range(2)]
                wd = sb(sc, "wd", [128, 8, D], BF16)
                wdb = Buf(); ch_wd = S.chan("wd%d" % li)
                cw = sb(sc, "cw", [128, NFC, 3], F32)
                cb = sb(sc, "cb", [128, NFC], F32)
                prevT = sb(sc, "prevT", [128, NFC, 4, 2], F32)
                cpb = Buf(); ch_cp = S.chan("cp%d" % li)
                a32 = [sb(sc, "a32%d" % i, [128, 2 + SEQ], F32) for i in range(2)]
                a32b = [Buf(), Buf()]
                as32 = [sb(sc, "as32%d" % i, [128, 4, 10], F32) for i in range(2)]
                as32b = [Buf(), Buf()]
                c32 = [sb(sc, "c32%d" % i, [128, 512], F32) for i in range(2)]
                s32 = [sb(sc, "s32%d" % i, [128, 512], F32) for i in range(2)]
                c32b = [Buf(), Buf()]; s32b = [Buf(), Buf()]
                cst = sb(sc, "cst", [128, NFC, 2], F32)
                csts = sb(sc, "csts", [128, NFC, 4, 2], F32)
                cstb = Buf(); ch_co = S.chan("co%d" % li)
                S.dma("sp", ch_cp, cw[:], conv_wl[li], writes=[cpb])
                S.dma("sp", ch_cp, cb[:], conv_bl[li], writes=[cpb])
                S.dma("sp", ch_cp, prevT[:], conv_prevT[li], writes=[cpb])
                for i in range(2):
                    S.op("pool", lambda h: h.memset(a32[i][:, 0:2], 0.0), [], [a32b[i]])
                w_up_v = w_up[li].rearrange("(c p) n -> p c n", p=128)

                def load_wu(fc, i):
                    S.dma("pool", ch_wu[i], wu[i][:, :, 0:128], w_up_v[:, :, fc * 128:(fc + 1) * 128], writes=[wub[i]])
                    S.dma("pool", ch_wu[i], wu[i][:, :, 128:256], w_up_v[:, :, DFF + fc * 128:DFF + (fc + 1) * 128], writes=[wub[i]])

                cnt = [0]
                load_wu(0, 0)
                fcn = 0
                for grp_fcs in FFN_GROUPS:
                    for k, fc in enumerate(grp_fcs):
                        wi = fcn % 2
                        if fc + 1 < NFC:
                            load_wu(fc + 1, (fcn + 1) % 2)
                        ai = fcn % 2
                        fcn += 1
                        for tbk in range(5):
                            T0 = tbk * 512
                            Nt = 512 if tbk < 4 else NS
                            pa = psF[(2 * cnt[0]) % 4]; pab = psFb[(2 * cnt[0]) % 4]
                            pb_ = psF[(2 * cnt[0] + 1) % 4]; pbb = psFb[(2 * cnt[0] + 1) % 4]
                            ci = cnt[0] % 2
                            cnt[0] += 1
                            for c in range(8):
                                S.op("pe", lambda h: h.matmul(pa[:, 0:Nt], lhsT=wu[wi][:, c, 0:128], rhs=hT[:, c, T0:T0 + Nt],
                                                              start=(c == 0), stop=(c == 7)), [wub[wi], hTb], [pab], inc=(c == 7))
                            for c in range(8):
                                S.op("pe", lambda h: h.matmul(pb_[:, 0:Nt], lhsT=wu[wi][:, c, 128:256], rhs=hT[:, c, T0:T0 + Nt],
                                                              start=(c == 0), stop=(c == 7)), [wub[wi], hTb], [pbb], inc=(c == 7))
                            if tbk < 4:
                                A = a32[ai]
                                S.op("act", lambda h: h.activation(out=A[:, 2 + T0:2 + T0 + Nt], in_=pa[:, 0:Nt], func=AF.Copy),
                                     [pab], [a32b[ai]])
                                srcs = [A[:, T0 + j:T0 + j + Nt] for j in range(3)]
                                cdst = c32[ci][:, 0:Nt]; sdst = s32[ci][:, 0:Nt]
                                gdst = gT[:, k, T0:T0 + Nt]
                                rd = [a32b[ai], cpb]
                            else:
                                A = as32[ai]
                                S.op("pool", lambda h: h.tensor_copy(out=A[:, :, 0:2], in_=prevT[:, fc, :, :]), [cpb], [as32b[ai]])
                                S.op("act", lambda h: h.activation(out=A[:, :, 2:10], in_=pa[:, 0:Nt].rearrange("p (b t) -> p b t", t=8),
                                                                   func=AF.Copy), [pab], [as32b[ai]])
                                srcs = [A[:, :, j:j + 8] for j in range(3)]
                                cdst = c32[ci][:, 0:Nt].rearrange("p (b t) -> p b t", t=8)
                                sdst = s32[ci][:, 0:Nt]
                                gdst = gT[:, k, T0:T0 + Nt]
                                rd = [as32b[ai], cpb]
                            S.op("dve", lambda h: h.tensor_scalar(out=cdst, in0=srcs[0], scalar1=cw[:, fc, 0:1], scalar2=cb[:, fc:fc + 1],
                                                                  op0=ALU.mult, op1=ALU.add), rd, [c32b[ci]])
                            for j in (1, 2):
                                S.op("dve", lambda h: h.scalar_tensor_tensor(out=cdst, in0=srcs[j], scalar=cw[:, fc, j:j + 1], in1=cdst,
                                                                             op0=ALU.mult, op1=ALU.add), rd + [c32b[ci]], [c32b[ci]])
                            S.op("act", lambda h: h.activation(out=sdst, in_=c32[ci][:, 0:Nt], func=AF.Silu), [c32b[ci]], [s32b[ci]])
                            S.op("dve", lambda h: h.tensor_tensor(out=gdst, in0=s32[ci][:, 0:Nt], in1=pb_[:, 0:Nt], op=ALU.mult),
                                 [s32b[ci], pbb], [gTb])
                        S.op("pool", lambda h: h.tensor_copy(out=cst[:, fc, :], in_=a32[ai][:, SEQ:SEQ + 2]), [a32b[ai]], [cstb])
                        S.op("pool", lambda h: h.tensor_copy(out=csts[:, fc, :, :], in_=as32[ai][:, :, 8:10]), [as32b[ai]], [cstb])
                        S.dma("pool", ch_wd, wd[:, k, :], w_down[li, fc * 128:(fc + 1) * 128, :], writes=[wdb])
                    ng = len(grp_fcs)
                    for s in range(17):
                        P = tp(s)
                        for hv in range(2):
                            pi = 4 + (2 * s + hv) % 2
                            ps = psF[pi]; psb_ = psFb[pi]
                            for k in range(ng):
                                S.op("pe", lambda h: h.matmul(ps[:P, :], lhsT=gT[:, k, tcols(s)], rhs=wd[:, k, hv * 512:(hv + 1) * 512],
                                                              start=(k == 0), stop=(k == ng - 1)), [gTb, wdb], [psb_], inc=(k == ng - 1))
                            S.op("dve", lambda h: h.tensor_tensor(out=x_sb[:P, s, hv * 512:(hv + 1) * 512],
                                                                  in0=x_sb[:P, s, hv * 512:(hv + 1) * 512], in1=ps[:P, :], op=ALU.add),
                                 [psb_, xb[s]], [xb[s]])
                S.dma("sp", ch_co, conv_p[li], cst[:], reads=[cstb])
                S.dma("sp", ch_co, conv_s[li], csts[:], reads=[cstb])
                S.barrier()

        conv_ffn(0)

        NCH = 260
        with ExitStack() as sc_s5:
          if "s5" not in DEBUG_SKIP:
              U = sb(sc_s5, "U", [128, 64, NCH], BF16)
              Ub = Buf()
              with ExitStack() as scu:
                  u_tok = sb(scu, "u_tok", [128, 2, 64, 128], BF16)
                  u_toks = sb(scu, "u_toks", [4, 64, 128], BF16)
                  utb = Buf()
                  with ExitStack() as sch:
                      hT = sb(sch, "hTs", [128, 8, NT], BF16)
                      hTb = Buf()
                      rms_to_hT(norm_mix[1:2, :], hT, hTb)
                      win = sb(sch, "win", [128, 8, 512], BF16)
                      winb = Buf(); ch_win = S.chan("win")
                      w_in_v = w_in.rearrange("(c p) n -> p c n", p=128)
                      n = 0
                      for hv in range(2):
                          S.dma("pool", ch_win, win[:], w_in_v[:, :, hv * 512:(hv + 1) * 512], writes=[winb])
                          for s in range(16):
                              ps = psF[n % 4]; psb_ = psFb[n % 4]; n += 1
                              for c in range(8):
                                  S.op("pe", lambda h: h.matmul(ps[:, :], lhsT=hT[:, c, tcols(s)], rhs=win[:, c, :],
                                                                start=(c == 0), stop=(c == 7)), [hTb, winb], [psb_], inc=(c == 7))
                              evac(u_tok[:, s // 8, hv * 32:(hv + 1) * 32, (s % 8) * 16:(s % 8 + 1) * 16], ps[:, :].rearrange("p (g c) -> p g c", c=16), [psb_], [utb])
                          for j in range(8):
                              ps = psF[n % 4]; psb_ = psFb[n % 4]; n += 1
                              for c in range(8):
                                  S.op("pe", lambda h: h.matmul(ps[0:4, :], lhsT=hT[:, c, SEQ + j:NT:8], rhs=win[:, c, :],
                                                                start=(c == 0), stop=(c == 7)), [hTb, winb], [psb_], inc=(c == 7))
                              evac(u_toks[0:4, hv * 32:(hv + 1) * 32, j * 16:(j + 1) * 16], ps[0:4, :].rearrange("p (g c) -> p g c", c=16), [psb_], [utb])
                      S.barrier()
                  for g0 in range(0, 64, 3):
                      gs = list(range(g0, min(g0 + 3, 64)))
                      ti = (g0 // 3) % 2
                      for gi, g in enumerate(gs):
                          base = gi * NCH
                          for hf in range(2):
                              S.op("pe", lambda h: h.transpose(psB[ti][:, base + hf * 128:base + (hf + 1) * 128],
                                                               u_tok[:, hf, g, :], ident[:, :]),
                                   [utb, constb], [psBb[ti]], inc=False)
                          S.op("pe", lambda h: h.transpose(psB[ti][:, base + 256:base + 260], u_toks[0:4, g, :], ident[0:4, 0:4]),
                               [utb, constb], [psBb[ti]], inc=(gi == len(gs) - 1))
                      evac(U[:, g0:g0 + len(gs), :], psB[ti][:, 0:len(gs) * NCH].rearrange("p (g n) -> p g n", n=NCH), [psBb[ti]], [Ub])
                  S.barrier()

              with ExitStack() as scw:
                  Mw = sb(scw, "Mw", [128, 64, 128], BF16)
                  WBre = sb(scw, "WBre", [128, 32, 128], BF16)
                  WBim = sb(scw, "WBim", [128, 32, 128], BF16)
                  WCre = sb(scw, "WCre", [128, 32, 128], BF16)
                  nWCim = sb(scw, "nWCim", [128, 32, 128], BF16)
                  a8re = sb(scw, "a8re", [128, 32], F32); a8im = sb(scw, "a8im", [128, 32], F32); na8im = sb(scw, "na8im", [128, 32], F32)
                  r8 = sb(scw, "r8", [128, 32], F32); phir = sb(scw, "phir", [128, 32], F32)
                  dlay = sb(scw, "dlay", [128, 64], F32)
                  spr = sb(scw, "spr", [128, 32, 4], F32); spi = sb(scw, "spi", [128, 32, 4], F32)
                  spr16 = sb(scw, "spr16", [128, 32, 4], BF16); spi16 = sb(scw, "spi16", [128, 32, 4], BF16)
                  fin_re = sb(scw, "fin_re", [128, 32], F32); fin_im = sb(scw, "fin_im", [128, 32], F32)
                  fins_re = sb(scw, "fins_re", [128, 32, 4], F32); fins_im = sb(scw, "fins_im", [128, 32, 4], F32)
                  iotan = sb(scw, "iotan", [128, 256], F32)
                  maskM = sb(scw, "maskM", [128, 128], F32)
                  Wb = Buf(); prmb = Buf(); finb = Buf()
                  ch_p = S.chan("s5p"); ch_fo = S.chan("s5o")
                  with ExitStack() as scp:
                      def t32(name, shape):
                          return sb(scp, name, shape, F32)
                      lr = t32("lr", [128, 32]); lim = t32("lim", [128, 32]); ldt = t32("ldt", [128, 32])
                      Bre = t32("Bre", [128, 32, 16]); Bim = t32("Bim", [128, 32, 16])
                      Cre = t32("Cre", [128, 32, 16]); Cim = t32("Cim", [128, 32, 16])
                      for dst, src in ((lr, lamT_re), (lim, lamT_im), (ldt, logdtT), (Bre, BT_re), (Bim, BT_im), (Cre, CT_re), (Cim, CT_im),
                                       (dlay, d_lay), (spr, sprev_re), (spi, sprev_im), (iotan, c_iotan), (maskM, c_maskM)):
                          S.dma("sp", ch_p, dst[:], src, writes=[prmb])
                      dt_ = t32("dt_", [128, 32]); lrdt = t32("lrdt", [128, 32]); lidt = t32("lidt", [128, 32]); mag = t32("mag", [128, 32])
                      ang = t32("ang", [128, 32]); sinv = t32("sinv", [128, 32]); cosv = t32("cosv", [128, 32])
                      are = t32("are", [128, 32]); aim = t32("aim", [128, 32]); den = t32("den", [128, 32]); tmp = t32("tmp", [128, 32]); tmp2 = t32("tmp2", [128, 32])
                      zr = t32("zr", [128, 32]); zi = t32("zi", [128, 32]); am1 = t32("am1", [128, 32])
                      Apr = t32("Apr", [128, 32, 9]); Api = t32("Api", [128, 32, 9])
                      Bbr = t32("Bbr", [128, 32, 16]); Bbi = t32("Bbi", [128, 32, 16])
                      T1 = t32("T1", [128, 32, 16]); T2 = t32("T2", [128, 32, 16])
                      Aipr = t32("Aipr", [128, 32, 8]); Aipi = t32("Aipi", [128, 32, 8])
                      ivr = t32("ivr", [128, 32]); ivi = t32("ivi", [128, 32]); m2 = t32("m2", [128, 32])

                      def P_(eng, fn):
                          S.op(eng, fn, [prmb], [prmb])

                      def tt(out, a, b, op, eng="dve"):
                          P_(eng, lambda h: h.tensor_tensor(out=out, in0=a, in1=b, op=op))

                      def ts(out, a, s1, s2, op0, op1=None, eng="dve"):
                          if op1 is None:
                              P_(eng, lambda h: h.tensor_scalar(out=out, in0=a, scalar1=s1, scalar2=None, op0=op0))
                          else:
                              P_(eng, lambda h: h.tensor_scalar(out=out, in0=a, scalar1=s1, scalar2=s2, op0=op0, op1=op1))

                      def act(out, a, func, bias=0.0, scale=1.0):
                          P_("act", lambda h: h.activation(out=out, in_=a, func=func, bias=bias, scale=scale))

                      def cmul(o_re, o_im, ar, ai, br, bi, t1, t2, neg_im=False):
                          tt(t1, ar, br, ALU.mult); tt(t2, ai, bi, ALU.mult)
                          tt(o_re, t1, t2, ALU.subtract)
                          tt(t1, ar, bi, ALU.mult); tt(t2, ai, br, ALU.mult)
                          if neg_im:
                              tt(t1, t1, t2, ALU.add)
                              ts(o_im, t1, -1.0, None, ALU.mult)
                          else:
                              tt(o_im, t1, t2, ALU.add)

                      act(dt_[:], ldt[:], AF.Exp)
                      tt(lrdt[:], lr[:], dt_[:], ALU.mult); tt(lidt[:], lim[:], dt_[:], ALU.mult)
                      act(mag[:], lrdt[:], AF.Exp)
                      ki = sb(scp, "ki", [128, 32], I32)

                      def reduce_pi(dst, x):
                          ts(ki[:], x, 1.0 / (2 * PI), None, ALU.mult)
                          P_("dve", lambda h: h.scalar_tensor_tensor(out=dst, in0=ki[:], scalar=-2 * PI, in1=x, op0=ALU.mult, op1=ALU.add))
                          ts(dst, dst, -PI_SAFE, PI_SAFE, ALU.max, ALU.min)

                      reduce_pi(ang[:], lidt[:])
                      act(sinv[:], ang[:], AF.Sin)
                      act(tmp[:], ang[:], AF.Abs)
                      act(cosv[:], tmp[:], AF.Sin, bias=PI / 2, scale=-1.0)
                      tt(are[:], mag[:], cosv[:], ALU.mult); tt(aim[:], mag[:], sinv[:], ALU.mult)
                      tt(den[:], lr[:], lr[:], ALU.mult); tt(tmp[:], lim[:], lim[:], ALU.mult); tt(den[:], den[:], tmp[:], ALU.add)
                      P_("dve", lambda h: h.reciprocal(out=den[:], in_=den[:]))
                      ts(am1[:], are[:], -1.0, None, ALU.add)
                      tt(tmp[:], am1[:], lr[:], ALU.mult); tt(tmp2[:], aim[:], lim[:], ALU.mult); tt(tmp[:], tmp[:], tmp2[:], ALU.add)
                      tt(zr[:], tmp[:], den[:], ALU.mult)
                      tt(tmp[:], aim[:], lr[:], ALU.mult); tt(tmp2[:], am1[:], lim[:], ALU.mult); tt(tmp[:], tmp[:], tmp2[:], ALU.subtract)
                      tt(zi[:], tmp[:], den[:], ALU.mult)
                      zrb = zr[:, :].unsqueeze(2).broadcast_to([128, 32, 16]); zib = zi[:, :].unsqueeze(2).broadcast_to([128, 32, 16])
                      cmul(Bbr[:], Bbi[:], zrb, zib, Bre[:], Bim[:], T1[:], T2[:])
                      P_("dve", lambda h: h.memset(Apr[:, :, 0], 1.0)); P_("dve", lambda h: h.memset(Api[:, :, 0], 0.0))
                      for k in range(1, 9):
                          cmul(Apr[:, :, k], Api[:, :, k], Apr[:, :, k - 1], Api[:, :, k - 1], are[:], aim[:], tmp[:], tmp2[:])
                      tt(m2[:], mag[:], mag[:], ALU.mult)
                      P_("dve", lambda h: h.reciprocal(out=m2[:], in_=m2[:]))
                      tt(ivr[:], are[:], m2[:], ALU.mult); tt(ivi[:], aim[:], m2[:], ALU.mult); ts(ivi[:], ivi[:], -1.0, None, ALU.mult)
                      P_("dve", lambda h: h.memset(Aipr[:, :, 0], 1.0)); P_("dve", lambda h: h.memset(Aipi[:, :, 0], 0.0))
                      for k in range(1, 8):
                          cmul(Aipr[:, :, k], Aipi[:, :, k], Aipr[:, :, k - 1], Aipi[:, :, k - 1], ivr[:], ivi[:], tmp[:], tmp2[:])
                      P_("dve", lambda h: h.tensor_copy(out=a8re[:], in_=Apr[:, :, 8])); P_("dve", lambda h: h.tensor_copy(out=a8im[:], in_=Api[:, :, 8]))
                      ts(na8im[:], a8im[:], -1.0, None, ALU.mult)
                      ts(tmp[:], lrdt[:], 8.0, None, ALU.mult)
                      act(r8[:], tmp[:], AF.Exp)
                      ts(tmp2[:], lidt[:], 8.0, None, ALU.mult)
                      reduce_pi(phir[:], tmp2[:])

                      def b16(t, gsl, k):
                          ng = gsl.stop - gsl.start
                          return t[:, gsl, k:k + 1].broadcast_to([128, ng, 16])

                      allg = slice(0, 32)
                      for i in range(8):
                          cmul(WCre[:, :, i * 16:(i + 1) * 16], nWCim[:, :, i * 16:(i + 1) * 16], b16(Apr, allg, i + 1), b16(Api, allg, i + 1),
                               Cre[:], Cim[:], T1[:], T2[:], neg_im=True)
                      P_("dve", lambda h: h.tensor_copy(out=spr16[:], in_=spr[:])); P_("dve", lambda h: h.tensor_copy(out=spi16[:], in_=spi[:]))
                      for half in range(2):
                          gsl = slice(16 * half, 16 * half + 16)
                          with ExitStack() as sch2:
                              Lre = sb(sch2, "Lre", [128, 16, 128], BF16); nLim = sb(sch2, "nLim", [128, 16, 128], BF16)
                              Rre = sb(sch2, "Rre", [128, 16, 128], BF16); Rim = sb(sch2, "Rim", [128, 16, 128], BF16)
                              WBr_qp = sb(sch2, "WBr_qp", [128, 16, 128], BF16); WBi_qp = sb(sch2, "WBi_qp", [128, 16, 128], BF16)
                              t1 = T1[:, 0:16, :]; t2 = T2[:, 0:16, :]
                              for j in range(8):
                                  js = slice(j * 16, (j + 1) * 16)
                                  cmul(Lre[:, :, js], nLim[:, :, js], b16(Aipr, gsl, j), b16(Aipi, gsl, j), Bbr[:, gsl, :], Bbi[:, gsl, :], t1, t2, neg_im=True)
                                  cmul(Rre[:, :, js], Rim[:, :, js], b16(Apr, gsl, j), b16(Api, gsl, j), Cre[:, gsl, :], Cim[:, gsl, :], t1, t2)
                                  cmul(WBr_qp[:, :, js], WBi_qp[:, :, js], b16(Apr, gsl, 7 - j), b16(Api, gsl, 7 - j), Bbr[:, gsl, :], Bbi[:, gsl, :], t1, t2)
                              for gl_ in range(16):
                                  gp = 16 * half + gl_
                                  ti = gp % 2
                                  S.op("pe", lambda h: h.transpose(psB[ti][:, 0:128], WBr_qp[:, gl_, :], ident[:, :]), [prmb, constb], [psBb[ti]], inc=False)
                                  S.op("pe", lambda h: h.transpose(psB[ti][:, 128:256], WBi_qp[:, gl_, :], ident[:, :]), [prmb, constb], [psBb[ti]])
                                  evac(WBre[:, gp, :], psB[ti][:, 0:128], [psBb[ti]], [Wb])
                                  evac(WBim[:, gp, :], psB[ti][:, 128:256], [psBb[ti]], [Wb])
                              for gg in range(32):
                                  gl_, q = divmod(gg, 2)
                                  g = 32 * half + gg
                                  r = slice(q * 64, (q + 1) * 64)
                                  pi = g % 4
                                  S.op("pe", lambda h: h.matmul(psF[pi][:, 0:128], lhsT=Lre[r, gl_, :], rhs=Rre[r, gl_, :], start=True, stop=False),
                                       [prmb], [psFb[pi]], inc=False)
                                  S.op("pe", lambda h: h.matmul(psF[pi][:, 0:128], lhsT=nLim[r, gl_, :], rhs=Rim[r, gl_, :], start=False, stop=True),
                                       [prmb], [psFb[pi]])
                                  S.op("dve", lambda h: h.tensor_tensor(out=Mw[:, g, :], in0=psF[pi][:, 0:128], in1=maskM[:, :], op=ALU.mult),
                                       [psFb[pi], prmb], [Wb])
                              S.barrier()
                      S.barrier()

                  with ExitStack() as scc:
                      def t32(name, shape):
                          return sb(scc, name, shape, F32)
                      NB = 2
                      angt = [t32("angt%d" % i, [128, 256]) for i in range(NB)]
                      angc = [t32("angc%d" % i, [128, 256]) for i in range(NB)]
                      kI = [sb(scc, "kI%d" % i, [128, 256], I32) for i in range(NB)]
                      sn = [t32("sn%d" % i, [128, 256]) for i in range(NB)]
                      cn = [t32("cn%d" % i, [128, 256]) for i in range(NB)]
                      Fre = [t32("Fre%d" % i, [128, NCH]) for i in range(NB)]
                      Fim = [t32("Fim%d" % i, [128, NCH]) for i in range(NB)]
                      Gre = [t32("Gre%d" % i, [128, 256]) for i in range(NB)]
                      Gim = [t32("Gim%d" % i, [128, 256]) for i in range(NB)]
                      Tre = [t32("Tre%d" % i, [128, 256]) for i in range(NB)]
                      Tim = [t32("Tim%d" % i, [128, 256]) for i in range(NB)]
                      q1 = [t32("q1%d" % i, [128, 256]) for i in range(NB)]
                      q2 = [t32("q2%d" % i, [128, 256]) for i in range(NB)]
                      Sre = [t32("Sre%d" % i, [128, 256]) for i in range(NB)]
                      Sim = [t32("Sim%d" % i, [128, 256]) for i in range(NB)]
                      Sxr = [sb(scc, "Sxr%d" % i, [128, 257], BF16) for i in range(NB)]
                      Sxi = [sb(scc, "Sxi%d" % i, [128, 257], BF16) for i in range(NB)]
                      y32 = [t32("y32%d" % i, [128, NCH]) for i in range(2)]
                      z32 = [t32("z32%d" % i, [128, NCH]) for i in range(2)]
                      tb_ = [Buf() for _ in range(NB)]
                      yb = [Buf(), Buf()]
                      for i in range(NB):
                          S.op("pool", lambda h: h.memset(Sxr[i][:, 0:1], 0.0), [], [tb_[i]])
                          S.op("pool", lambda h: h.memset(Sxi[i][:, 0:1], 0.0), [], [tb_[i]])
                      GC = math.sqrt(2.0 / PI)
                      for gp in range(32):
                          i = gp % NB
                          B_ = tb_[i]
                          S.op("pool", lambda h: h.tensor_scalar(out=angc[i][:], in0=iotan[:], scalar1=phir[:, gp:gp + 1], scalar2=None,
                                                                 op0=ALU.mult), [prmb, B_], [B_])
                          S.op("pool", lambda h: h.tensor_scalar(out=kI[i][:], in0=angc[i][:], scalar1=1.0 / (2 * PI), scalar2=None,
                                                                 op0=ALU.mult), [B_], [B_])
                          S.op("dve", lambda h: h.scalar_tensor_tensor(out=angt[i][:], in0=kI[i][:], scalar=-2 * PI, in1=angc[i][:],
                                                                        op0=ALU.mult, op1=ALU.add), [B_], [B_])
                          S.op("pool", lambda h: h.tensor_scalar(out=angt[i][:], in0=angt[i][:], scalar1=-PI_SAFE, scalar2=PI_SAFE,
                                                                 op0=ALU.max, op1=ALU.min), [B_], [B_])
                          S.op("act", lambda h: h.activation(out=sn[i][:], in_=angt[i][:], func=AF.Sin), [B_], [B_])
                          S.op("act", lambda h: h.activation(out=angc[i][:], in_=angt[i][:], func=AF.Abs), [B_], [B_])
                          S.op("act", lambda h: h.activation(out=cn[i][:], in_=angc[i][:], func=AF.Sin, bias=PI / 2, scale=-1.0), [B_], [B_])
                          for q in range(2):
                              g = 2 * gp + q
                              r = slice(q * 64, (q + 1) * 64)
                              S.op("pe", lambda h: h.matmul(psF[0][r, 0:NCH], lhsT=WBre[:, gp, r], rhs=U[:, g, :], start=True, stop=True),
                                   [Wb, Ub], [psFb[0]])
                              S.op("pe", lambda h: h.matmul(psF[1][r, 0:NCH], lhsT=WBim[:, gp, r], rhs=U[:, g, :], start=True, stop=True),
                                   [Wb, Ub], [psFb[1]])
                          S.op("act", lambda h: h.activation(out=Fre[i][:], in_=psF[0][:, 0:NCH], func=AF.Copy), [psFb[0], B_], [B_])
                          S.op("act", lambda h: h.activation(out=Fim[i][:], in_=psF[1][:, 0:NCH], func=AF.Copy), [psFb[1], B_], [B_])

                          def D_(fn, eng="dve"):
                              S.op(eng, fn, [B_, prmb], [B_])
                          D_(lambda h: h.tensor_tensor(out=q1[i][:], in0=cn[i][:], in1=Fre[i][:, 0:256], op=ALU.mult))
                          D_(lambda h: h.tensor_tensor(out=q2[i][:], in0=sn[i][:], in1=Fim[i][:, 0:256], op=ALU.mult), "pool")
                          D_(lambda h: h.tensor_tensor(out=Gre[i][:], in0=q1[i][:], in1=q2[i][:], op=ALU.add))
                          D_(lambda h: h.tensor_tensor(out=q1[i][:], in0=cn[i][:], in1=Fim[i][:, 0:256], op=ALU.mult))
                          D_(lambda h: h.tensor_tensor(out=q2[i][:], in0=sn[i][:], in1=Fre[i][:, 0:256], op=ALU.mult), "pool")
                          D_(lambda h: h.tensor_tensor(out=Gim[i][:], in0=q1[i][:], in1=q2[i][:], op=ALU.subtract))
                          D_(lambda h: h.tensor_tensor_scan(out=Tre[i][:], data0=r8[:, gp:gp + 1].broadcast_to([128, 256]), data1=Gre[i][:], initial=0.0, op0=ALU.mult, op1=ALU.add))
                          D_(lambda h: h.tensor_tensor_scan(out=Tim[i][:], data0=r8[:, gp:gp + 1].broadcast_to([128, 256]), data1=Gim[i][:], initial=0.0, op0=ALU.mult, op1=ALU.add))
                          D_(lambda h: h.tensor_tensor(out=q1[i][:], in0=cn[i][:], in1=Tre[i][:], op=ALU.mult))
                          D_(lambda h: h.tensor_tensor(out=q2[i][:], in0=sn[i][:], in1=Tim[i][:], op=ALU.mult), "pool")
                          D_(lambda h: h.tensor_tensor(out=Sre[i][:], in0=q1[i][:], in1=q2[i][:], op=ALU.subtract))
                          D_(lambda h: h.tensor_tensor(out=q1[i][:], in0=cn[i][:], in1=Tim[i][:], op=ALU.mult))
                          D_(lambda h: h.tensor_tensor(out=q2[i][:], in0=sn[i][:], in1=Tre[i][:], op=ALU.mult), "pool")
                          D_(lambda h: h.tensor_tensor(out=Sim[i][:], in0=q1[i][:], in1=q2[i][:], op=ALU.add))
                          D_(lambda h: h.tensor_copy(out=Sxr[i][:, 1:257], in_=Sre[i][:]), "pool")
                          D_(lambda h: h.tensor_copy(out=Sxi[i][:, 1:257], in_=Sim[i][:]), "pool")
                          S.op("pool", lambda h: h.tensor_copy(out=fin_re[:, gp:gp + 1], in_=Sre[i][:, 255:256]), [B_], [finb])
                          S.op("pool", lambda h: h.tensor_copy(out=fin_im[:, gp:gp + 1], in_=Sim[i][:, 255:256]), [B_], [finb])
                          S.op("dve", lambda h: h.scalar_tensor_tensor(out=q1[i][:, 0:4], in0=spr[:, gp, :], scalar=a8re[:, gp:gp + 1], in1=Fre[i][:, 256:260],
                                                                       op0=ALU.mult, op1=ALU.add), [B_, prmb], [B_])
                          S.op("dve", lambda h: h.scalar_tensor_tensor(out=fins_re[:, gp, :], in0=spi[:, gp, :], scalar=na8im[:, gp:gp + 1], in1=q1[i][:, 0:4],
                                                                       op0=ALU.mult, op1=ALU.add), [B_, prmb], [finb])
                          S.op("dve", lambda h: h.scalar_tensor_tensor(out=q2[i][:, 0:4], in0=spi[:, gp, :], scalar=a8re[:, gp:gp + 1], in1=Fim[i][:, 256:260],
                                                                       op0=ALU.mult, op1=ALU.add), [B_, prmb], [B_])
                          S.op("dve", lambda h: h.scalar_tensor_tensor(out=fins_im[:, gp, :], in0=spr[:, gp, :], scalar=a8im[:, gp:gp + 1], in1=q2[i][:, 0:4],
                                                                       op0=ALU.mult, op1=ALU.add), [B_, prmb], [finb])
                          for q in range(2):
                              g = 2 * gp + q
                              r = slice(q * 64, (q + 1) * 64)
                              pi = 2 + g % 2
                              py = psF[pi]; pyb = psFb[pi]
                              S.op("pe", lambda h: h.matmul(py[:, 0:NCH], lhsT=Mw[:, g, :], rhs=U[:, g, :], start=True, stop=False, skip_group_check=True),
                                   [Wb, Ub], [pyb], inc=False)
                              S.op("pe", lambda h: h.matmul(py[:, 0:256], lhsT=WCre[r, gp, :], rhs=Sxr[i][r, 0:256], start=False, stop=False, skip_group_check=True),
                                   [prmb, B_], [pyb], inc=False)
                              S.op("pe", lambda h: h.matmul(py[:, 0:256], lhsT=nWCim[r, gp, :], rhs=Sxi[i][r, 0:256], start=False, stop=False, skip_group_check=True),
                                   [prmb, B_], [pyb], inc=False)
                              S.op("pe", lambda h: h.matmul(py[:, 256:260], lhsT=WCre[r, gp, :], rhs=spr16[r, gp, :], start=False, stop=False, skip_group_check=True),
                                   [prmb], [pyb], inc=False)
                              S.op("pe", lambda h: h.matmul(py[:, 256:260], lhsT=nWCim[r, gp, :], rhs=spi16[r, gp, :], start=False, stop=True, skip_group_check=True),
                                   [prmb], [pyb])
                              yi = g % 2
                              S.op("dve", lambda h: h.scalar_tensor_tensor(out=y32[yi][:], in0=U[:, g, :], scalar=dlay[:, g:g + 1], in1=py[:, 0:NCH],
                                                                           op0=ALU.mult, op1=ALU.add), [pyb, Ub, prmb], [yb[yi]])
                              S.op("pool", lambda h: h.tensor_tensor(out=z32[yi][:], in0=y32[yi][:], in1=y32[yi][:], op=ALU.mult), [yb[yi]], [yb[yi]])
                              S.op("pool", lambda h: h.tensor_scalar(out=z32[yi][:], in0=z32[yi][:], scalar1=0.044715, scalar2=1.0, op0=ALU.mult, op1=ALU.add),
                                   [yb[yi]], [yb[yi]])
                              S.op("pool", lambda h: h.tensor_tensor(out=z32[yi][:], in0=z32[yi][:], in1=y32[yi][:], op=ALU.mult), [yb[yi]], [yb[yi]])
                              S.op("pool", lambda h: h.tensor_scalar(out=y32[yi][:], in0=y32[yi][:], scalar1=0.5, scalar2=None, op0=ALU.mult),
                                   [yb[yi]], [yb[yi]])
                              S.op("act", lambda h: h.activation(out=z32[yi][:], in_=z32[yi][:], func=AF.Tanh, scale=GC), [yb[yi]], [yb[yi]])
                              S.op("dve", lambda h: h.scalar_tensor_tensor(out=U[:, g, :], in0=z32[yi][:], scalar=1.0, in1=y32[yi][:],
                                                                           op0=ALU.add, op1=ALU.mult), [yb[yi], Ub], [Ub])
                      S.dma("sp", ch_fo, sre_p[:, :], fin_re[:], reads=[finb])
                      S.dma("sp", ch_fo, sim_p[:, :], fin_im[:], reads=[finb])
                      S.dma("sp", ch_fo, sre_s[:, :, :], fins_re[:], reads=[finb])
                      S.dma("sp", ch_fo, sim_s[:, :, :], fins_im[:], reads=[finb])
                      S.barrier()

              with ExitStack() as scg:
                  sel = sb(scg, "sel", [128, 64, 128], BF16)
                  selb = Buf(); ch_sel = S.chan("sel")
                  S.dma("pool", ch_sel, sel[:], c_sel[:, :, :], writes=[selb])
                  gTs = sb(scg, "gTs", [128, 8, NT], BF16)
                  gTsb = Buf()
                  n = 0
                  for fcn in range(8):
                      for i in range(8):
                          ps = psF[n % 4]; psb_ = psFb[n % 4]; n += 1
                          for gl in range(8):
                              S.op("pe", lambda h: h.matmul(ps[:, 0:NCH], lhsT=sel[:, gl * 8 + i, :], rhs=U[:, 8 * fcn + gl, :],
                                                            start=(gl == 0), stop=(gl == 7)), [selb, Ub], [psb_], inc=(gl == 7))
                          evac(gTs[:, fcn, i:NT:8], ps[:, 0:NCH], [psb_], [gTsb])
                  wg = [sb(scg, "wg%d" % i, [128, 8, 1024], BF16) for i in range(2)]
                  wgb = [Buf(), Buf()]; ch_wg = [S.chan("wg0"), S.chan("wg1")]
                  sg32 = [sb(scg, "sg32%d" % i, [128, 512], F32) for i in range(2)]
                  sgb = [Buf(), Buf()]
                  w_glu_v = w_glu.rearrange("(c p) n -> p c n", p=128)
                  for hv in range(2):
                      S.dma("pool", ch_wg[hv], wg[hv][:, :, 0:512], w_glu_v[:, :, hv * 512:(hv + 1) * 512], writes=[wgb[hv]])
                      S.dma("pool", ch_wg[hv], wg[hv][:, :, 512:1024], w_glu_v[:, :, D + hv * 512:D + (hv + 1) * 512], writes=[wgb[hv]])
                  n = 0
                  for hv in range(2):
                      for s in range(17):
                          P = tp(s)
                          pa = psF[(2 * n) % 4]; pab = psFb[(2 * n) % 4]
                          pb_ = psF[(2 * n + 1) % 4]; pbb = psFb[(2 * n + 1) % 4]
                          si = n % 2
                          n += 1
                          for c in range(8):
                              S.op("pe", lambda h: h.matmul(pa[:P, :], lhsT=gTs[:, c, tcols(s)], rhs=wg[hv][:, c, 0:512], start=(c == 0), stop=(c == 7)),
                                   [gTsb, wgb[hv]], [pab], inc=(c == 7))
                          for c in range(8):
                              S.op("pe", lambda h: h.matmul(pb_[:P, :], lhsT=gTs[:, c, tcols(s)], rhs=wg[hv][:, c, 512:1024], start=(c == 0), stop=(c == 7)),
                                   [gTsb, wgb[hv]], [pbb], inc=(c == 7))
                          S.op("act", lambda h: h.activation(out=sg32[si][:P, :], in_=pb_[:P, :], func=AF.Sigmoid), [pbb], [sgb[si]])
                          S.op("dve", lambda h: h.tensor_tensor(out=sg32[si][:P, :], in0=sg32[si][:P, :], in1=pa[:P, :], op=ALU.mult), [sgb[si], pab], [sgb[si]])
                          S.op("dve", lambda h: h.tensor_tensor(out=x_sb[:P, s, hv * 512:(hv + 1) * 512], in0=x_sb[:P, s, hv * 512:(hv + 1) * 512],
                                                                in1=sg32[si][:P, :], op=ALU.add), [sgb[si], xb[s]], [xb[s]])
                  S.barrier()

        conv_ffn(1)

        ch_y = S.chan("yout")
        for hf in range(2):
            S.dma("sp", ch_y, y_p[hf * 1024:(hf + 1) * 1024, :].rearrange("(p j) d -> p j d", j=8), x_sb[:, hf * 8:(hf + 1) * 8, :],
                  reads=xb[hf * 8:(hf + 1) * 8])
        S.dma("sp", ch_y, y_s[:, :], x_sb[0:NS, 16, :], reads=[xb[16]])
        S.barrier()
    return nc


def _consts():
    c = {}
    c["c_ident"] = np.eye(128, dtype=np.float32)
    j = np.arange(128)
    c["c_trim8"] = np.where(j[:, None] >= j[None, :], -8.0, 0.0).astype(np.float32)
    c["c_onesm8"] = np.full((128, 128), -8.0, np.float32)
    c["c_maskd"] = (j[:, None] < j[None, :]).astype(np.float32)
    hq = np.arange(128)
    tq = hq % 8
    mn = np.zeros((128, 4, 32), np.float32)
    for bl in range(4):
        for t in range(8):
            mn[:, bl, bl * 8 + t] = (t < tq).astype(np.float32)
    c["c_masknew"] = mn
    jj = np.arange(128) // 16
    c["c_maskM"] = (jj[None, :] >= jj[:, None]).astype(np.float32)
    sel = np.zeros((128, 64, 128), np.float32)
    for gl in range(8):
        for i in range(8):
            for cc in range(16):
                sel[i * 16 + cc, gl * 8 + i, gl * 16 + cc] = 1.0
    c["c_sel"] = sel
    c["c_iotap"] = np.arange(128, dtype=np.float32).reshape(128, 1)
    c["c_iotan"] = np.tile(np.arange(256, dtype=np.float32)[None, :], (128, 1))
    return c


def _qp(a):
    rest = a.shape[2:]
    return np.ascontiguousarray(a.reshape((32, 2, 64) + rest).transpose((1, 2, 0) + tuple(range(3, 3 + len(rest)))).reshape((128, 32) + rest))


def _qp_inv(a):
    rest = a.shape[2:]
    return a.reshape((2, 64, 32) + rest).transpose((2, 0, 1) + tuple(range(3, 3 + len(rest)))).reshape((64, 64) + rest)


def _host_prep(x_prompt, x_sample, cache_k, cache_v, state_ssm_re, state_ssm_im, state_ffn_conv, page_table,
               norm_mix, norm_ffn, attn_w_qkv, attn_q_gain, attn_k_gain, attn_logit_bias, attn_w_o,
               ssm_w_in, ssm_lambda_re, ssm_lambda_im, ssm_log_dt, ssm_b_re, ssm_b_im, ssm_c_re, ssm_c_im,
               ssm_d, ssm_w_glu, ffn_w_up, ffn_conv_w, ffn_conv_b, ffn_w_down, cores=None):
    f = lambda a: np.ascontiguousarray(np.asarray(a, dtype=np.float32))
    shared = dict(_consts())
    ckf = f(cache_k).reshape(2560 * 128, D)
    cvf = f(cache_v).reshape(2560 * 128, D)
    shared.update(
        ck=ckf, cv=cvf, norm_mix=f(norm_mix), norm_ffn=f(norm_ffn), w_qkv=f(attn_w_qkv)[0], q_gain=f(attn_q_gain), k_gain=f(attn_k_gain),
        lbias=f(attn_logit_bias), lbias_hq=f(np.repeat(np.asarray(attn_logit_bias)[0], 8).reshape(128, 1)),
        w_o=f(attn_w_o)[0], w_in=f(ssm_w_in)[0], w_glu=f(ssm_w_glu)[0],
        lamT_re=_qp(f(ssm_lambda_re)[0]), lamT_im=_qp(f(ssm_lambda_im)[0]),
        logdtT=_qp(f(np.repeat(np.asarray(ssm_log_dt)[0][:, None], 64, axis=1))),
        BT_re=_qp(f(ssm_b_re)[0]), BT_im=_qp(f(ssm_b_im)[0]),
        CT_re=_qp(f(np.transpose(np.asarray(ssm_c_re)[0], (0, 2, 1)))), CT_im=_qp(f(np.transpose(np.asarray(ssm_c_im)[0], (0, 2, 1)))),
        d_lay=f(np.tile(np.asarray(ssm_d)[0].reshape(64, 16).T, (8, 1))),
        w_up=f(ffn_w_up), w_down=f(ffn_w_down),
        conv_wl=f(np.asarray(ffn_conv_w).reshape(2, 3, NFC, 128).transpose(0, 3, 2, 1)),
        conv_bl=f(np.asarray(ffn_conv_b).reshape(2, NFC, 128).transpose(0, 2, 1)),
    )
    in_maps = []
    sfc = np.asarray(state_ffn_conv, dtype=np.float32)
    for c in (cores if cores is not None else range(NCORES)):
        m = dict(shared)
        bs = slice(4 * c, 4 * c + 4)
        m["xp"] = f(x_prompt[c])
        m["xs"] = f(np.asarray(x_sample)[bs].reshape(NS, D))
        m["ptab"] = np.ascontiguousarray(np.asarray(page_table)[bs].reshape(1, 256).astype(np.int32))
        m["sprev_re"] = _qp(f(np.transpose(np.asarray(state_ssm_re)[0, bs], (1, 2, 0))))
        m["sprev_im"] = _qp(f(np.transpose(np.asarray(state_ssm_im)[0, bs], (1, 2, 0))))
        m["conv_prevT"] = f(sfc[:, bs].reshape(2, 4, 2, NFC, 128).transpose(0, 4, 3, 1, 2))
        in_maps.append(m)
    return in_maps


def _assemble(R):
    n = len(R)
    y_prompt = np.stack([R[c]["y_p"] for c in range(n)]).astype(np.float32)
    y_sample = np.concatenate([R[c]["y_s"].reshape(4, 8, D) for c in range(n)]).astype(np.float32)
    def kv_p(a):
        return np.ascontiguousarray(a[:, :, 0:16, :].transpose(2, 1, 0, 3)).reshape(SEQ, 16, 64)

    def kv_s(a):
        return np.ascontiguousarray(a[:, 0:NS, 16, :].transpose(1, 0, 2)).reshape(4, 8, 16, 64)

    k_prompt = np.stack([kv_p(R[c]["kv_dev"][..., 0:128]) for c in range(n)])[None].astype(np.float32)
    v_prompt = np.stack([kv_p(R[c]["kv_dev"][..., 128:256]) for c in range(n)])[None].astype(np.float32)
    k_sample = np.concatenate([kv_s(R[c]["kv_dev"][..., 0:128]) for c in range(n)])[None].astype(np.float32)
    v_sample = np.concatenate([kv_s(R[c]["kv_dev"][..., 128:256]) for c in range(n)])[None].astype(np.float32)
    sre_p = np.stack([_qp_inv(R[c]["sre_p"]) for c in range(n)])[None].astype(np.float32)
    sim_p = np.stack([_qp_inv(R[c]["sim_p"]) for c in range(n)])[None].astype(np.float32)
    sre_s = np.concatenate([np.transpose(_qp_inv(R[c]["sre_s"]), (2, 0, 1)) for c in range(n)])[None].astype(np.float32)
    sim_s = np.concatenate([np.transpose(_qp_inv(R[c]["sim_s"]), (2, 0, 1)) for c in range(n)])[None].astype(np.float32)
    conv_prompt = np.stack([R[c]["conv_p"].transpose(0, 3, 2, 1).reshape(2, 2, DFF) for c in range(n)], axis=1).astype(np.float32)
    conv_sample = np.concatenate([R[c]["conv_s"].transpose(0, 3, 4, 2, 1).reshape(2, 4, 2, DFF) for c in range(n)], axis=1).astype(np.float32)
    return (y_prompt, y_sample, k_prompt, v_prompt, k_sample, v_sample, sre_p, sim_p, sre_s, sim_s, conv_prompt, conv_sample)


def kernel(**inputs):
    in_maps = _host_prep(**inputs)
    nc = build_program()
    res = run_bass_kernel_spmd(nc, in_maps, core_ids=list(range(NCORES)))
    return _assemble(res.results)
```
